# Optimizing a Trainium2 kernel written in Bass

```python
import math
import jax
import jax.numpy as jnp
from jax import lax
import numpy as np

D_MODEL = 1024
BATCH = 2
SEQ = 8192
DEPTH = 4
DEC_BATCH = 128
DEC_SEQ = 8
PAST_LEN = 2048
PAGE_SIZE = 128

N_MIXERS = 3
N_SSD = (DEPTH + 2) // 3
N_POOL = (DEPTH + 1) // 3
N_NSA = DEPTH // 3
EPS = 1e-6
D_FF = 4 * D_MODEL

SSD_EXPAND = 2
SSD_D_INNER = SSD_EXPAND * D_MODEL
SSD_HEAD_DIM = 64
SSD_HEADS = SSD_D_INNER // SSD_HEAD_DIM
SSD_GROUPS = 4
SSD_HG = SSD_HEADS // SSD_GROUPS
SSD_STATE = 128
SSD_CONV = 4
SSD_CHUNK = 128
SSD_CONV_DIM = SSD_D_INNER + 2 * SSD_GROUPS * SSD_STATE
SSD_IN_DIM = SSD_D_INNER + SSD_CONV_DIM + SSD_HEADS
DT_MIN = 1e-3
DT_MAX = 1e-1

POOL_WINDOWS = (2, 4, 8, 16)
POOL_GROUPS = len(POOL_WINDOWS)
POOL_GC = D_MODEL // POOL_GROUPS
POOL_BUF = max(POOL_WINDOWS) - 1

ATT_HEADS = 16
ATT_HEAD_DIM = 64
ATT_KV = 4
ATT_HG = ATT_HEADS // ATT_KV
Q_DIM = ATT_HEADS * ATT_HEAD_DIM
KV_DIM = 6 * ATT_KV * ATT_HEAD_DIM
NSA_IN_DIM = Q_DIM + KV_DIM + 3 * ATT_HEADS
CMP_BLOCK = 32
CMP_STRIDE = 16
CMP_HIDDEN = 128
SEL_BLOCK = 64
N_SEL = 8
N_LOCAL = 2
FORCE_BONUS = 1000.0
WINDOW = 512
Q_BLOCK = 128

N_BUCKETS = 32
MAX_DISTANCE = 128

kernel_name = 'hybrid_ssd_pool_nsa_decoder_step'


def rmsnorm(x, g):
    xf = x.astype(jnp.float32)
    y = xf * lax.rsqrt(jnp.mean(xf * xf, axis=-1, keepdims=True) + EPS)
    return (y * g.astype(jnp.float32)).astype(x.dtype)


def sqrelu_mlp(h, w_up, w_down):
    a = jax.nn.relu(h @ w_up)
    return (a * a) @ w_down


def rel_bucket(dist):
    n = jnp.maximum(dist, 0)
    max_exact = N_BUCKETS // 2
    nf = jnp.maximum(n, 1).astype(jnp.float32)
    large = max_exact + (jnp.log(nf / max_exact) / math.log(MAX_DISTANCE / max_exact)
                         * (N_BUCKETS - max_exact)).astype(jnp.int32)
    large = jnp.minimum(large, N_BUCKETS - 1)
    return jnp.where(n < max_exact, n, large)


def masked_softmax(s, valid):
    s = jnp.where(valid, s.astype(jnp.float32), -jnp.inf)
    m = jnp.max(s, axis=-1, keepdims=True)
    m = jnp.where(jnp.isfinite(m), m, 0.0)
    e = jnp.where(valid, jnp.exp(s - m), 0.0)
    return e / jnp.maximum(jnp.sum(e, axis=-1, keepdims=True), 1e-30)


def causal_dwconv(u, buf, w, b):
    up = jnp.concatenate([buf, u], axis=1)
    L = u.shape[1]
    out = b
    for k in range(SSD_CONV):
        out = out + up[:, k:k + L] * w[k]
    return out, up[:, -(SSD_CONV - 1):]


def ssd_chunked(x, dt, a, bm, cm, h0, chunk):
    bsz, L = x.shape[:2]
    nc = L // chunk
    r = lambda t: t.reshape((bsz, nc, chunk) + t.shape[2:])
    x, dt, bm, cm = r(x), r(dt), r(bm), r(cm)
    acum = jnp.cumsum(dt * a, axis=2)
    seg = acum[:, :, :, None] - acum[:, :, None, :]
    causal = jnp.tril(jnp.ones((chunk, chunk), bool))[:, :, None, None]
    decay = jnp.exp(jnp.where(causal, seg, -jnp.inf))
    cb = jnp.einsum('bctgn,bcsgn->bctsg', cm, bm)
    wts = cb[..., None] * decay * dt[:, :, None]
    y_intra = jnp.einsum('bctsgh,bcsghp->bctghp', wts, x)
    decay_end = jnp.exp(acum[:, :, -1:] - acum)
    s_chunk = jnp.einsum('bcsgh,bcsgn,bcsghp->bcghpn', decay_end * dt, bm, x)
    chunk_decay = jnp.exp(acum[:, :, -1])

    def step(h, inp):
        s_c, d_c = inp
        return h * d_c[..., None, None] + s_c, h

    h_last, h_in = lax.scan(step, h0.astype(jnp.float32),
                            (jnp.moveaxis(s_chunk, 1, 0).astype(jnp.float32),
                             jnp.moveaxis(chunk_decay, 1, 0)))
    h_in = jnp.moveaxis(h_in, 0, 1)
    y_inter = jnp.einsum('bctgn,bcghpn->bctghp', cm, h_in) * jnp.exp(acum)[..., None]
    y = (y_intra + y_inter).reshape(bsz, L, x.shape[3], x.shape[4], x.shape[5])
    return y, h_last


def ssd_mixer(h, conv_buf, ssm0, w_in, conv_w, conv_b, dt_bias, a_log, d_skip, norm_g, w_out):
    bsz, L, _ = h.shape
    G, HG, P, N = SSD_GROUPS, SSD_HG, SSD_HEAD_DIM, SSD_STATE
    proj = h @ w_in
    z = proj[..., :SSD_D_INNER]
    xbc = proj[..., SSD_D_INNER:SSD_D_INNER + SSD_CONV_DIM]
    dt_raw = proj[..., SSD_D_INNER + SSD_CONV_DIM:]
    xbc_c, conv_new = causal_dwconv(xbc, conv_buf.astype(xbc.dtype), conv_w, conv_b)
    xbc_c = jax.nn.silu(xbc_c)
    xh = xbc_c[..., :SSD_D_INNER].reshape(bsz, L, G, HG, P)
    bm = xbc_c[..., SSD_D_INNER:SSD_D_INNER + G * N].reshape(bsz, L, G, N)
    cm = xbc_c[..., SSD_D_INNER + G * N:].reshape(bsz, L, G, N)
    dt = jax.nn.softplus(dt_raw.astype(jnp.float32) + dt_bias.astype(jnp.float32)).reshape(bsz, L, G, HG)
    a = -jnp.exp(a_log.astype(jnp.float32)).reshape(G, HG)
    chunk = SSD_CHUNK if L % SSD_CHUNK == 0 else L
    y, h_last = ssd_chunked(xh, dt, a, bm, cm, ssm0.reshape(bsz, G, HG, P, N), chunk)
    y = y + xh * d_skip.reshape(G, HG)[..., None]
    y = (y.reshape(bsz, L, SSD_D_INNER) * jax.nn.silu(z)).reshape(bsz, L, G, SSD_D_INNER // G)
    y = rmsnorm(y, norm_g.reshape(G, SSD_D_INNER // G)).reshape(bsz, L, SSD_D_INNER)
    return y @ w_out, conv_new, h_last.reshape(bsz, SSD_HEADS, P, N)


def pool_mixer(h, buf, start, w_grp, scale):
    bsz, L, D = h.shape
    cat = jnp.concatenate([buf.astype(h.dtype), h], axis=1)
    cs = jnp.pad(jnp.cumsum(cat.astype(jnp.float32), axis=1), ((0, 0), (1, 0), (0, 0)))
    pos = start + jnp.arange(L, dtype=jnp.int32)
    parts = []
    for gi, w in enumerate(POOL_WINDOWS):
        lo, hi = gi * POOL_GC, (gi + 1) * POOL_GC
        tot = cs[:, POOL_BUF + 1:POOL_BUF + 1 + L, lo:hi] - cs[:, POOL_BUF + 1 - w:POOL_BUF + 1 - w + L, lo:hi]
        cnt = jnp.minimum(pos + 1, w).astype(jnp.float32)[None, :, None]
        parts.append(tot / cnt)
    pooled = jnp.concatenate(parts, axis=-1).astype(h.dtype)
    diff = (pooled - h).reshape(bsz, L, POOL_GROUPS, POOL_GC)
    y = jnp.einsum('blgc,gcd->blgd', diff, w_grp).reshape(bsz, L, D) * scale
    return y, cat[:, -POOL_BUF:]


def compress_blocks(kf, pe, w1, w2):
    bsz, lp = kf.shape[:2]
    sub = kf.reshape(bsz, lp // CMP_STRIDE, CMP_STRIDE, ATT_KV, ATT_HEAD_DIM)
    w1r = w1.reshape(CMP_BLOCK // CMP_STRIDE, CMP_STRIDE, ATT_HEAD_DIM, CMP_HIDDEN)
    pre = jnp.einsum('bnskd,jsde->bnjke', sub, w1r)
    hid = pre[:, :-1, 0] + pre[:, 1:, 1] + pe.reshape(-1) @ w1
    return jax.nn.silu(hid) @ w2


def nsa_mixer(h, kv_past, win_buf, start, n_keep, w_in, cmp_pe, cmp_w1, cmp_w2, w_out, rel_bias):
    bsz, L, _ = h.shape
    proj = h @ w_in
    q = proj[..., :Q_DIM].reshape(bsz, L, ATT_KV, ATT_HG, ATT_HEAD_DIM) * (ATT_HEAD_DIM ** -0.5)
    kv_new = proj[..., Q_DIM:Q_DIM + KV_DIM].reshape(bsz, L, 6, ATT_KV, ATT_HEAD_DIM)
    gates = jax.nn.sigmoid(proj[..., Q_DIM + KV_DIM:].astype(jnp.float32)).reshape(bsz, L, ATT_KV, ATT_HG, 3)
    paged_new = kv_new[:, :, :4]
    win_new = kv_new[:, :, 4:]
    full = jnp.concatenate([kv_past.astype(h.dtype), paged_new], axis=1)
    n_keys = full.shape[1]
    n_pad = -(-n_keys // SEL_BLOCK) * SEL_BLOCK
    full = jnp.pad(full, ((0, 0), (0, n_pad - n_keys), (0, 0), (0, 0), (0, 0)))
    n_blk = n_pad // SEL_BLOCK
    n_sel = min(N_SEL, n_blk)
    kc = compress_blocks(full[:, :, 0], cmp_pe[0], cmp_w1[0], cmp_w2[0])
    vc = compress_blocks(full[:, :, 1], cmp_pe[1], cmp_w1[1], cmp_w2[1])
    n_cmp = kc.shape[1]
    cmp_end = jnp.arange(n_cmp, dtype=jnp.int32) * CMP_STRIDE + (CMP_BLOCK - 1)
    to_blocks = lambda t: jnp.moveaxis(t.reshape(bsz, n_blk, SEL_BLOCK, ATT_KV, ATT_HEAD_DIM), 3, 1)
    ks_blk = to_blocks(full[:, :, 2])
    vs_blk = to_blocks(full[:, :, 3])
    win_all = jnp.concatenate([win_buf.astype(h.dtype), win_new], axis=1)
    n_buf = win_buf.shape[1]
    win_pos = start - n_buf + jnp.arange(n_buf + L, dtype=jnp.int32)
    kw_all, vw_all = win_all[:, :, 0], win_all[:, :, 1]
    head_id = jnp.arange(ATT_HEADS, dtype=jnp.int32).reshape(ATT_KV, ATT_HG)
    bias_flat = rel_bias.T.reshape(-1)
    take = jax.vmap(jax.vmap(lambda blk, ix: blk[ix]))

    def token_bias(dist):
        b = rel_bias[rel_bucket(dist)]
        return jnp.transpose(b.reshape(dist.shape + (ATT_KV, ATT_HG)), (2, 3, 0, 1))

    def attend_block(qb, gb, qpos, kw, vw, kwpos):
        nq = qb.shape[1]
        d_c = qpos[:, None] - cmp_end[None, :]
        s_c = jnp.einsum('bqghd,bngd->bghqn', qb, kc) + token_bias(d_c)
        p_c = masked_softmax(s_c, d_c >= 0)
        o_c = jnp.einsum('bghqn,bngd->bqghd', p_c.astype(vc.dtype), vc)
        imp = jnp.pad(p_c.sum(axis=2), ((0, 0), (0, 0), (0, 0), (1, 1)))
        imp = (imp[..., 1:] + imp[..., :-1]).reshape(bsz, ATT_KV, nq, n_blk, SEL_BLOCK // CMP_STRIDE).sum(-1)
        qblk = qpos // SEL_BLOCK
        jb = jnp.arange(n_blk, dtype=jnp.int32)
        lag = qblk[:, None] - jb[None, :]
        allowed = lag >= 0
        forced = (jb[None, :] == 0) | (allowed & (lag < N_LOCAL))
        score = jnp.where(allowed, imp + jnp.where(forced, FORCE_BONUS, 0.0), -1.0)
        _, idx = lax.top_k(score, n_sel)
        k_s = take(ks_blk, idx)
        v_s = take(vs_blk, idx)
        kpos = idx[..., None] * SEL_BLOCK + jnp.arange(SEL_BLOCK, dtype=jnp.int32)
        d_s = qpos[None, None, :, None, None] - kpos
        bias_s = bias_flat[head_id[None, :, :, None, None, None] * N_BUCKETS + rel_bucket(d_s)[:, :, None]]
        s_s = jnp.einsum('bqghd,bgqnkd->bghqnk', qb, k_s) + bias_s
        shp = s_s.shape[:4] + (n_sel * SEL_BLOCK,)
        ok_s = jnp.broadcast_to((d_s >= 0)[:, :, None], s_s.shape).reshape(shp)
        p_s = masked_softmax(s_s.reshape(shp), ok_s)
        o_s = jnp.einsum('bghqm,bgqmd->bqghd', p_s.astype(v_s.dtype),
                         v_s.reshape(bsz, ATT_KV, nq, n_sel * SEL_BLOCK, ATT_HEAD_DIM))
        d_w = qpos[:, None] - kwpos[None, :]
        ok_w = (d_w >= 0) & (d_w <= WINDOW) & (kwpos[None, :] >= 0)
        s_w = jnp.einsum('bqghd,bsgd->bghqs', qb, kw) + token_bias(d_w)
        p_w = masked_softmax(s_w, ok_w)
        o_w = jnp.einsum('bghqs,bsgd->bqghd', p_w.astype(vw.dtype), vw)
        return gb[..., 0:1] * o_c + gb[..., 1:2] * o_s + gb[..., 2:3] * o_w

    qpos_all = start + jnp.arange(L, dtype=jnp.int32)
    if L <= Q_BLOCK:
        o = attend_block(q, gates, qpos_all, kw_all, vw_all, win_pos)
    else:
        span = WINDOW + Q_BLOCK
        off = n_buf - WINDOW

        def one(i):
            q0 = i * Q_BLOCK
            sl = lambda t: lax.dynamic_slice_in_dim(t, q0, Q_BLOCK, axis=1)
            wl = lambda t: lax.dynamic_slice_in_dim(t, q0 + off, span, axis=1)
            return attend_block(sl(q), sl(gates), lax.dynamic_slice_in_dim(qpos_all, q0, Q_BLOCK),
                                wl(kw_all), wl(vw_all), lax.dynamic_slice_in_dim(win_pos, q0 + off, span))

        o = lax.map(one, jnp.arange(L // Q_BLOCK, dtype=jnp.int32))
        o = jnp.moveaxis(o, 0, 1).reshape(bsz, L, ATT_KV, ATT_HG, ATT_HEAD_DIM)
    y = o.reshape(bsz, L, Q_DIM).astype(h.dtype) @ w_out
    return y, paged_new, win_all[:, -n_keep:]


def setup_inputs(seed: int = 0) -> dict:
    key = jax.random.key(seed)
    keys = iter(jax.random.split(key, 40))

    def nrm(shape, scale=1.0):
        return jax.random.normal(next(keys), shape, jnp.float32) * scale

    def gain(shape):
        return 1.0 + 0.05 * jax.random.normal(next(keys), shape, jnp.float32)

    n_pages = PAST_LEN // PAGE_SIZE
    n_used = DEC_BATCH * n_pages
    n_phys = n_used + max(1, n_used // 4)
    win_rows = min(WINDOW, PAST_LEN)
    x_prompt = nrm((BATCH, SEQ, D_MODEL))
    x_sample = nrm((DEC_BATCH, DEC_SEQ, D_MODEL))
    cache_nsa_kv = nrm((N_NSA, n_phys, PAGE_SIZE, 4, ATT_KV, ATT_HEAD_DIM))
    state_nsa_win = nrm((N_NSA, DEC_BATCH, win_rows, 2, ATT_KV, ATT_HEAD_DIM))
    state_ssm = nrm((N_SSD, DEC_BATCH, SSD_HEADS, SSD_HEAD_DIM, SSD_STATE), 0.5)
    state_conv = nrm((N_SSD, DEC_BATCH, SSD_CONV - 1, SSD_CONV_DIM))
    state_pool = nrm((N_POOL, DEC_BATCH, POOL_BUF, D_MODEL))
    page_table = jax.random.permutation(next(keys), n_phys)[:n_used].reshape(DEC_BATCH, n_pages).astype(jnp.int32)
    dt = jnp.exp(jax.random.uniform(next(keys), (N_SSD, SSD_HEADS), jnp.float32,
                                    math.log(DT_MIN), math.log(DT_MAX)))
    ssd_dt_bias = dt + jnp.log(-jnp.expm1(-dt))
    ssd_a_log = jnp.log(jax.random.uniform(next(keys), (N_SSD, SSD_HEADS), jnp.float32, 1.0, 16.0))
    return {
        'x_prompt': x_prompt,
        'x_sample': x_sample,
        'cache_nsa_kv': cache_nsa_kv,
        'state_nsa_win': state_nsa_win,
        'state_ssm': state_ssm,
        'state_conv': state_conv,
        'state_pool': state_pool,
        'page_table': page_table,
        'rel_bias': nrm((N_BUCKETS, ATT_HEADS), 0.5),
        'norm_mix': gain((DEPTH, D_MODEL)),
        'norm_ffn': gain((DEPTH, D_MODEL)),
        'norm_out': gain((D_MODEL,)),
        'ffn_w_up': nrm((DEPTH, D_MODEL, D_FF), D_MODEL ** -0.5),
        'ffn_w_down': nrm((DEPTH, D_FF, D_MODEL), 0.5 * D_FF ** -0.5),
        'ssd_w_in': nrm((N_SSD, D_MODEL, SSD_IN_DIM), D_MODEL ** -0.5),
        'ssd_conv_w': nrm((N_SSD, SSD_CONV, SSD_CONV_DIM), SSD_CONV ** -0.5),
        'ssd_conv_b': nrm((N_SSD, SSD_CONV_DIM), 0.01),
        'ssd_dt_bias': ssd_dt_bias,
        'ssd_a_log': ssd_a_log,
        'ssd_d': 1.0 + 0.1 * nrm((N_SSD, SSD_HEADS)),
        'ssd_norm': gain((N_SSD, SSD_D_INNER)),
        'ssd_w_out': nrm((N_SSD, SSD_D_INNER, D_MODEL), SSD_D_INNER ** -0.5),
        'pool_w': nrm((N_POOL, POOL_GROUPS, POOL_GC, POOL_GC), POOL_GC ** -0.5),
        'pool_scale': 1.0 + 0.1 * nrm((N_POOL, D_MODEL)),
        'nsa_w_in': nrm((N_NSA, D_MODEL, NSA_IN_DIM), D_MODEL ** -0.5),
        'nsa_cmp_pe': nrm((N_NSA, 2, CMP_BLOCK, ATT_HEAD_DIM), 0.5),
        'nsa_cmp_w1': nrm((N_NSA, 2, CMP_BLOCK * ATT_HEAD_DIM, CMP_HIDDEN), (CMP_BLOCK * ATT_HEAD_DIM) ** -0.5),
        'nsa_cmp_w2': nrm((N_NSA, 2, CMP_HIDDEN, ATT_HEAD_DIM), CMP_HIDDEN ** -0.5),
        'nsa_w_out': nrm((N_NSA, Q_DIM, D_MODEL), Q_DIM ** -0.5),
    }


def reference(x_prompt, x_sample, cache_nsa_kv, state_nsa_win, state_ssm, state_conv, state_pool, page_table,
              rel_bias, norm_mix, norm_ffn, norm_out, ffn_w_up, ffn_w_down,
              ssd_w_in, ssd_conv_w, ssd_conv_b, ssd_dt_bias, ssd_a_log, ssd_d, ssd_norm, ssd_w_out,
              pool_w, pool_scale, nsa_w_in, nsa_cmp_pe, nsa_cmp_w1, nsa_cmp_w2, nsa_w_out):
    bp, lp, _ = x_prompt.shape
    bs = x_sample.shape[0]
    past_len = page_table.shape[1] * PAGE_SIZE
    xp, xs = x_prompt, x_sample
    kv_p, kv_s, win_p, win_s = [], [], [], []
    ssm_p, ssm_s, conv_p, conv_s, pool_p, pool_s = [], [], [], [], [], []
    for i in range(DEPTH):
        kind, li = i % N_MIXERS, i // N_MIXERS
        hp = rmsnorm(xp, norm_mix[i])
        hs = rmsnorm(xs, norm_mix[i])
        if kind == 0:
            w = (ssd_w_in[li], ssd_conv_w[li], ssd_conv_b[li], ssd_dt_bias[li], ssd_a_log[li],
                 ssd_d[li], ssd_norm[li], ssd_w_out[li])
            conv0 = jnp.zeros((bp, SSD_CONV - 1, SSD_CONV_DIM), hp.dtype)
            ssm0 = jnp.zeros((bp, SSD_HEADS, SSD_HEAD_DIM, SSD_STATE), jnp.float32)
            yp, c_new_p, s_new_p = ssd_mixer(hp, conv0, ssm0, *w)
            ys, c_new_s, s_new_s = ssd_mixer(hs, state_conv[li], state_ssm[li], *w)
            conv_p.append(c_new_p)
            conv_s.append(c_new_s)
            ssm_p.append(s_new_p)
            ssm_s.append(s_new_s)
        elif kind == 1:
            buf0 = jnp.zeros((bp, POOL_BUF, D_MODEL), hp.dtype)
            yp, b_new_p = pool_mixer(hp, buf0, 0, pool_w[li], pool_scale[li])
            ys, b_new_s = pool_mixer(hs, state_pool[li], past_len, pool_w[li], pool_scale[li])
            pool_p.append(b_new_p)
            pool_s.append(b_new_s)
        else:
            w = (nsa_w_in[li], nsa_cmp_pe[li], nsa_cmp_w1[li], nsa_cmp_w2[li], nsa_w_out[li], rel_bias)
            kv0 = jnp.zeros((bp, 0, 4, ATT_KV, ATT_HEAD_DIM), hp.dtype)
            win0 = jnp.zeros((bp, WINDOW, 2, ATT_KV, ATT_HEAD_DIM), hp.dtype)
            yp, kvn_p, wn_p = nsa_mixer(hp, kv0, win0, 0, min(WINDOW, lp), *w)
            past = cache_nsa_kv[li][page_table].reshape(bs, past_len, 4, ATT_KV, ATT_HEAD_DIM)
            ys, kvn_s, wn_s = nsa_mixer(hs, past, state_nsa_win[li], past_len, state_nsa_win.shape[2], *w)
            kv_p.append(kvn_p)
            kv_s.append(kvn_s)
            win_p.append(wn_p)
            win_s.append(wn_s)
        xp = xp + yp
        xs = xs + ys
        xp = xp + sqrelu_mlp(rmsnorm(xp, norm_ffn[i]), ffn_w_up[i], ffn_w_down[i])
        xs = xs + sqrelu_mlp(rmsnorm(xs, norm_ffn[i]), ffn_w_up[i], ffn_w_down[i])
    y_prompt = rmsnorm(xp, norm_out)
    y_sample = rmsnorm(xs, norm_out)
    nsa_kv_prompt = jnp.stack(kv_p)
    nsa_kv_sample = jnp.stack(kv_s)
    nsa_win_prompt = jnp.stack(win_p)
    nsa_win_sample = jnp.stack(win_s)
    ssm_prompt = jnp.stack(ssm_p)
    ssm_sample = jnp.stack(ssm_s)
    conv_prompt = jnp.stack(conv_p)
    conv_sample = jnp.stack(conv_s)
    pool_prompt = jnp.stack(pool_p)
    pool_sample = jnp.stack(pool_s)
    return (y_prompt, y_sample, nsa_kv_prompt, nsa_kv_sample, nsa_win_prompt, nsa_win_sample,
            ssm_prompt, ssm_sample, conv_prompt, conv_sample, pool_prompt, pool_sample)
```

```python
import numpy as np
import concourse.bass as bass
import concourse.mybir as mybir
from concourse.bass_utils import run_bass_kernel_spmd
from contextlib import ExitStack

F32 = mybir.dt.float32
BF16 = mybir.dt.bfloat16
I32 = mybir.dt.int32
U32 = mybir.dt.uint32
ALU = mybir.AluOpType
AF = mybir.ActivationFunctionType
AX = mybir.AxisListType


class Buf:
    __slots__ = ("name", "lw", "rd")

    def __init__(self, name=""):
        self.name = name
        self.lw = None
        self.rd = {}


class Op:
    __slots__ = ("eng", "fn", "deps", "sig", "sigval", "dma", "dsem", "dval", "prev")

    def __init__(self, eng, fn, dma):
        self.eng = eng
        self.fn = fn
        self.deps = []
        self.sig = False
        self.sigval = 0
        self.dma = dma
        self.dsem = None
        self.dval = 0
        self.prev = None


ENGS = ["pe", "act", "dve", "pool", "sp"]
NDMA = {"sp": 12, "pool": 6, "act": 4}


class Sched:
    def __init__(self, nc, es):
        self.nc = nc
        self.q = {e: [] for e in ENGS}
        self.ndma = {e: 0 for e in ENGS}
        self.dmas = {e: [] for e in ENGS}
        self.csem = {e: es.enter_context(nc.semaphore("c_" + e)) for e in ["pe", "act", "dve", "pool"]}
        self.dsem = {e: [es.enter_context(nc.semaphore("d_%s%d" % (e, i))) for i in range(NDMA[e])]
                     for e in NDMA}
        self.ccount = {e: 0 for e in ["pe", "act", "dve", "pool"]}
        self.nops = 0

    def add(self, eng, fn, rd=(), wr=(), dma=False):
        op = Op(eng, fn, dma)
        deps = {}

        def need(d, raw):
            if d is None or d is op:
                return
            if (not d.dma) and (not dma) and d.eng == eng and (eng == "pe" or not raw):
                return
            deps[id(d)] = d

        for b in rd:
            need(b.lw, True)
        for b in wr:
            need(b.lw, False)
            for d in b.rd.values():
                need(d, False)
        op.deps = list(deps.values())
        for d in op.deps:
            d.sig = True
        if dma:
            n = self.ndma[eng]
            self.ndma[eng] = n + 1
            K = NDMA[eng]
            op.dsem = n % K
            op.dval = 16 * (n // K + 1)
            if n >= K:
                op.prev = self.dmas[eng][n - K]
            self.dmas[eng].append(op)
        self.q[eng].append(op)
        self.nops += 1
        for b in rd:
            b.rd[(eng + str(self.nops)) if dma else eng] = op
        for b in wr:
            b.lw = op
            b.rd = {}
        return op

    def emit(self):
        nc = self.nc
        with ExitStack() as es:
            csem, dsem = self.csem, self.dsem
            for e in ["pe", "act", "dve", "pool"]:
                c = self.ccount[e]
                for op in self.q[e]:
                    if op.sig and not op.dma:
                        c += 1
                        op.sigval = c
                self.ccount[e] = c
            block = es.enter_context(nc.Block())
            engobj = {"pe": "tensor", "act": "scalar", "dve": "vector", "pool": "gpsimd", "sp": "sync"}

            def run(e, eng):
                waited = {}

                def wait(sem, val):
                    k = id(sem)
                    if waited.get(k, 0) >= val:
                        return
                    waited[k] = val
                    eng.wait_ge(sem, val)

                def wait_op(d):
                    if d.dma:
                        wait(dsem[d.eng][d.dsem], d.dval)
                    else:
                        wait(csem[d.eng], d.sigval)

                for op in self.q[e]:
                    for d in op.deps:
                        wait_op(d)
                    if op.prev is not None:
                        wait_op(op.prev)
                    ins = op.fn()
                    if op.dma:
                        ins.then_inc(dsem[e][op.dsem], 16)
                    elif op.sig:
                        ins.then_inc(csem[e], 1)
                if e == "sp":
                    for ee in NDMA:
                        K = NDMA[ee]
                        n = self.ndma[ee]
                        for i in range(min(K, n)):
                            cnt = (n - 1 - i) // K + 1
                            wait(dsem[ee][i], 16 * cnt)

            @block.tensor
            def _(eng):
                run("pe", eng)

            @block.scalar
            def _(eng):
                run("act", eng)

            @block.vector
            def _(eng):
                run("dve", eng)

            @block.gpsimd
            def _(eng):
                run("pool", eng)

            @block.sync
            def _(eng):
                run("sp", eng)

        self.q = {e: [] for e in ENGS}

D = 1024
DFF = 4096
EPS = 1e-6


class KB:
    def __init__(self, nc, NP, NS):
        self.nc = nc
        self.NP, self.NS = NP, NS
        self.NT = NP + NS
        self.es = ExitStack()
        self.pes = ExitStack()
        self.S = Sched(nc, self.es)
        self._db = {}
        self.din = {}
        self.dout = {}
        self.groups = [(t, 512) for t in range(0, NP, 512)] + ([(NP, NS)] if NS else [])
        self._uid = 0

    def uid(self, p):
        self._uid += 1
        return "%s_%d" % (p, self._uid)

    def dram_in(self, name, shape, dt=F32):
        t = self.nc.dram_tensor(name, list(shape), dt, kind="ExternalInput")
        self.din[name] = t
        return t

    def dram_out(self, name, shape, dt=F32):
        t = self.nc.dram_tensor(name, list(shape), dt, kind="ExternalOutput")
        self.dout[name] = t
        return t

    def dram_tmp(self, name, shape, dt):
        return self.nc.dram_tensor(name, list(shape), dt, kind="Internal")

    def sb(self, name, shape, dt, es=None):
        return (es or self.pes).enter_context(self.nc.sbuf_tensor(self.uid(name), list(shape), dt))

    def gsb(self, name, shape, dt):
        return self.es.enter_context(self.nc.sbuf_tensor(self.uid(name), list(shape), dt))

    def db(self, t, sub=None):
        k = (t.name, sub)
        if k not in self._db:
            self._db[k] = Buf(str(k))
        return self._db[k]

    def end_phase(self):
        self.S.emit()
        self.pes.close()
        self.pes = ExitStack()
        self._cs = None

    def ps(self, name, shape, dt):
        return self.es.enter_context(self.nc.psum_tensor(self.uid(name), list(shape), dt))

    def dma(self, out, in_, rd=(), wr=(), q="sp", **kw):
        eng = {"sp": self.nc.sync, "pool": self.nc.gpsimd, "act": self.nc.scalar}[q]
        return self.S.add(q, lambda: eng.dma_start(out=out, in_=in_, **kw), rd, wr, dma=True)

    def mm(self, out, lhsT, rhs, start, stop, rd=(), wr=(), **kw):
        nc = self.nc
        return self.S.add("pe", lambda: nc.tensor.matmul(out, lhsT, rhs, start=start, stop=stop, **kw), rd, wr)

    def tr(self, out, in_, ident, rd=(), wr=()):
        nc = self.nc
        return self.S.add("pe", lambda: nc.tensor.transpose(out, in_, ident), rd, wr)

    def act(self, out, in_, func, rd=(), wr=(), **kw):
        nc = self.nc
        return self.S.add("act", lambda: nc.scalar.activation(out, in_, func, **kw), rd, wr)

    def tt(self, out, in0, in1, op, rd=(), wr=(), eng="dve"):
        e = self.nc.vector if eng == "dve" else self.nc.gpsimd
        return self.S.add(eng, lambda: e.tensor_tensor(out, in0, in1, op), rd, wr)

    def ts(self, out, in0, s1, s2, op0, op1=None, rd=(), wr=(), eng="dve", **kw):
        e = self.nc.vector if eng == "dve" else self.nc.gpsimd
        if op1 is None:
            return self.S.add(eng, lambda: e.tensor_scalar(out, in0, s1, None, op0, **kw), rd, wr)
        return self.S.add(eng, lambda: e.tensor_scalar(out, in0, s1, s2, op0, op1, **kw), rd, wr)

    def stt(self, out, in0, scalar, in1, op0, op1, rd=(), wr=(), **kw):
        nc = self.nc
        return self.S.add("dve", lambda: nc.vector.scalar_tensor_tensor(out, in0, scalar, in1, op0, op1, **kw), rd, wr)

    def copy(self, out, in_, rd=(), wr=(), eng="dve"):
        nc = self.nc
        if eng == "act":
            return self.S.add("act", lambda: nc.scalar.copy(out, in_), rd, wr)
        e = nc.vector if eng == "dve" else nc.gpsimd
        return self.S.add(eng, lambda: e.tensor_copy(out, in_), rd, wr)

    def recip(self, out, in_, rd=(), wr=()):
        nc = self.nc
        return self.S.add("dve", lambda: nc.vector.reciprocal(out, in_), rd, wr)

    def memset(self, ap, val, wr=(), eng="dve"):
        e = self.nc.vector if eng == "dve" else self.nc.gpsimd
        return self.S.add(eng, lambda: e.memset(ap, val), (), wr)

    def setup_common(self):
        nc = self.nc
        self.pbank = [self.ps("pb", [128, 512], F32) for _ in range(8)]
        self.pbuf = [Buf("pb%d" % i) for i in range(8)]
        self._pn = 0
        self._held = set()
        cst = self.dram_in("c_ident", [128, 128])
        self.ident_f = self.gsb("identf", [128, 128], F32)
        self.ident_b = self.gsb("identb", [128, 128], BF16)
        self.ones_b = self.gsb("onesb", [128, 128], BF16)
        self.cb = Buf("consts")
        self.dma(self.ident_f[:], cst.ap(), wr=[self.cb])
        self.copy(self.ident_b[:], self.ident_f[:], rd=[self.cb], wr=[self.cb])
        self.memset(self.ones_b[:], 1.0, wr=[self.cb])
        self.epscol = self.gsb("eps", [128, 1], F32)
        self.memset(self.epscol[:], EPS, wr=[self.cb])
        self.onecol = self.gsb("one", [128, 1], F32)
        self.memset(self.onecol[:], 1.0, wr=[self.cb])

    def set_wslots(self, k, size=8192):
        self.NW, self.WS = k, size
        self.wt = [self.sb("wt", [128, size], BF16) for _ in range(k)]
        self.wb = [Buf("wt%d" % i) for i in range(k)]
        self._wn = 0

    def actmul(self, out, in_, c, rd=(), wr=()):
        nc = self.nc
        return self.S.add("act", lambda: nc.scalar.mul(out, in_, c), rd, wr)

    def bank(self, hold=False):
        while True:
            i = self._pn % 8
            self._pn += 1
            if i not in self._held:
                break
        if hold:
            self._held.add(i)
        return self.pbank[i], self.pbuf[i]

    def release(self, buf):
        self._held.discard(self.pbuf.index(buf))

    def wslot(self):
        i = self._wn % self.NW
        self._wn += 1
        return self.wt[i], self.wb[i]

    def cast_weight(self, src, dst, rows, cols):
        if getattr(self, "_cs", None) is None:
            self._cs = [(self.sb("cs", [128, 4096], F32), self.sb("cd", [128, 4096], BF16), Buf("cs"), Buf("cd"))
                        for _ in range(2)]
            self._cn = 0
        total = rows * cols
        per = total // 128
        sv = src.reshape([128, per]).ap()
        dv = dst.reshape([128, per]).ap()
        for c0 in range(0, per, 4096):
            n = min(4096, per - c0)
            a, b, ba, bb = self._cs[self._cn % 2]
            eng = ["dve", "act", "pool"][self._cn % 3]
            self._cn += 1
            self.dma(a[:, :n], sv[:, c0:c0 + n], wr=[ba])
            self.copy(b[:, :n], a[:, :n], rd=[ba], wr=[bb], eng=eng)
            self.dma(dv[:, c0:c0 + n], b[:, :n], rd=[bb], wr=[self.db(dst)])

    def rmsnorm_group(self, xg, xbuf, n, gcol, hT, hbuf, KC=8, tmp=None):
        sq, sqb, rs, rsb = tmp
        pb, pbb = self.bank()
        for kc in range(KC):
            self.act(sq[:, kc, :n], xg[:, kc, :n], AF.Square, rd=[xbuf], wr=[sqb])
        for kc in range(KC):
            self.mm(pb[:, :n], self.ones_b[:], sq[:, kc, :n], kc == 0, kc == KC - 1, rd=[sqb, self.cb], wr=[pbb])
        self.act(rs[:, :n], pb[:, :n], AF.Sqrt, rd=[pbb], wr=[rsb], scale=1.0 / (128 * KC), bias=self.epscol[:, 0:1])
        self.recip(rs[:, :n], rs[:, :n], rd=[rsb], wr=[rsb])
        for kc in range(KC):
            self.stt(hT[:, kc, :n], xg[:, kc, :n], gcol[:, kc:kc + 1], rs[:, :n], ALU.mult, ALU.mult,
                     rd=[xbuf, rsb, self.cb], wr=[hbuf])

    def dense_fm(self, wv, wrd, KC, F, hT, hbuf, n, evac, fpiece=None, cwid=128, skip=None):
        if fpiece is None:
            fpiece = self.WS // KC
        for f0 in range(0, F, fpiece):
            fw = min(fpiece, F - f0)
            wt, wb = self.wslot()
            wtv = wt[:, :KC * fw].rearrange("p (kc f) -> p kc f", kc=KC)
            self.dma(wtv, wv[:, :, f0:f0 + fw], rd=[wrd], wr=[wb])
            for c0 in range(0, fw, cwid):
                cw = min(cwid, fw - c0)
                if skip is not None and skip((f0 + c0) // cwid):
                    continue
                pb, pbb = self.bank()
                for kc in range(KC):
                    self.mm(pb[:cw, :n], wtv[:, kc, c0:c0 + cw], hT[:, kc, :n], kc == 0, kc == KC - 1,
                            rd=[wb, hbuf], wr=[pbb])
                evac((f0 + c0) // cwid, pb[:cw, :n], pbb)

    def mlp_layer(self, li, xT, gcols):
        es = None
        self.set_wslots(3)
        xg = [self.sb("xg", [128, 8, 512], F32, es) for _ in range(2)]
        xgb = [Buf("xg") for _ in range(2)]
        hT = self.sb("hT", [128, 8, 512], BF16, es)
        hb = Buf("hT")
        aT = self.sb("aT", [128, 32, 512], BF16, es)
        ab = Buf("aT")
        sq = self.sb("sq", [128, 8, 512], BF16, es)
        rs = self.sb("rs", [128, 512], F32, es)
        rl = [self.sb("rl", [128, 512], BF16, es) for _ in range(2)]
        rlb = [Buf("rl") for _ in range(2)]
        xo = [self.sb("xo", [128, 512], F32, es) for _ in range(2)]
        xob = [Buf("xo") for _ in range(2)]
        tmp = (sq, Buf("sq"), rs, Buf("rs"))
        xv = xT.ap().rearrange("(kc p) t -> p kc t", p=128)
        wup, wdn = self.w_up[li], self.w_dn[li]
        G = self.groups

        def load(gi):
            t0, n = G[gi]
            self.dma(xg[gi % 2][:, :, :n], xv[:, :, t0:t0 + n], rd=[self.db(xT, gi)], wr=[xgb[gi % 2]])

        load(0)
        cnt = [0]
        for gi, (t0, n) in enumerate(G):
            if gi + 1 < len(G):
                load(gi + 1)
            x, xb = xg[gi % 2], xgb[gi % 2]
            self.rmsnorm_group(x, xb, n, gcols, hT, hb, tmp=tmp)

            def ev_up(fc, p, pbb):
                j = cnt[0] % 2
                cnt[0] += 1
                self.act(rl[j][:, :n], p, AF.Relu, rd=[pbb], wr=[rlb[j]])
                self.tt(aT[:, fc, :n], rl[j][:, :n], rl[j][:, :n], ALU.mult, rd=[rlb[j]], wr=[ab])

            self.dense_fm(wup.ap().rearrange("(kc p) f -> p kc f", p=128), self.db(wup), 8, DFF, hT, hb, n, ev_up)

            def ev_dn(dc, p, pbb):
                j = cnt[0] % 2
                cnt[0] += 1
                self.tt(xo[j][:, :n], p, x[:, dc, :n], ALU.add, rd=[pbb, xb], wr=[xob[j]])
                self.dma(xT.ap()[dc * 128:(dc + 1) * 128, t0:t0 + n], xo[j][:, :n], rd=[xob[j]], wr=[self.db(xT, gi)])

            self.dense_fm(wdn.ap().rearrange("(kc p) f -> p kc f", p=128), self.db(wdn), 32, D, aT, ab, n, ev_dn)

def pool_layer(self, xres, gcol, scol, wpool_b, invc_d, spoolT, out_pp, out_ps):
    NPg = self.NP // 512
    self.set_wslots(1, 64)
    wp = self.sb("wp", [128, 8, 256], BF16)
    wpb = Buf("wp")
    self.dma(wp[:], wpool_b.ap().rearrange("(a p) d -> p a d", p=128), rd=[self.db(wpool_b)], wr=[wpb])
    invc = self.sb("invc", [128, 4, 16], F32)
    self.dma(invc[:], invc_d.ap(), wr=[wpb])
    xg = self.sb("xg", [128, 8, 512], F32)
    xgb = Buf("xg")
    hx = [self.sb("hx", [128, 8, 527], F32) for _ in range(2)]
    hxb = [Buf("hx") for _ in range(2)]
    st = [self.sb("st", [128, 527], F32) for _ in range(4)]
    stb = [Buf("st") for _ in range(4)]
    dT = self.sb("dT", [128, 8, 512], BF16)
    dTb = Buf("dT")
    sqt = [self.sb("sq", [128, 512], BF16) for _ in range(2)]
    sqb = [Buf("sq") for _ in range(2)]
    rs = self.sb("rs", [128, 512], F32)
    rsb = Buf("rs")
    xo = [self.sb("xo", [128, 512], F32) for _ in range(2)]
    xob = [Buf("xo") for _ in range(2)]
    xv = xres.ap().rearrange("(kc p) t -> p kc t", p=128)
    cnt = 0
    for gi, (t0, n) in enumerate(self.groups):
        samp = gi >= NPg
        nseq, L = (16, 8) if samp else (1, 512)
        W = 15 + L
        cur, curb = hx[gi % 2], hxb[gi % 2]
        prv, prvb = hx[(gi + 1) % 2], hxb[(gi + 1) % 2]
        v3 = lambda ap: ap.rearrange("p (s l) -> p s l", s=nseq)
        hv = lambda kc: cur[:, kc, :nseq * W].rearrange("p (s w) -> p s w", s=nseq)
        self.dma(xg[:, :, :n], xv[:, :, t0:t0 + n], rd=[self.db(xres, gi)], wr=[xgb])
        if samp:
            for kc in range(8):
                self.dma(hv(kc)[:, :, 0:15], spoolT.ap()[kc * 128:(kc + 1) * 128, :].rearrange("p (s r) -> p s r", s=16),
                         wr=[curb])
        elif gi == 0:
            self.memset(cur[:, :, 0:15], 0.0, wr=[curb], eng="pool")
        else:
            self.copy(cur[:, :, 0:15], prv[:, :, 512:527], rd=[prvb], wr=[curb], eng="pool")
        pb, pbb = self.bank()
        for kc in range(8):
            j = kc % 2
            self.act(sqt[j][:, :n], xg[:, kc, :n], AF.Square, rd=[xgb], wr=[sqb[j]])
            self.mm(pb[:, :n], self.ones_b[:], sqt[j][:, :n], kc == 0, kc == 7, rd=[sqb[j], self.cb], wr=[pbb])
        self.act(rs[:, :n], pb[:, :n], AF.Sqrt, rd=[pbb], wr=[rsb], scale=1.0 / 1024, bias=self.epscol[:, 0:1])
        self.recip(rs[:, :n], rs[:, :n], rd=[rsb], wr=[rsb])
        for kc in range(8):
            self.stt(hv(kc)[:, :, 15:W], v3(xg[:, kc, :n]), gcol[:, kc:kc + 1], v3(rs[:, :n]), ALU.mult, ALU.mult,
                     rd=[xgb, rsb, self.cb], wr=[curb])
        for kc in range(8):
            g = kc // 2
            w = 2 << g
            src, srcb = hv(kc), curb
            for lev in range(g + 1):
                sh = 1 << lev
                lo = (2 << lev) - 1
                k = cnt % 4
                cnt += 1
                dst = st[k][:, :nseq * W].rearrange("p (s w) -> p s w", s=nseq)
                self.tt(dst[:, :, lo:W], src[:, :, lo:W], src[:, :, lo - sh:W - sh], ALU.add, rd=[srcb], wr=[stb[k]],
                        eng="pool" if kc % 2 else "dve")
                src, srcb = dst, stb[k]
            self.stt(v3(dT[:, kc, :n]), src[:, :, 15:W], 1.0 / w, hv(kc)[:, :, 15:W], ALU.mult, ALU.subtract,
                     rd=[srcb, curb], wr=[dTb])
            if gi == 0:
                k = cnt % 4
                cnt += 1
                self.tt(st[k][:, 0:16], src[:, 0, 15:31], invc[:, g, :], ALU.mult, rd=[srcb, wpb], wr=[stb[k]])
                self.tt(dT[:, kc, 0:16], st[k][:, 0:16], cur[:, kc, 15:31], ALU.subtract, rd=[stb[k], curb], wr=[dTb])
        for g in range(4):
            for dc in range(2):
                pb, pbb = self.bank()
                for k2 in range(2):
                    self.mm(pb[:, :n], wp[:, 2 * g + k2, dc * 128:(dc + 1) * 128], dT[:, 2 * g + k2, :n], k2 == 0, k2 == 1,
                            rd=[wpb, dTb], wr=[pbb])
                o = 2 * g + dc
                j = cnt % 2
                cnt += 1
                self.stt(xo[j][:, :n], pb[:, :n], scol[:, o:o + 1], xg[:, o, :n], ALU.mult, ALU.add,
                         rd=[pbb, xgb, self.cb], wr=[xob[j]])
                self.dma(xres.ap()[o * 128:(o + 1) * 128, t0:t0 + n], xo[j][:, :n], rd=[xob[j]], wr=[self.db(xres, gi)])
        if gi == NPg - 1:
            self.dma(out_pp.ap(), cur[:, :, 512:527], rd=[curb], wr=[self.db(out_pp)])
        if samp:
            for kc in range(8):
                self.dma(out_ps.ap()[:, kc], hv(kc)[:, :, 8:23], rd=[curb], wr=[self.db(out_ps)])


KB.pool_layer = pool_layer

def ssd_layer(self, xres, gcol, P):
    GL = 512
    groups = [(t, GL) for t in range(0, self.NP, GL)] + [(self.NP, self.NS)]
    NPg = self.NP // GL
    self.set_wslots(2, 4096)
    M = self.masks
    mb = self.cb
    cnt = [0]

    def rot(lst):
        cnt[0] += 1
        return lst[cnt[0] % len(lst)]

    def mk(name, shape, dt, k=1):
        return [(self.sb(name, shape, dt), Buf(name)) for _ in range(k)]

    cw = self.sb("cw", [128, 24, 4], F32)
    cbias = self.sb("cbias", [128, 24], F32)
    dtb = self.sb("dtb", [128, 32], F32)
    abc = self.sb("abc", [128, 32], F32)
    dsk = self.sb("dsk", [128, 32], F32)
    ng = self.sb("ng", [128, 2048], F32)
    wdt = self.sb("wdt", [128, 8, 32], BF16)
    pb_ = Buf("params")
    self.dma(cw[:], P["convw"].ap(), wr=[pb_])
    self.dma(cbias[:], P["convb"].ap(), wr=[pb_])
    self.dma(dtb[:], P["dtb"].ap(), wr=[pb_])
    self.dma(abc[:], P["alog"].ap(), wr=[pb_])
    self.dma(dsk[:], P["dskip"].ap(), wr=[pb_])
    self.dma(ng[:], P["normg"].ap(), wr=[pb_])
    wiv = P["w_in_b"].ap().rearrange("(kc p) f -> p kc f", p=128)
    self.dma(wdt[:], wiv[:, :, 5120:5152], rd=[self.db(P["w_in_b"])], wr=[pb_])
    self.act(abc[:], abc[:], AF.Exp, rd=[pb_], wr=[pb_])
    self.ts(abc[:], abc[:], -1.0, None, ALU.mult, rd=[pb_], wr=[pb_])
    halo = self.sb("halo", [128, 24, 16, 3], F32)
    halob = Buf("halo")
    self.memset(halo[:, :, 0, :], 0.0, wr=[halob], eng="pool")

    (xg, xgb), = mk("xg", [128, 8, GL], F32)
    (hT, hb), = mk("hT", [128, 8, GL], BF16)
    sqs = mk("sq", [128, GL], BF16, 2)
    (rs, rsb), = mk("rs", [128, GL], F32)
    xos = mk("xo", [128, GL], F32, 2)
    upcs = mk("upc", [128, GL + 8], F32, 2)
    accs = mk("acc", [128, GL], F32, 2)
    (xcT, xcb), = mk("xcT", [128, 24, GL], BF16)
    (zs, zsb), = mk("zs", [128, 4, 2048], BF16)
    (dt, dtbuf), = mk("dt", [128, 4, 32], F32)
    (dta, dtab), = mk("dta", [128, 4, 32], F32)
    dtr = mk("dtr", [128, 32], F32, 2)
    (x_tm, xtb), = mk("x_tm", [128, 2048], BF16)
    (B_tm, btb), = mk("B_tm", [128, 512], BF16)
    (CBm, cbmb), = mk("CBm", [128, 4, 128], F32)
    (acT, actb), = mk("acT", [32, 128], F32)
    (nac, nacb), = mk("nac", [128, 32], F32)
    (eac, eacb), = mk("eac", [128, 32], F32)
    (wgt, wgtb), = mk("wgt", [128, 32], F32)
    (cdec, cdecb), = mk("cdec", [128, 32], F32)
    lexps = mk("lexp", [128, 128], F32, 4)
    whs = mk("wh", [128, 128], BF16, 4)
    (HT, htb), = mk("HT", [128, 2048], F32)
    (HTb, htbb), = mk("HTb", [128, 2048], BF16)
    (xw, xwb), = mk("xw", [128, 2048], BF16)
    (ytm, ytb), = mk("ytm", [128, 2048], F32)
    t1s = mk("t1", [128, 512], F32, 1)
    t2s = mk("t2", [128, 512], F32, 1)
    t3s = mk("t3", [128, 512], F32, 1)
    (ss, ssb), = mk("ss", [128, 4], F32)
    (junk, junkb), = mk("junk", [128, 512], BF16)
    (yn, ynb), = mk("yn", [128, 2048], BF16)
    yT, yTb = xg[:].rearrange("p k t -> p (k t)").bitcast(BF16).rearrange("p (j t) -> p j t", j=16), xgb
    xrs = mk("xr", [128, GL], F32, 2)
    bds = mk("bd", [32, 1024], F32, 2)
    h0s = [(HT[:, 0:512], Buf("h0")), (HT[:, 512:1024], Buf("h0"))]
    hbv = HT[:, 1216:1728].bitcast(BF16)
    h0bs = [(hbv[:, 0:512], Buf("h0b")), (hbv[:, 512:1024], Buf("h0b"))]
    CTm, ctmb = HTb[:, :].rearrange("p (s t) -> p s t", s=16), Buf("CTm")
    bmv = HT[:, 1024:1152].bitcast(BF16)
    bms = [(bmv[:, 0:128], Buf("bm")), (bmv[:, 128:256], Buf("bm"))]
    cds = [(HT[:, 1152:1184], Buf("cd")), (HT[:, 1184:1216], Buf("cd"))]

    self.memset(HT[:], 0.0, wr=[htb], eng="pool")
    self.memset(HTb[:], 0.0, wr=[htbb], eng="pool")
    xv = xres.ap().rearrange("(kc p) t -> p kc t", p=128)
    wov = P["w_out_b"].ap().rearrange("(kc p) f -> p kc f", p=128)

    for gi, (t0, n) in enumerate(groups):
        samp = gi >= NPg
        if samp:
            self.S.emit()
        nseq, L = (16, 8) if samp else (1, GL)
        W = L + 3
        nch = n // 128
        TRI = M["trib"] if samp else M["tri"]
        NEGm = M["negb"] if samp else M["neg"]
        BLK = M["blk"] if samp else M["onesf"]
        v3 = lambda ap: ap.rearrange("p (s l) -> p s l", s=nseq)
        self.dma(xg[:, :, :n], xv[:, :, t0:t0 + n], rd=[self.db(xres)], wr=[xgb])
        pb, pbb = self.bank()
        for kc in range(8):
            sq, sqb = rot(sqs)
            self.act(sq[:, :n], xg[:, kc, :n], AF.Square, rd=[xgb], wr=[sqb])
            self.mm(pb[:, :n], self.ones_b[:], sq[:, :n], kc == 0, kc == 7, rd=[sqb, mb], wr=[pbb])
        self.act(rs[:, :n], pb[:, :n], AF.Sqrt, rd=[pbb], wr=[rsb], scale=1.0 / 1024, bias=self.epscol[:, 0:1])
        self.recip(rs[:, :n], rs[:, :n], rd=[rsb], wr=[rsb])
        for kc in range(8):
            self.stt(hT[:, kc, :n], xg[:, kc, :n], gcol[:, kc:kc + 1], rs[:, :n], ALU.mult, ALU.mult,
                     rd=[xgb, rsb, mb], wr=[hb])
        if samp:
            self.dma(halo[:], P["convT_in"].ap(), wr=[halob])

        def ev_xbc(fc, p, pbb2):
            upc, upb = rot(upcs)
            acc, accb = rot(accs)
            u3 = upc[:, :nseq * W].rearrange("p (s w) -> p s w", s=nseq)
            self.copy(u3[:, :, 3:W], v3(p), rd=[pbb2], wr=[upb], eng="act")
            self.copy(u3[:, :, 0:3], halo[:, fc, :nseq, :], rd=[halob], wr=[upb], eng="pool")
            if samp:
                self.dma(P["conv_ps"].ap()[:, fc], u3[:, :, L:W], rd=[upb], wr=[self.db(P["conv_ps"])])
            else:
                self.copy(halo[:, fc, 0:1, :], u3[:, :, L:W], rd=[upb], wr=[halob], eng="pool")
            a3 = v3(acc[:, :n])
            self.ts(a3, u3[:, :, 0:L], cw[:, fc, 0:1], cbias[:, fc:fc + 1], ALU.mult, ALU.add,
                    rd=[upb, pb_], wr=[accb], eng="pool")
            for k in range(1, 4):
                self.stt(a3, u3[:, :, k:k + L], cw[:, fc, k:k + 1], a3, ALU.mult, ALU.add, rd=[upb, accb, pb_], wr=[accb])
            self.act(xcT[:, fc, :n], acc[:, :n], AF.Silu, rd=[accb], wr=[xcb])

        self.dense_fm(wiv[:, :, 2048:5120], self.db(P["w_in_b"]), 8, 3072, hT, hb, n, ev_xbc)
        for piece in range(4):
            wt, wb = self.wslot()
            wtv = wt[:, :4096].rearrange("p (kc f) -> p kc f", kc=8)
            self.dma(wtv, wiv[:, :, piece * 512:(piece + 1) * 512], rd=[self.db(P["w_in_b"])], wr=[wb])
            for ci in range(nch):
                pb, pbb = self.bank()
                for kc in range(8):
                    self.mm(pb[:, :], hT[:, kc, ci * 128:(ci + 1) * 128], wtv[:, kc, :],
                            kc == 0, kc == 7, rd=[hb, wb], wr=[pbb])
                c0 = piece * 512
                self.act(zs[:, ci, c0:c0 + 512], pb[:, :], AF.Silu, rd=[pbb], wr=[zsb])
        for ci in range(nch):
            pb, pbb = self.bank()
            for kc in range(8):
                self.mm(pb[:, :32], hT[:, kc, ci * 128:(ci + 1) * 128], wdt[:, kc, :], kc == 0, kc == 7, rd=[hb, pb_], wr=[pbb])
            d_, d_b = rot(dtr)
            self.tt(d_[:], pb[:, :32], dtb[:], ALU.add, rd=[pbb, pb_], wr=[d_b])
            self.act(d_[:], d_[:], AF.Exp, rd=[d_b], wr=[d_b])
            self.act(dt[:, ci, :], d_[:], AF.Ln, rd=[d_b], wr=[dtbuf], bias=self.onecol[:, 0:1])
            self.tt(dta[:, ci, :], dt[:, ci, :], abc[:], ALU.mult, rd=[dtbuf, pb_], wr=[dtab])
        pendT = [None]
        for ci in range(nch):
            cs = slice(ci * 128, (ci + 1) * 128)
            pb, pbb = self.bank()
            self.mm(pb[:32, 0:128], dta[:, ci, :], TRI[:, :], True, True, rd=[dtab, mb], wr=[pbb])
            self.mm(pb[:, 128:160], TRI[:, :], dta[:, ci, :], True, True, rd=[dtab, mb], wr=[pbb])
            self.mm(pb[:, 160:192], BLK[:, :], dta[:, ci, :], True, True, rd=[dtab, mb], wr=[pbb])
            self.copy(acT[:], pb[:32, 0:128], rd=[pbb], wr=[actb], eng="act")
            self.actmul(nac[:], pb[:, 128:160], -1.0, rd=[pbb], wr=[nacb])
            self.act(eac[:], pb[:, 128:160], AF.Exp, rd=[pbb], wr=[eacb])
            self.act(cdec[:], pb[:, 160:192], AF.Exp, rd=[pbb], wr=[cdecb])
            self.tt(wgt[:], pb[:, 160:192], nac[:], ALU.add, rd=[pbb, nacb], wr=[wgtb])
            self.act(wgt[:], wgt[:], AF.Exp, rd=[wgtb], wr=[wgtb])
            self.tt(wgt[:], wgt[:], dt[:, ci, :], ALU.mult, rd=[wgtb, dtbuf], wr=[wgtb])
            for j4 in range(5):
                pb, pbb = self.bank()
                pbv = pb[:, 0:256].bitcast(BF16)
                for jj in range(4):
                    j = j4 * 4 + jj
                    self.tr(pbv[:, jj * 128:(jj + 1) * 128], xcT[:, j, cs], self.ident_b[:], rd=[xcb, mb], wr=[pbb])
                if j4 < 4:
                    self.copy(x_tm[:, j4 * 512:(j4 + 1) * 512], pbv, rd=[pbb], wr=[xtb], eng="act" if j4 % 2 else "dve")
                else:
                    self.copy(B_tm[:], pbv, rd=[pbb], wr=[btb], eng="act")
            pb, pbb = self.bank()
            for g in range(4):
                self.mm(pb[:, g * 128:(g + 1) * 128], xcT[:, 16 + g, cs], xcT[:, 20 + g, cs], True, True, rd=[xcb], wr=[pbb])
            self.tt(CBm[:], pb[:, :].rearrange("p (g t) -> p g t", g=4), TRI[:, :].unsqueeze(1).to_broadcast([128, 4, 128]),
                    ALU.mult, rd=[pbb, mb], wr=[cbmb])
            self.tt(xw[:].rearrange("p (h q) -> p h q", h=32), x_tm[:].rearrange("p (h q) -> p h q", h=32),
                    wgt[:].unsqueeze(2).to_broadcast([128, 32, 64]), ALU.mult, rd=[xtb, wgtb], wr=[xwb], eng="pool")
            if pendT[0] is not None:
                pendT[0]()
                pendT[0] = None
            batches = [(g_, h4_) for g_ in range(4) for h4_ in range(2)]

            bdcur = [None]

            def emit_L(bi):
                g_, h4_ = batches[bi]
                if h4_ == 0:
                    bd, bdb = rot(bds)
                    self.tt(bd[:].rearrange("p (h t) -> p h t", h=8), acT[:, :].unsqueeze(1).to_broadcast([32, 8, 128]),
                            self.ident_f[0:32, g_ * 8:(g_ + 1) * 8].unsqueeze(2).to_broadcast([32, 8, 128]), ALU.mult,
                            rd=[actb, mb], wr=[bdb])
                    bdcur[0] = (bd, bdb)
                bd, bdb = bdcur[0]
                pbL_, pbLb_ = self.bank(hold=True)
                self.mm(pbL_[:, :], M["onesf"][0:32, :], bd[:, h4_ * 512:(h4_ + 1) * 512], True, True, rd=[bdb, mb], wr=[pbLb_])
                return pbL_, pbLb_

            Ls = {0: emit_L(0)}
            for bi, (g, hh4) in enumerate(batches):
                if bi + 1 < len(batches):
                    Ls[bi + 1] = emit_L(bi + 1)
                if hh4 == 0:
                    pbA, pbAb = self.bank(hold=True)
                    pbB, pbBb = self.bank(hold=True)
                pbL, pbLb = Ls.pop(bi)
                for k in range(4):
                    h = g * 8 + hh4 * 4 + k
                    hh = hh4 * 4 + k
                    sl = slice(k * 128, (k + 1) * 128)
                    lx, lxb = rot(lexps)
                    wh, whb = rot(whs)
                    self.ts(lx[:], pbL[:, sl], nac[:, h:h + 1], 0.0, ALU.add, ALU.min, rd=[pbLb, nacb], wr=[lxb])
                    self.act(lx[:], lx[:], AF.Exp, rd=[lxb], wr=[lxb])
                    self.stt(wh[:], lx[:], dt[:, ci, h:h + 1], CBm[:, g, :], ALU.mult, ALU.mult,
                             rd=[lxb, dtbuf, cbmb], wr=[whb])
                    self.mm(pbA[:, hh * 64:(hh + 1) * 64], wh[:], x_tm[:, h * 64:(h + 1) * 64], True, True,
                            rd=[whb, xtb], wr=[pbAb])
                self.release(pbLb)
                if hh4 == 0:
                    continue
                gs = slice(g * 512, (g + 1) * 512)
                if not samp:
                    self.mm(pbB[:, :], xcT[:, 20 + g, cs], HTb[:, gs], True, True, rd=[xcb, htbb], wr=[pbBb])
                else:
                    self.tt(CTm, xcT[:, 20 + g, cs].unsqueeze(1).to_broadcast([128, 16, 128]), M["sm"][:, :, :], ALU.mult,
                            rd=[xcb, mb], wr=[ctmb])
                    for sq_ in range(16):
                        h0, h0b = rot(h0s)
                        h0c, h0cb = rot(h0bs)
                        self.dma(h0, P["ssmT_in"].ap()[sq_, :, gs], wr=[h0b])
                        self.copy(h0c, h0, rd=[h0b], wr=[h0cb], eng="pool")
                        self.mm(pbB[:, :], CTm[:, sq_, :], h0c, sq_ == 0, sq_ == 15, rd=[ctmb, h0cb], wr=[pbBb])
                        bm, bmb = rot(bms)
                        self.ts(bm, B_tm[:, g * 128:(g + 1) * 128], M["cseq"][:, sq_:sq_ + 1], None, ALU.mult,
                                rd=[btb, mb], wr=[bmb])
                        pbS, pbSb = self.bank()
                        self.mm(pbS[:, :], bm, xw[:, gs], True, True, rd=[bmb, xwb], wr=[pbSb])
                        pbc, pbcb = self.bank()
                        self.mm(pbc[:, :32], M["cseq"][:, sq_:sq_ + 1].to_broadcast([128, 128]), dta[:, ci, :], True, True,
                                rd=[dtab, mb], wr=[pbcb])
                        cd, cdb = rot(cds)
                        self.act(cd, pbc[:, :32], AF.Exp, rd=[pbcb], wr=[cdb])
                        self.tt(h0.rearrange("p (h q) -> p h q", h=8), h0.rearrange("p (h q) -> p h q", h=8),
                                cd[:, g * 8:(g + 1) * 8].unsqueeze(2).to_broadcast([128, 8, 64]), ALU.mult,
                                rd=[h0b, cdb, h0cb], wr=[h0b])
                        self.tt(h0, pbS[:, :], h0, ALU.add, rd=[pbSb, h0b], wr=[h0b])
                        self.dma(P["ssm_ps"].ap()[sq_, :, gs], h0, rd=[h0b], wr=[self.db(P["ssm_ps"])])
                t1, t1b = rot(t1s)
                t2, t2b = rot(t2s)
                t3, t3b = rot(t3s)
                v8 = lambda ap: ap.rearrange("p (h q) -> p h q", h=8)
                self.tt(v8(t1[:]), v8(pbB[:, :]), eac[:, g * 8:(g + 1) * 8].unsqueeze(2).to_broadcast([128, 8, 64]), ALU.mult,
                        rd=[pbBb, eacb], wr=[t1b])
                self.tt(t2[:], pbA[:, :], t1[:], ALU.add, rd=[pbAb, t1b], wr=[t2b])
                self.tt(v8(t3[:]), v8(x_tm[:, gs]), dsk[:, g * 8:(g + 1) * 8].unsqueeze(2).to_broadcast([128, 8, 64]), ALU.mult,
                        rd=[xtb, pb_], wr=[t3b], eng="pool")
                self.tt(t2[:], t2[:], t3[:], ALU.add, rd=[t2b, t3b], wr=[t2b], eng="pool")
                self.tt(ytm[:, gs], t2[:], zs[:, ci, gs], ALU.mult, rd=[t2b, zsb], wr=[ytb])
                self.release(pbAb)
                self.release(pbBb)
                self.act(junk[:], ytm[:, gs], AF.Square, rd=[ytb], wr=[junkb, ssb], accum_out=ss[:, g:g + 1])
            if not samp:
                for g in range(4):
                    gs = slice(g * 512, (g + 1) * 512)
                    pbS, pbSb = self.bank()
                    self.mm(pbS[:, :], B_tm[:, g * 128:(g + 1) * 128], xw[:, gs], True, True, rd=[btb, xwb], wr=[pbSb])
                    v8 = lambda ap: ap.rearrange("p (h q) -> p h q", h=8)
                    self.tt(v8(HT[:, gs]), v8(HT[:, gs]), cdec[:, g * 8:(g + 1) * 8].unsqueeze(2).to_broadcast([128, 8, 64]),
                            ALU.mult, rd=[htb, cdecb, htbb], wr=[htb])
                    self.tt(HT[:, gs], pbS[:, :], HT[:, gs], ALU.add, rd=[pbSb, htb], wr=[htb])
                    self.copy(HTb[:, gs], HT[:, gs], rd=[htb], wr=[htbb], eng="act")

            def tail(ci=ci, cs=cs):
                self.act(ss[:], ss[:], AF.Sqrt, rd=[ssb], wr=[ssb], scale=1.0 / 512, bias=self.epscol[:, 0:1])
                self.recip(ss[:], ss[:], rd=[ssb], wr=[ssb])
                for g in range(4):
                    gs = slice(g * 512, (g + 1) * 512)
                    self.stt(yn[:, gs], ytm[:, gs], ss[:, g:g + 1], ng[:, gs], ALU.mult, ALU.mult, rd=[ytb, ssb, pb_], wr=[ynb])
                for j4 in range(4):
                    pb, pbb = self.bank()
                    pbv = pb[:, 0:256].bitcast(BF16)
                    for jj in range(4):
                        j = j4 * 4 + jj
                        self.tr(pbv[:, jj * 128:(jj + 1) * 128], yn[:, j * 128:(j + 1) * 128], self.ident_b[:], rd=[ynb, mb], wr=[pbb])
                    self.copy(yT[:, j4 * 4:(j4 + 1) * 4, cs], pbv.rearrange("p (j t) -> p j t", j=4), rd=[pbb], wr=[yTb],
                              eng="act" if j4 % 2 else "dve")

            pendT[0] = tail
        if pendT[0] is not None:
            pendT[0]()
            pendT[0] = None

        def ev_out(dc, p, pbb2):
            xo, xob = rot(xos)
            xr, xrb = rot(xrs)
            self.dma(xr[:, :n], xres.ap()[dc * 128:(dc + 1) * 128, t0:t0 + n], rd=[self.db(xres)], wr=[xrb])
            self.tt(xo[:, :n], p, xr[:, :n], ALU.add, rd=[pbb2, xrb], wr=[xob])
            self.dma(xres.ap()[dc * 128:(dc + 1) * 128, t0:t0 + n], xo[:, :n], rd=[xob], wr=[self.db(xres, ("s", gi))])

        self.dense_fm(wov, self.db(P["w_out_b"]), 16, 1024, yT, yTb, n, ev_out)
        if gi == NPg - 1:
            self.dma(P["conv_pp"].ap(), halo[:, :, 0, :], rd=[halob], wr=[self.db(P["conv_pp"])])
            self.dma(P["ssm_pp"].ap(), HT[:], rd=[htb], wr=[self.db(P["ssm_pp"])])


KB.ssd_layer = ssd_layer

OFF = 512
NEGV = -30000.0


def nsa_layer(self, xres, gcol, P):
    NPg = self.NP // 512
    NP, NT = self.NP, self.NT
    S = P["scr"]
    cnt = [0]

    def rot(lst):
        cnt[0] += 1
        return lst[cnt[0] % len(lst)]

    def mk(name, shape, dt, k=1):
        return [(self.sb(name, shape, dt), Buf(name)) for _ in range(k)]

    cb = self.cb
    flip = P["flip"]
    self.set_wslots(1, 64)
    (rb, rbb), = mk("rb", [32, 16], F32)
    (rb31, _x), = mk("rb31", [32, 16], F32)
    (ohd, _x), = mk("ohd", [32, 128], F32)
    (vrow, vrb), = mk("vrow", [16, 3072], F32)
    self.dma(rb[:], P["rel_bias"].ap(), wr=[rbb])
    self.dma(rb31[:], P["rb31"].ap(), wr=[rbb])
    self.dma(ohd[:], P["c_ohd"].ap(), wr=[rbb])
    self.tt(rb[:], rb[:], rb31[:], ALU.subtract, rd=[rbb], wr=[rbb])
    self.memset(vrow[:, 0:OFF], NEGV, wr=[vrb])
    self.memset(vrow[:, OFF + 128:], 0.0, wr=[vrb])
    pb, pbb = self.bank()
    self.mm(pb[:16, :128], rb[:], ohd[:], True, True, rd=[rbb], wr=[pbb])
    self.copy(vrow[:, OFF:OFF + 128], pb[:16, :128], rd=[pbb], wr=[vrb])
    vext = S["vext"]
    self.dma(vext.ap(), vrow[:], rd=[vrb], wr=[self.db(vext)])
    self.end_phase()

    self.set_wslots(2, 4096)
    (xg, xgb), = mk("xg", [128, 8, 512], F32)
    (hT, hb), = mk("hT", [128, 8, 512], BF16)
    sqs = mk("sq", [128, 512], BF16, 2)
    (rs, rsb), = mk("rs", [128, 512], F32)
    kvs = mk("kv", [128, 512], F32, 3)
    vbs = mk("vb", [128, 256], BF16, 2)
    fms = mk("fm", [64, 512], BF16, 3)
    gts = mk("gt", [128, 48], F32, 2)
    (wg, wgb), = mk("wg", [128, 8, 48], BF16)
    xv = xres.ap().rearrange("(kc p) t -> p kc t", p=128)
    wiv = P["w_in_b"].ap().rearrange("(kc p) f -> p kc f", p=128)
    self.dma(wg[:], wiv[:, :, 2560:2608], rd=[self.db(P["w_in_b"])], wr=[wgb])
    self.dma(P["win_ps"].ap()[:, 0:504, :], P["win_in"].ap()[:, 8:512, :], wr=[self.db(P["win_ps"])])
    qT_s, kT_s = S["qT"], S["kT"]
    fm_map = {}
    for h in range(16):
        fm_map[h] = ("q", h)
    for ty, j in enumerate((0, 1, 2, 4)):
        for g in range(4):
            fm_map[16 + j * 4 + g] = ("k", ty * 4 + g)
    for gi, (t0, n) in enumerate(self.groups):
        samp = gi >= NPg
        self.dma(xg[:, :, :n], xv[:, :, t0:t0 + n], rd=[self.db(xres, gi)], wr=[xgb])
        pb, pbb = self.bank()
        for kc in range(8):
            sq, sqb = rot(sqs)
            self.act(sq[:, :n], xg[:, kc, :n], AF.Square, rd=[xgb], wr=[sqb])
            self.mm(pb[:, :n], self.ones_b[:], sq[:, :n], kc == 0, kc == 7, rd=[sqb, cb], wr=[pbb])
        self.act(rs[:, :n], pb[:, :n], AF.Sqrt, rd=[pbb], wr=[rsb], scale=1.0 / 1024, bias=self.epscol[:, 0:1])
        self.recip(rs[:, :n], rs[:, :n], rd=[rsb], wr=[rsb])
        for kc in range(8):
            self.stt(hT[:, kc, :n], xg[:, kc, :n], gcol[:, kc:kc + 1], rs[:, :n], ALU.mult, ALU.mult,
                     rd=[xgb, rsb, cb], wr=[hb])
        for piece in range(3):
            wt, wb = self.wslot()
            wtv = wt[:, :4096].rearrange("p (kc f) -> p kc f", kc=8)
            self.dma(wtv, wiv[:, :, 1024 + piece * 512:1024 + (piece + 1) * 512], rd=[self.db(P["w_in_b"])], wr=[wb])
            for ci in range(n // 128):
                pb, pbb = self.bank()
                for kc in range(8):
                    self.mm(pb[:, :], hT[:, kc, ci * 128:(ci + 1) * 128], wtv[:, kc, :], kc == 0, kc == 7, rd=[hb, wb], wr=[pbb])
                kv, kvb = rot(kvs)
                self.copy(kv[:], pb[:, :], rd=[pbb], wr=[kvb], eng="act" if ci % 2 else "dve")
                r0 = t0 + ci * 128
                if piece < 2:
                    if not samp:
                        self.dma(P["kv_pp"].ap()[r0:r0 + 128, piece * 512:(piece + 1) * 512], kv[:], rd=[kvb],
                                 wr=[self.db(P["kv_pp"], r0)])
                    else:
                        self.dma(P["kv_ps"].ap()[:, piece * 512:(piece + 1) * 512], kv[:], rd=[kvb], wr=[self.db(P["kv_ps"], piece)])
                else:
                    if not samp and r0 >= NP - 512:
                        w0 = r0 - (NP - 512)
                        self.dma(P["win_pp"].ap()[w0:w0 + 128, :], kv[:], rd=[kvb], wr=[self.db(P["win_pp"], w0)])
                    if samp:
                        for s_ in range(16):
                            self.dma(P["win_ps"].ap()[s_, 504:512, :], kv[s_ * 8:(s_ + 1) * 8, :], rd=[kvb],
                                     wr=[self.db(P["win_ps"], s_)])
                if piece >= 1:
                    vb, vbb = rot(vbs)
                    self.copy(vb[:], kv[:, 256:512], rd=[kvb], wr=[vbb], eng="pool")
                    dst = S["vsel"] if piece == 1 else S["vwin"]
                    self.dma(dst.ap()[r0:r0 + 128, :], vb[:], rd=[vbb], wr=[self.db(dst, r0)])
        for ci in range(n // 128):
            pb, pbb = self.bank()
            for kc in range(8):
                self.mm(pb[:, :48], hT[:, kc, ci * 128:(ci + 1) * 128], wg[:, kc, :], kc == 0, kc == 7, rd=[hb, wgb], wr=[pbb])
            gt, gtb = rot(gts)
            self.act(gt[:], pb[:, :48], AF.Sigmoid, rd=[pbb], wr=[gtb])
            r0 = t0 + ci * 128
            self.dma(S["gates"].ap()[r0:r0 + 128, :], gt[:], rd=[gtb], wr=[self.db(S["gates"], r0)])

        def ev_fm(c64, p, pbb2):
            kind, idx = fm_map[c64]
            fm, fmb = rot(fms)
            if kind == "q":
                self.actmul(fm[:, :n], p, 0.125, rd=[pbb2], wr=[fmb])
                self.dma(qT_s.ap()[idx, :, t0:t0 + n], fm[:, :n], rd=[fmb], wr=[self.db(qT_s, (idx, gi))])
            else:
                self.copy(fm[:, :n], p, rd=[pbb2], wr=[fmb])
                self.dma(kT_s.ap()[idx, :, t0:t0 + n], fm[:, :n], rd=[fmb], wr=[self.db(kT_s, (idx, gi))])

        self.dense_fm(wiv[:, :, 0:2560], self.db(P["w_in_b"]), 8, 2560, hT, hb, n, ev_fm, fpiece=512, cwid=64,
                      skip=lambda c: c not in fm_map)
    self.end_phase()

    self.set_wslots(1, 64)
    (D0, d0b), = mk("D0", [128, 4, 128], F32)
    (D1, d1b), = mk("D1", [128, 4, 128], F32)
    (Band, bandb), = mk("Band", [32, 4, 128], F32)
    (Ds1, ds1b), = mk("Ds1", [128, 16, 8], F32)
    (Bs, bsb), = mk("Bs", [128, 16, 8], F32)
    (Ds0, ds0b), = mk("Ds0", [8, 16, 8], F32)
    (EE, eeb), = mk("EE", [128, 8192], BF16)
    (Am, amb), = mk("Am", [128, 4, 128], F32)
    (Jt, jtb), = mk("Jt", [128, 128], F32)
    (col0, _x), = mk("col0", [128, 128], F32)
    (shiftI, _x), = mk("shiftI", [32, 384], F32)
    (neglow, _x), = mk("neglow", [128, 128], BF16)
    (dw4s, _x), = mk("dw4s", [128, 32], F32)
    (sampA, _x), = mk("sampA", [8, 128], F32)
    (sampC, _x), = mk("sampC", [8, 128], F32)
    (w1b, w1bb), = mk("w1b", [64, 64, 128], BF16)
    (w2b, w2bb), = mk("w2b", [128, 2, 64], BF16)
    (peT, petb), = mk("peT", [64, 64], BF16)
    (cbias, cbb), = mk("cbias", [128, 2], F32)
    (kcT, kctb), = mk("kcT", [64, 4, 512], BF16)
    (vcA, vcab), = mk("vcA", [128, 4, 4, 65], BF16)
    (tmpf, tmpfb), = mk("tmpf", [128, 512], F32)
    cst = Buf("nsaconst")

    def toeplitz(dst, dstb, nrow, ncol, base, pstride, fl, h0=0, nh=16):
        src = bass.AP(tensor=vext, offset=base + 3072 * h0, ap=[[pstride, nrow], [3072, nh], [1, ncol]])
        tv = tmpf[:nrow, :nh * ncol].rearrange("p (h c) -> p h c", h=nh)
        self.dma(tv, src, rd=[self.db(vext)], wr=[tmpfb])
        tot = nh * ncol
        for c0 in range(0, tot, 512):
            cw = min(512, tot - c0)
            pb, pbb = self.bank()
            self.mm(pb[:nrow, :cw], fl, tmpf[:nrow, c0:c0 + cw], True, True, rd=[tmpfb, cb], wr=[pbb])
            self.copy(dst.rearrange("p h c -> p (h c)")[:, c0:c0 + cw], pb[:nrow, :cw], rd=[pbb], wr=[dstb])

    toeplitz(Ds1[:], ds1b, 128, 8, OFF + 1, 1, flip[:, :])
    toeplitz(Bs[:], bsb, 128, 8, OFF - 15, 16, flip[:, :])
    toeplitz(Ds0[:], ds0b, 8, 8, OFF - 7, 1, flip[0:8, 120:128])
    for c0 in range(0, 8192, 512):
        self.dma(tmpf[:, :], P["c_EE"].ap()[:, c0:c0 + 512], wr=[tmpfb])
        self.copy(EE[:, c0:c0 + 512], tmpf[:, :], rd=[tmpfb], wr=[eeb], eng="pool")
    self.dma(Am[:], P["c_A"].ap(), wr=[cst])
    self.dma(Jt[:], P["c_J"].ap(), wr=[cst])
    self.dma(col0[:], P["c_col0"].ap(), wr=[cst])
    self.dma(shiftI[:], P["c_shift"].ap(), wr=[cst])
    self.dma(dw4s[:], P["c_dw4s"].ap(), wr=[cst])
    self.dma(sampA[:], P["c_sampA"].ap(), wr=[cst])
    self.dma(sampC[:], P["c_sampC"].ap(), wr=[cst])
    self.dma(tmpf[:, 0:128], P["c_neglow"].ap(), wr=[tmpfb])
    self.copy(neglow[:], tmpf[:, 0:128], rd=[tmpfb], wr=[cst])
    self.dma(w1b[:], P["w1_b"].ap().rearrange("(a d) e -> d a e", d=64), rd=[self.db(P["w1_b"])], wr=[w1bb])
    self.dma(tmpf[:, 0:128], P["cmp_w2"].ap(), wr=[tmpfb])
    self.copy(w2b[:].rearrange("p a d -> p (a d)"), tmpf[:, 0:128], rd=[tmpfb], wr=[w2bb])
    self.dma(tmpf[:64, 128:192], P["peT"].ap(), wr=[tmpfb])
    self.copy(peT[:], tmpf[:64, 128:192], rd=[tmpfb], wr=[petb])
    for kv in range(2):
        pb, pbb = self.bank()
        for js in range(32):
            self.mm(pb[:, 0:1], w1b[:, kv * 32 + js, :], peT[:, kv * 32 + js:kv * 32 + js + 1], js == 0, js == 31,
                    rd=[w1bb, petb], wr=[pbb])
        self.copy(cbias[:, kv:kv + 1], pb[:, 0:1], rd=[pbb], wr=[cbb])
    self.memset(vcA[:, :, :, 64:65], 1.0, wr=[vcab])

    (kbig, kbigb), = mk("kbig", [64, 8192], BF16)
    (kbig2, kbig2b), = mk("kbig2", [64, 8192], BF16)
    hids = mk("hid", [128, 512], BF16, 2)
    NC = 511
    for g in range(4):
        for kv in range(2):
            kb_, kbb_ = (kbig, kbigb) if kv == 0 else (kbig2, kbig2b)
            self.dma(kb_[:], kT_s.ap()[kv * 4 + g, :, 0:NP], wr=[kbb_])
            pb, pbb = self.bank()
            for js in range(32):
                rhs = kb_[:, js:js + 16 * (NC - 1) + 1:16]
                self.mm(pb[:, :NC], w1b[:, kv * 32 + js, :], rhs, js == 0, js == 31, rd=[w1bb, kbb_], wr=[pbb])
            hid, hidb = rot(hids)
            self.act(hid[:, :NC], pb[:, :NC], AF.Silu, rd=[pbb, cbb], wr=[hidb], bias=cbias[:, kv:kv + 1])
            if kv == 0:
                pb2, pbb2 = self.bank()
                self.mm(pb2[:64, :NC], w2b[:, 0, :], hid[:, :NC], True, True, rd=[w2bb, hidb], wr=[pbb2])
                self.copy(kcT[:, g, :NC], pb2[:64, :NC], rd=[pbb2], wr=[kctb])
            else:
                for m in range(4):
                    nn = min(128, NC - m * 128)
                    pb2, pbb2 = self.bank()
                    self.mm(pb2[:nn, :64], hid[:, m * 128:m * 128 + nn], w2b[:, 1, :], True, True, rd=[w2bb, hidb], wr=[pbb2])
                    self.copy(vcA[:nn, m, g, 0:64], pb2[:nn, :64], rd=[pbb2], wr=[vcab])

    KselT, kselb = kbig, kbigb
    KwinT, kwinb = kbig2, kbig2b
    (VselA, vselb), = mk("VselA", [128, 68, 65], BF16)
    (VwinA, vwinb), = mk("VwinA", [128, 64, 65], BF16)
    self.memset(VselA[:, :, 64:65], 1.0, wr=[vselb])
    self.memset(VwinA[:, :, 64:65], 1.0, wr=[vwinb])
    qts = mk("qt", [64, 4, 128], BF16, 2)
    gats = mk("gat", [128, 48], F32, 2)
    ETs = mk("ET", [128, 4, 512], BF16, 1)
    ets = mk("et", [128, 512], BF16, 5)
    (rden, rdenb), = mk("rden", [128, 512], F32)
    (PT, ptb), = mk("PT", [128, 512], F32)
    (PsT, pstb), = mk("PsT", [128, 4, 128], F32)
    (Ai, aib), = mk("Ai", [128, 128], F32)
    (Ci, cib), = mk("Ci", [128, 128], F32)
    (sc, scb), = mk("sc", [128, 128], F32)
    (m8, m8b), = mk("m8", [128, 8], F32)
    (sel, selb), = mk("sel", [128, 128], BF16)
    (selT, seltb), = mk("selT", [128, 128], BF16)
    (coef, coefb), = mk("coef", [128, 4], F32)
    os_ = mk("o", [128, 256], F32, 2)
    obs = mk("ob", [128, 256], BF16, 2)
    masks_ = mk("mk", [128, 128], BF16, 2)
    ET, etb = ETs[0]
    o_s = S["o"]

    def tile_stage1(KT, ktb_, kcols, nk, qt, qtb, nq, add_tile=None, add_neg=None, band=None, mask_sel=None, keepE=None):
        N = 4 * nq
        pbS, pbSb = self.bank()
        extra = (add_tile is not None) or (add_neg is not None) or (band is not None)
        self.mm(pbS[:nk, :N], KT[:, kcols], qt.rearrange("p h q -> p (h q)"), True, not extra, rd=[ktb_, qtb], wr=[pbSb])
        if add_tile is not None:
            t_ap, t_b = add_tile
            self.mm(pbS[:nk, :N], self.ident_f[:nk, :nk], t_ap, False, (add_neg is None), rd=[t_b, cb], wr=[pbSb])
        if add_neg is not None:
            self.mm(pbS[:nk, :N], self.ident_b[:nk, :nk], add_neg, False, True, rd=[cst, cb], wr=[pbSb])
        if band is not None:
            sh_ap, b_ap = band
            self.mm(pbS[:nk, :N], sh_ap, b_ap, False, True, rd=[cst, bandb], wr=[pbSb])
        if mask_sel is not None:
            ee_cols, selT_ap = mask_sel
            pbM, pbMb = self.bank()
            self.mm(pbM[:nk, :nq], EE[:, ee_cols], selT_ap, True, True, rd=[eeb, seltb], wr=[pbMb])
        if keepE is not None:
            e_ap, e_b = keepE
        else:
            e_t, e_b = rot(ets)
            e_ap = e_t[:nk, :N]
        self.act(e_ap, pbS[:nk, :N], AF.Exp, rd=[pbSb], wr=[e_b])
        if mask_sel is not None:
            ev = e_ap.rearrange("p (h q) -> p h q", h=4)
            self.tt(ev, ev, pbM[:nk, :nq].unsqueeze(1).to_broadcast([nk, 4, nq]), ALU.mult, rd=[e_b, pbMb], wr=[e_b])
        return e_ap, e_b

    def tile_stage2(acc, accb, e, nq, VA_ap, vab_, first, last):
        e_ap, e_b = e
        for h in range(4):
            self.mm(acc[:nq, h * 65:(h + 1) * 65], e_ap[:, h * nq:(h + 1) * nq], VA_ap, first and h == 0, last,
                    rd=[e_b, vab_], wr=[accb], skip_group_check=True)

    def run_tiles(acc, accb, specs, nq, look=2):
        pend = []
        nt = len(specs)
        for t, (s1, VA_ap, vab_) in enumerate(specs):
            pend.append((tile_stage1(**s1), VA_ap, vab_))
            if t >= look:
                e, va, vb_ = pend[t - look]
                tile_stage2(acc, accb, e, nq, va, vb_, t - look == 0, t - look == nt - 1)
        for t in range(max(0, nt - look), nt):
            e, va, vb_ = pend[t]
            tile_stage2(acc, accb, e, nq, va, vb_, t == 0, t == nt - 1)

    def epilogue(acc, accb, nq, gat, gatb, g, br, o, ob, firstbr):
        a3 = acc[:nq, 0:260].rearrange("p (h d) -> p h d", h=4)
        self.ts(coef[:nq, :], a3[:, :, 64], 1e-30, None, ALU.max, rd=[accb], wr=[coefb])
        self.recip(coef[:nq, :], coef[:nq, :], rd=[coefb], wr=[coefb])
        gv = gat[:nq, g * 12:(g + 1) * 12].rearrange("p (h b) -> p h b", b=3)[:, :, br]
        self.tt(coef[:nq, :], coef[:nq, :], gv, ALU.mult, rd=[coefb, gatb], wr=[coefb])
        for h in range(4):
            if firstbr:
                self.ts(o[:nq, h * 64:(h + 1) * 64], a3[:, h, 0:64], coef[:nq, h:h + 1], None, ALU.mult,
                        rd=[accb, coefb], wr=[ob])
            else:
                self.stt(o[:nq, h * 64:(h + 1) * 64], a3[:, h, 0:64], coef[:nq, h:h + 1], o[:nq, h * 64:(h + 1) * 64],
                         ALU.mult, ALU.add, rd=[accb, coefb, ob], wr=[ob])

    def select_part1(nq, imp_ps, imp_b, A_ap, C_ap, arb):
        self.tt(sc[:nq, :], imp_ps, A_ap, ALU.mult, rd=[imp_b] + arb, wr=[scb])
        self.tt(sc[:nq, :], sc[:nq, :], C_ap, ALU.add, rd=[scb] + arb, wr=[scb])
        nc = self.nc
        self.S.add("dve", lambda: nc.vector.max(m8[:nq, :], sc[:nq, :]), [scb], [m8b])
        self.ts(sel[:nq, :], sc[:nq, :], m8[:nq, 7:8], None, ALU.is_ge, rd=[scb, m8b], wr=[selb])

    def select_part2(nq):
        pbT, pbTb = self.bank()
        pv = pbT[:, 0:64].bitcast(BF16)
        self.tr(pv[:, :nq], sel[:nq, :], self.ident_b[:nq, :nq], rd=[selb, cb], wr=[pbTb])
        self.copy(selT[:, :nq], pv[:, :nq], rd=[pbTb], wr=[seltb], eng="act")

    def select_blocks(nq, imp_ps, imp_b, A_ap, C_ap, arb):
        select_part1(nq, imp_ps, imp_b, A_ap, C_ap, arb)
        select_part2(nq)

    for g in range(4):
        toeplitz(D0[:], d0b, 128, 128, OFF - 127, 1, flip[:, :], 4 * g, 4)
        toeplitz(D1[:], d1b, 128, 128, OFF + 128 - 127, 1, flip[:, :], 4 * g, 4)
        toeplitz(Band[:], bandb, 32, 128, OFF - 367, 16, flip[0:32, 96:128], 4 * g, 4)
        self.dma(KselT[:], kT_s.ap()[8 + g, :, 0:NP], wr=[kselb])
        self.dma(KwinT[:], kT_s.ap()[12 + g, :, 0:NP], wr=[kwinb])
        self.dma(VselA[:, 0:64, 0:64], S["vsel"].ap()[0:NP, g * 64:(g + 1) * 64].rearrange("(kt p) d -> p kt d", p=128), wr=[vselb])
        self.dma(VwinA[:, :, 0:64], S["vwin"].ap()[0:NP, g * 64:(g + 1) * 64].rearrange("(kt p) d -> p kt d", p=128), wr=[vwinb])
        for i in range(NP // 128):
            q0 = i * 128
            qt, qtb = rot(qts)
            gat, gatb = rot(gats)
            self.dma(qt[:], qT_s.ap()[4 * g:4 * g + 4, :, q0:q0 + 128].rearrange("h d q -> d h q"), wr=[qtb])
            self.dma(gat[:], S["gates"].ap()[q0:q0 + 128, :], wr=[gatb])
            o, ob = rot(os_)
            ncmp = min(NC, 8 * i + 7)
            ntile = (ncmp + 127) // 128
            accC, accCb = self.bank(hold=True)
            blo, bhi = 8 * i - 10, 8 * i + 6
            specs = []
            for m in range(ntile):
                nn = min(128, ncmp - m * 128)
                band = None
                r0 = blo - 128 * m
                if bhi >= 128 * m and blo < 128 * m + nn:
                    c0 = 160 - r0
                    band = (shiftI[:, c0:c0 + nn], Band[:].rearrange("p h q -> p (h q)"))
                specs.append((dict(KT=kcT[:, g, :], ktb_=kctb, kcols=slice(m * 128, m * 128 + nn), nk=nn, qt=qt[:], qtb=qtb,
                                   nq=128, band=band, keepE=(ET[:nn, m, :], etb)), vcA[:nn, m, g, :], vcab))
            run_tiles(accC, accCb, specs, 128)
            pbD, pbDb = self.bank()
            for m in range(ntile):
                nn = min(128, ncmp - m * 128)
                self.mm(pbD[:, :], self.ones_b[:nn, :], ET[:nn, m, :], m == 0, m == ntile - 1, rd=[etb, cb], wr=[pbDb])
            self.ts(rden[:], pbD[:, :], 1e-30, None, ALU.max, rd=[pbDb], wr=[rdenb])
            self.recip(rden[:], rden[:], rd=[rdenb], wr=[rdenb])
            pbI, pbIb = self.bank(hold=True)
            for m in range(ntile):
                nn = min(128, ncmp - m * 128)
                self.tt(PT[:nn, :], ET[:nn, m, :], rden[:nn, :], ALU.mult, rd=[etb, rdenb], wr=[ptb])
                nc = self.nc
                pin = PT[:nn, :].rearrange("p (h q) -> p q h", h=4)
                pout = PsT[:nn, m, :]
                self.S.add("dve", lambda pout=pout, pin=pin: nc.vector.tensor_reduce(pout, pin, AX.X, ALU.add), [ptb], [pstb])
                self.mm(pbI[:, :128], PsT[:nn, m, :], Am[:nn, m, :], m == 0, m == ntile - 1, rd=[pstb, amb, cst], wr=[pbIb])
            self.ts(Ai[:], Jt[:], float(2 * i), None, ALU.is_le, rd=[cst], wr=[aib])
            self.ts(Ci[:], Jt[:], float(2 * i), None, ALU.is_equal, rd=[cst], wr=[cib])
            self.stt(Ci[:], Jt[:], float(2 * i - 1), Ci[:], ALU.is_equal, ALU.max, rd=[cst, cib], wr=[cib])
            self.tt(Ci[:], Ci[:], col0[:], ALU.max, rd=[cib, cst], wr=[cib])
            self.stt(Ci[:], Ci[:], 1000.0, Ai[:], ALU.mult, ALU.add, rd=[cib, aib], wr=[cib])
            self.ts(Ci[:], Ci[:], -1.0, None, ALU.add, rd=[cib], wr=[cib])
            select_part1(128, pbI[:, :128], pbIb, Ai[:], Ci[:], [aib, cib])
            self.release(pbIb)
            epilogue(accC, accCb, 128, gat, gatb, g, 0, o, ob, True)
            self.release(accCb)
            accW, accWb = self.bank(hold=True)
            kts = list(range(max(0, i - 4), i + 1))
            specs = []
            for kt in kts:
                add = None
                neg = None
                if kt == i:
                    add = (D0[:].rearrange("p h q -> p (h q)"), d0b)
                elif kt == i - 1:
                    add = (D1[:].rearrange("p h q -> p (h q)"), d1b)
                elif kt == i - 4:
                    neg = neglow[:, :].unsqueeze(1).to_broadcast([128, 4, 128])
                specs.append((dict(KT=KwinT, ktb_=kwinb, kcols=slice(kt * 128, (kt + 1) * 128), nk=128, qt=qt[:], qtb=qtb,
                                   nq=128, add_tile=add, add_neg=neg), VwinA[:, kt, :], vwinb))
            run_tiles(accW, accWb, specs, 128)
            select_part2(128)
            epilogue(accW, accWb, 128, gat, gatb, g, 2, o, ob, False)
            self.release(accWb)
            accS, accSb = self.bank(hold=True)
            specs = []
            for kt in range(i + 1):
                add = None
                if kt == i:
                    add = (D0[:].rearrange("p h q -> p (h q)"), d0b)
                elif kt == i - 1:
                    add = (D1[:].rearrange("p h q -> p (h q)"), d1b)
                specs.append((dict(KT=KselT, ktb_=kselb, kcols=slice(kt * 128, (kt + 1) * 128), nk=128, qt=qt[:], qtb=qtb,
                                   nq=128, add_tile=add, mask_sel=(slice(kt * 128, (kt + 1) * 128), selT[:, :128])),
                              VselA[:, kt, :], vselb))
            run_tiles(accS, accSb, specs, 128)
            epilogue(accS, accSb, 128, gat, gatb, g, 1, o, ob, False)
            self.release(accSb)
            obf, obfb = rot(obs)
            self.copy(obf[:], o[:], rd=[ob], wr=[obfb], eng="pool")
            self.dma(o_s.ap()[q0:q0 + 128, g * 256:(g + 1) * 256], obf[:], rd=[obfb], wr=[self.db(o_s, (q0, g))])
    self.S.emit()

    KsS = kbig[:, :].rearrange("p (g t) -> p g t", g=4)
    KcS = kbig2[:, :].rearrange("p (g t) -> p g t", g=4)
    (VcSt, vcsb), = mk("VcS", [64, 4, 2048], BF16)
    VcS = VcSt[:, :, :]
    (KsN, ksnb), = mk("KsN", [64, 4, 8], BF16)
    (KwS, kwsb), = mk("KwS", [64, 4, 520], BF16)
    VsS = VselA[:, 0:68, :].rearrange("p (kt g) d -> p kt g d", g=4)
    VwS = VwinA[:, 0:20, :].rearrange("p (kt g) d -> p kt g d", g=4)
    pgs = mk("pg", [128, 1024], F32, 2)
    (idx, idxb), = mk("idx", [128, 256], I32)
    (ptb_, ptbb), = mk("ptbc", [128, 256], I32)
    winrs = mk("winr", [128, 512], F32, 2)
    (hidS, hidSb), = mk("hidS", [128, 512], BF16)
    (kcS, kcsb), = mk("kcS", [64, 4, 128], BF16)
    (vcS, vcsab), = mk("vcSA", [128, 4, 65], BF16)
    s32s = mk("s32", [128, 32], F32, 3)
    qtss = mk("qts", [64, 4, 8], BF16, 2)
    (ETs_, etsb), = mk("ETs", [128, 32], BF16)
    self.memset(vcS[:, :, 64:65], 1.0, wr=[vcsab])
    self.dma(ptb_[:], P["page_table"].ap().rearrange("s p -> (s p)").partition_broadcast(128), wr=[ptbb])
    self.ts(idx[:], ptb_[:], 128.0, P["iota_col"], ALU.mult, ALU.add, rd=[ptbb, cb], wr=[idxb])
    cache = P["cache"]
    for sq_ in range(16):
        tcol = NP + sq_ * 8
        for pg_i in range(16):
            pg, pgb = rot(pgs)
            nc = self.nc
            col = sq_ * 16 + pg_i
            self.S.add("pool", lambda pg=pg, col=col: nc.gpsimd.indirect_dma_start(
                out=pg[:, :], out_offset=None, in_=cache.ap(),
                in_offset=bass.IndirectOffsetOnAxis(ap=idx[:, col:col + 1], axis=0)), [idxb], [pgb], dma=True)
            self.copy(VsS[:, pg_i, :, 0:64], pg[:, 768:1024].rearrange("p (g d) -> p g d", g=4), rd=[pgb], wr=[vselb], eng="pool")
            for ty, dstT, dstb_ in ((0, KcS, kbig2b), (1, VcS, vcsb), (2, KsS, kbigb)):
                pb, pbb = self.bank()
                for g in range(4):
                    c0 = ty * 256 + g * 64
                    self.tr(pb[:64, g * 128:(g + 1) * 128], pg[:, c0:c0 + 64], self.ident_f[:, :], rd=[pgb, cb], wr=[pbb])
                self.copy(dstT[:, :, pg_i * 128:(pg_i + 1) * 128], pb[:64, :].rearrange("p (g t) -> p g t", g=4),
                          rd=[pbb], wr=[dstb_], eng="act" if ty % 2 else "dve")
        self.dma(KsN[:], kT_s.ap()[8:12, :, tcol:tcol + 8].rearrange("g d t -> d g t"), wr=[ksnb])
        self.dma(VsS[0:8, 16, :, 0:64], S["vsel"].ap()[tcol:tcol + 8, :].rearrange("t (g d) -> t g d", g=4), wr=[vselb])
        for kt in range(4):
            winr, winrb = rot(winrs)
            self.dma(winr[:], P["win_in"].ap()[sq_, kt * 128:(kt + 1) * 128, :], wr=[winrb])
            self.copy(VwS[:, kt, :, 0:64], winr[:, 256:512].rearrange("p (g d) -> p g d", g=4), rd=[winrb], wr=[vwinb], eng="pool")
            pb, pbb = self.bank()
            for g in range(4):
                self.tr(pb[:64, g * 128:(g + 1) * 128], winr[:, g * 64:(g + 1) * 64], self.ident_f[:, :], rd=[winrb, cb], wr=[pbb])
            self.copy(KwS[:, :, kt * 128:(kt + 1) * 128], pb[:64, :].rearrange("p (g t) -> p g t", g=4), rd=[pbb], wr=[kwsb])
        self.dma(KwS[:, :, 512:520], kT_s.ap()[12:16, :, tcol:tcol + 8].rearrange("g d t -> d g t"), wr=[kwsb])
        self.dma(VwS[0:8, 4, :, 0:64], S["vwin"].ap()[tcol:tcol + 8, :].rearrange("t (g d) -> t g d", g=4), wr=[vwinb])
        NCs = 127
        for kv in range(2):
            src, srcb = (KcS, kbig2b) if kv == 0 else (VcS, vcsb)
            pb, pbb = self.bank()
            for js in range(32):
                rhs = src[:, :, js:js + 16 * (NCs - 1) + 1:16]
                self.mm(pb[:, :4 * NCs].rearrange("p (g n) -> p g n", g=4), w1b[:, kv * 32 + js, :], rhs, js == 0, js == 31,
                        rd=[w1bb, srcb], wr=[pbb])
            self.act(hidS[:, :4 * NCs], pb[:, :4 * NCs], AF.Silu, rd=[pbb, cbb], wr=[hidSb], bias=cbias[:, kv:kv + 1])
            if kv == 0:
                pb2, pbb2 = self.bank()
                self.mm(pb2[:64, :4 * NCs], w2b[:, 0, :], hidS[:, :4 * NCs], True, True, rd=[w2bb, hidSb], wr=[pbb2])
                self.copy(kcS[:, :, :NCs], pb2[:64, :4 * NCs].rearrange("p (g n) -> p g n", g=4), rd=[pbb2], wr=[kcsb])
            else:
                for g in range(4):
                    pb2, pbb2 = self.bank()
                    self.mm(pb2[:NCs, :64], hidS[:, g * NCs:(g + 1) * NCs], w2b[:, 1, :], True, True, rd=[w2bb, hidSb], wr=[pbb2])
                    self.copy(vcS[:NCs, g, 0:64], pb2[:NCs, :64], rd=[pbb2], wr=[vcsab])
        gat, gatb = rot(gats)
        self.dma(gat[:8, :], S["gates"].ap()[tcol:tcol + 8, :], wr=[gatb])
        for g in range(4):
            qt, qtb = rot(qtss)
            self.dma(qt[:], qT_s.ap()[4 * g:4 * g + 4, :, tcol:tcol + 8].rearrange("h d q -> d h q"), wr=[qtb])
            o, ob = rot(os_)

            def small_s1(KT_ap, ktb_, nk, addt=None, mask=None, keep=False):
                pbS, pbSb = self.bank()
                self.mm(pbS[:nk, :32], KT_ap, qt[:].rearrange("p h q -> p (h q)"), True, True, rd=[ktb_, qtb], wr=[pbSb])
                if mask is not None:
                    pbM, pbMb = self.bank()
                    self.mm(pbM[:nk, :8], EE[:, mask], selT[:, :8], True, True, rd=[eeb, seltb], wr=[pbMb])
                src_ap = pbS[:nk, :32]
                if addt is not None:
                    s32, s32b = rot(s32s)
                    self.tt(s32[:nk, :], pbS[:nk, :32], addt[0], ALU.add, rd=[pbSb, addt[1]], wr=[s32b])
                    src_ap = s32[:nk, :]
                    rdl = [s32b]
                else:
                    rdl = [pbSb]
                if keep:
                    e_ap, e_b = ETs_[:nk, :], etsb
                else:
                    e_t, e_b = rot(ets)
                    e_ap = e_t[:nk, :32]
                self.act(e_ap, src_ap, AF.Exp, rd=rdl, wr=[e_b])
                if mask is not None:
                    ev = e_ap.rearrange("p (h q) -> p h q", h=4)
                    self.tt(ev, ev, pbM[:nk, :8].unsqueeze(1).to_broadcast([nk, 4, 8]), ALU.mult, rd=[e_b, pbMb], wr=[e_b])
                return e_ap, e_b

            def small_s2(acc, accb, e, VA_ap, vab_, first, last):
                e_ap, e_b = e
                for h in range(4):
                    self.mm(acc[:8, h * 65:(h + 1) * 65], e_ap[:, h * 8:(h + 1) * 8], VA_ap, first and h == 0, last,
                            rd=[e_b, vab_], wr=[accb], skip_group_check=True)

            def small_run(acc, accb, specs, look=2):
                pend = []
                nt = len(specs)
                for t, (s1, va, vb_) in enumerate(specs):
                    pend.append((small_s1(**s1), va, vb_))
                    if t >= look:
                        e, va2, vb2 = pend[t - look]
                        small_s2(acc, accb, e, va2, vb2, t - look == 0, t - look == nt - 1)
                for t in range(max(0, nt - look), nt):
                    e, va2, vb2 = pend[t]
                    small_s2(acc, accb, e, va2, vb2, t == 0, t == nt - 1)

            accC, accCb = self.bank(hold=True)
            small_run(accC, accCb, [(dict(KT_ap=kcS[:, g, :NCs], ktb_=kcsb, nk=NCs,
                                          addt=(Bs[:NCs, 4 * g:4 * g + 4, :].rearrange("p h q -> p (h q)"), bsb), keep=True),
                                     vcS[:NCs, g, :], vcsab)])
            pbD, pbDb = self.bank()
            self.mm(pbD[:, :32], self.ones_b[:NCs, :], ETs_[:NCs, :], True, True, rd=[etsb, cb], wr=[pbDb])
            self.ts(rden[:, :32], pbD[:, :32], 1e-30, None, ALU.max, rd=[pbDb], wr=[rdenb])
            self.recip(rden[:, :32], rden[:, :32], rd=[rdenb], wr=[rdenb])
            self.tt(PT[:NCs, :32], ETs_[:NCs, :], rden[:NCs, :32], ALU.mult, rd=[etsb, rdenb], wr=[ptb])
            nc = self.nc
            pin = PT[:NCs, :32].rearrange("p (h q) -> p q h", h=4)
            pout = PsT[:NCs, 0, :8]
            self.S.add("dve", lambda pout=pout, pin=pin: nc.vector.tensor_reduce(pout, pin, AX.X, ALU.add), [ptb], [pstb])
            pbI, pbIb = self.bank(hold=True)
            self.mm(pbI[:8, :128], PsT[:NCs, 0, :8], Am[:NCs, 0, :], True, True, rd=[pstb, cst], wr=[pbIb])
            epilogue(accC, accCb, 8, gat, gatb, g, 0, o, ob, True)
            self.release(accCb)
            select_blocks(8, pbI[:8, :128], pbIb, sampA[:, :], sampC[:, :], [cst])
            self.release(pbIb)
            accS, accSb = self.bank(hold=True)
            specs = []
            for kt in range(17):
                if kt < 16:
                    addt = (Ds1[:, 4 * g:4 * g + 4, :].rearrange("p h q -> p (h q)"), ds1b) if kt == 15 else None
                    specs.append((dict(KT_ap=KsS[:, g, kt * 128:(kt + 1) * 128], ktb_=kbigb, nk=128, addt=addt,
                                       mask=slice(kt * 128, (kt + 1) * 128)), VsS[:, kt, g, :], vselb))
                else:
                    specs.append((dict(KT_ap=KsN[:, g, :], ktb_=ksnb, nk=8,
                                       addt=(Ds0[:, 4 * g:4 * g + 4, :].rearrange("p h q -> p (h q)"), ds0b),
                                       mask=slice(2048, 2056)), VsS[0:8, 16, g, :], vselb))
            small_run(accS, accSb, specs)
            epilogue(accS, accSb, 8, gat, gatb, g, 1, o, ob, False)
            self.release(accSb)
            accW, accWb = self.bank(hold=True)
            specs = []
            for kt in range(5):
                if kt == 0:
                    addt = (dw4s[:, :], cst)
                elif kt == 3:
                    addt = (Ds1[:, 4 * g:4 * g + 4, :].rearrange("p h q -> p (h q)"), ds1b)
                elif kt == 4:
                    addt = (Ds0[:, 4 * g:4 * g + 4, :].rearrange("p h q -> p (h q)"), ds0b)
                else:
                    addt = None
                if kt < 4:
                    specs.append((dict(KT_ap=KwS[:, g, kt * 128:(kt + 1) * 128], ktb_=kwsb, nk=128, addt=addt),
                                  VwS[:, kt, g, :], vwinb))
                else:
                    specs.append((dict(KT_ap=KwS[:, g, 512:520], ktb_=kwsb, nk=8, addt=addt), VwS[0:8, 4, g, :], vwinb))
            small_run(accW, accWb, specs)
            epilogue(accW, accWb, 8, gat, gatb, g, 2, o, ob, False)
            self.release(accWb)
            obf, obfb = rot(obs)
            self.copy(obf[:8, :], o[:8, :], rd=[ob], wr=[obfb], eng="pool")
            self.dma(o_s.ap()[tcol:tcol + 8, g * 256:(g + 1) * 256], obf[:8, :], rd=[obfb], wr=[self.db(o_s, (tcol, g))])
    self.end_phase()

    self.set_wslots(1, 64)
    (wout, woutb), = mk("wout", [128, 8, 1024], BF16)
    self.dma(wout[:], P["w_out_b"].ap().rearrange("(kc p) f -> p kc f", p=128), rd=[self.db(P["w_out_b"])], wr=[woutb])
    (xg, xgb), = mk("xg", [128, 8, 512], F32)
    xv = xres.ap().rearrange("(kc p) t -> p kc t", p=128)
    (otm, otmb), = mk("otm", [128, 4, 1024], BF16)
    (oT, oTb), = mk("oT", [128, 8, 512], BF16)
    xos = mk("xo", [128, 512], F32, 2)
    for gi, (t0, n) in enumerate(self.groups):
        nchk = n // 128
        self.dma(xg[:, :, :n], xv[:, :, t0:t0 + n], rd=[self.db(xres, gi)], wr=[xgb])
        self.dma(otm[:, :nchk, :], o_s.ap()[t0:t0 + n, :].rearrange("(c p) f -> p c f", p=128), wr=[otmb])
        for ci in range(nchk):
            for j4 in range(2):
                pb, pbb = self.bank()
                pbv = pb[:, 0:256].bitcast(BF16)
                for jj in range(4):
                    j = j4 * 4 + jj
                    self.tr(pbv[:, jj * 128:(jj + 1) * 128], otm[:, ci, j * 128:(j + 1) * 128], self.ident_b[:], rd=[otmb, cb], wr=[pbb])
                self.copy(oT[:, j4 * 4:(j4 + 1) * 4, ci * 128:(ci + 1) * 128], pbv.rearrange("p (j t) -> p j t", j=4),
                          rd=[pbb], wr=[oTb], eng="act" if j4 % 2 else "dve")
        for dc in range(8):
            pb, pbb = self.bank()
            for kc in range(8):
                self.mm(pb[:, :n], wout[:, kc, dc * 128:(dc + 1) * 128], oT[:, kc, :n], kc == 0, kc == 7, rd=[woutb, oTb], wr=[pbb])
            xo, xob = rot(xos)
            self.tt(xo[:, :n], pb[:, :n], xg[:, dc, :n], ALU.add, rd=[pbb, xgb], wr=[xob])
            self.dma(xres.ap()[dc * 128:(dc + 1) * 128, t0:t0 + n], xo[:, :n], rd=[xob], wr=[self.db(xres, gi)])


KB.nsa_layer = nsa_layer
import math

def final_norm(self, xres, gcol, yT):
    self.set_wslots(1, 64)
    xg = self.sb("xg", [128, 8, 512], F32)
    xgb = Buf("xg")
    yo = self.sb("yo", [128, 8, 512], F32)
    yob = Buf("yo")
    sqs = [(self.sb("sq", [128, 512], BF16), Buf("sq")) for _ in range(2)]
    rs = self.sb("rs", [128, 512], F32)
    rsb = Buf("rs")
    xv = xres.ap().rearrange("(kc p) t -> p kc t", p=128)
    yv = yT.ap().rearrange("(kc p) t -> p kc t", p=128)
    for gi, (t0, n) in enumerate(self.groups):
        self.dma(xg[:, :, :n], xv[:, :, t0:t0 + n], rd=[self.db(xres, gi)], wr=[xgb])
        pb, pbb = self.bank()
        for kc in range(8):
            sq, sqb = sqs[kc % 2]
            self.act(sq[:, :n], xg[:, kc, :n], AF.Square, rd=[xgb], wr=[sqb])
            self.mm(pb[:, :n], self.ones_b[:], sq[:, :n], kc == 0, kc == 7, rd=[sqb, self.cb], wr=[pbb])
        self.act(rs[:, :n], pb[:, :n], AF.Sqrt, rd=[pbb], wr=[rsb], scale=1.0 / 1024, bias=self.epscol[:, 0:1])
        self.recip(rs[:, :n], rs[:, :n], rd=[rsb], wr=[rsb])
        for kc in range(8):
            self.stt(yo[:, kc, :n], xg[:, kc, :n], gcol[:, kc:kc + 1], rs[:, :n], ALU.mult, ALU.mult,
                     rd=[xgb, rsb, self.cb], wr=[yob])
        self.dma(yv[:, :, t0:t0 + n], yo[:, :, :n], rd=[yob], wr=[self.db(yT)])


KB.final_norm = final_norm

NP_, NS_ = 8192, 128


def build():
    nc = bass.Bass("TRN2", target_bir_lowering=False)
    k = KB(nc, NP_, NS_)
    NT = k.NT
    k.setup_common()
    cm = k.dram_in("c_masks", [128, 6, 128])
    csm = k.dram_in("c_sm", [128, 16, 128])
    cseq = k.dram_in("c_seq", [128, 16])
    gm = k.dram_in("g_mix", [128, 4, 8])
    gf = k.dram_in("g_ffn", [128, 4, 8])
    go = k.dram_in("g_out", [128, 8])
    psc = k.dram_in("pool_scale", [128, 8])
    mf = k.gsb("mf", [128, 6, 128], F32)
    negs = k.gsb("negs", [128, 2, 128], BF16)
    smb = k.gsb("smb", [128, 16, 128], BF16)
    cs = k.gsb("cseq", [128, 16], F32)
    gmix = k.gsb("gmix", [128, 4, 8], F32)
    gffn = k.gsb("gffn", [128, 4, 8], F32)
    gout = k.gsb("gout", [128, 8], F32)
    pscs = k.gsb("psc", [128, 8], F32)
    flipt = k.gsb("flip", [128, 128], F32)
    iotac = k.gsb("iotac", [128, 1], F32)
    cb = k.cb
    k.dma(mf[:], cm.ap(), wr=[cb])
    k.dma(cs[:], cseq.ap(), wr=[cb])
    k.dma(gmix[:], gm.ap(), wr=[cb])
    k.dma(gffn[:], gf.ap(), wr=[cb])
    k.dma(gout[:], go.ap(), wr=[cb])
    k.dma(pscs[:], psc.ap(), wr=[cb])
    k.dma(flipt[:], k.dram_in("c_flip", [128, 128]).ap(), wr=[cb])
    k.dma(iotac[:], k.dram_in("c_iota", [128, 1]).ap(), wr=[cb])
    k.copy(negs[:], mf[:, 2:4, :], rd=[cb], wr=[cb])
    k.masks = {"tri": mf[:, 0, :], "trib": mf[:, 1, :], "neg": negs[:, 0, :], "negb": negs[:, 1, :],
               "blk": mf[:, 4, :], "onesf": mf[:, 5, :], "sm": smb[:, :, :], "cseq": cs[:, :]}
    xT = k.dram_in("xT", [1024, NT])
    yT = k.dram_out("yT", [1024, NT])
    xres = k.dram_tmp("xres", [1024, NT], F32)
    wf = {}
    wb = {}

    def wdecl(name, rows, cols):
        wf[name] = k.dram_in(name, [rows, cols])
        wb[name] = k.dram_tmp(name + "_b", [rows, cols], BF16)

    for i in range(4):
        wdecl("w_up%d" % i, 1024, 4096)
        wdecl("w_dn%d" % i, 4096, 1024)
    for li in range(2):
        wdecl("ssd_in%d" % li, 1024, 5152)
        wdecl("ssd_out%d" % li, 2048, 1024)
    wdecl("pool_w", 1024, 256)
    wdecl("nsa_in", 1024, 2608)
    wdecl("nsa_out", 1024, 1024)
    wdecl("cmp_w1", 4096, 128)
    SP = []
    for li in range(2):
        SP.append({
            "w_in_b": wb["ssd_in%d" % li], "w_out_b": wb["ssd_out%d" % li],
            "convw": k.dram_in("convw%d" % li, [128, 24, 4]), "convb": k.dram_in("convb%d" % li, [128, 24]),
            "dtb": k.dram_in("dtb%d" % li, [128, 32]), "alog": k.dram_in("alog%d" % li, [128, 32]),
            "dskip": k.dram_in("dskip%d" % li, [128, 32]), "normg": k.dram_in("normg%d" % li, [128, 2048]),
            "ssmT_in": k.dram_in("ssmT_in%d" % li, [16, 128, 2048]),
            "convT_in": k.dram_in("convT_in%d" % li, [128, 24, 16, 3]),
            "conv_pp": k.dram_out("conv_pp%d" % li, [128, 24, 3]),
            "conv_ps": k.dram_out("conv_ps%d" % li, [128, 24, 16, 3]),
            "ssm_pp": k.dram_out("ssm_pp%d" % li, [128, 2048]),
            "ssm_ps": k.dram_out("ssm_ps%d" % li, [16, 128, 2048]),
        })
    invc = k.dram_in("c_invc", [128, 4, 16])
    spoolT = k.dram_in("spoolT", [1024, 16 * 15])
    pool_pp = k.dram_out("pool_pp", [128, 8, 15])
    pool_ps = k.dram_out("pool_ps", [128, 8, 16, 15])
    scr = {"qT": k.dram_tmp("qT_s", [16, 64, NT], BF16), "kT": k.dram_tmp("kT_s", [16, 64, NT], BF16),
           "vsel": k.dram_tmp("vsel_s", [NT, 256], BF16), "vwin": k.dram_tmp("vwin_s", [NT, 256], BF16),
           "gates": k.dram_tmp("gates_s", [NT, 48], F32), "o": k.dram_tmp("o_s", [NT, 1024], BF16),
           "vext": k.dram_tmp("vext", [16, 3072], F32),
           "gT": k.dram_tmp("gT_s", [48, NT], F32), "oT": k.dram_tmp("oT_s", [16, 64, NP_], BF16)}
    NP = {"w_in_b": wb["nsa_in"], "w_out_b": wb["nsa_out"], "w1_b": wb["cmp_w1"], "scr": scr,
          "flip": flipt, "iota_col": iotac[:, 0:1],
          "rel_bias": k.dram_in("rel_bias", [32, 16]), "rb31": k.dram_in("rb31", [32, 16]),
          "c_ohd": k.dram_in("c_ohd", [32, 128]), "c_EE": k.dram_in("c_EE", [128, 8192]),
          "c_A": k.dram_in("c_A", [128, 4, 128]), "c_J": k.dram_in("c_J", [128, 128]),
          "c_col0": k.dram_in("c_col0", [128, 128]), "c_shift": k.dram_in("c_shift", [32, 384]),
          "c_neglow": k.dram_in("c_neglow", [128, 128]), "c_dw4s": k.dram_in("c_dw4s", [128, 32]),
          "c_sampA": k.dram_in("c_sampA", [8, 128]), "c_sampC": k.dram_in("c_sampC", [8, 128]),
          "cmp_w2": k.dram_in("cmp_w2", [128, 128]), "peT": k.dram_in("peT", [64, 64]),
          "cache": k.dram_in("cache", [2560 * 128, 1024]), "page_table": k.dram_in("page_table", [16, 16], I32),
          "win_in": k.dram_in("win_in", [16, 512, 512]),
          "kv_pp": k.dram_out("kv_pp", [NP_, 1024]), "kv_ps": k.dram_out("kv_ps", [128, 1024]),
          "win_pp": k.dram_out("win_pp", [512, 512]), "win_ps": k.dram_out("win_ps", [16, 512, 512])}
    smf = k.sb("smf", [128, 16, 128], F32)
    k.dma(smf[:], csm.ap(), wr=[cb])
    k.copy(smb[:], smf[:], rd=[cb], wr=[cb])
    for gi, (t0, n) in enumerate(k.groups):
        k.dma(xres.ap()[:, t0:t0 + n], xT.ap()[:, t0:t0 + n], wr=[k.db(xres, gi)])
    for name in wf:
        r, c = wf[name].shape
        k.cast_weight(wf[name], wb[name], r, c)
    k.end_phase()
    k.w_up = [wb["w_up%d" % i] for i in range(4)]
    k.w_dn = [wb["w_dn%d" % i] for i in range(4)]
    for i in range(4):
        kind, li = i % 3, i // 3
        if kind == 0:
            k.ssd_layer(xres, gmix[:, i, :], SP[li])
        elif kind == 1:
            k.pool_layer(xres, gmix[:, i, :], pscs[:, :], wb["pool_w"], invc, spoolT, pool_pp, pool_ps)
        else:
            k.nsa_layer(xres, gmix[:, i, :], NP)
        k.end_phase()
        k.mlp_layer(i, xres, gffn[:, i, :])
        k.end_phase()
    k.final_norm(xres, gout[:, :], yT)
    k.end_phase()
    return nc, k


def _consts():
    s = np.arange(128)[:, None]
    t = np.arange(128)[None, :]
    same = (s // 8) == (t // 8)
    tri = (s <= t).astype(np.float32)
    trib = (same & (s <= t)).astype(np.float32)
    neg = np.where(s <= t, 0.0, -30000.0).astype(np.float32)
    negb = np.where(same & (s <= t), 0.0, -30000.0).astype(np.float32)
    blk = same.astype(np.float32)
    onesf = np.ones((128, 128), np.float32)
    masks = np.stack([tri, trib, neg, negb, blk, onesf], axis=1)
    sm = np.zeros((128, 16, 128), np.float32)
    for q in range(16):
        sm[:, q, q * 8:(q + 1) * 8] = 1.0
    cseq = np.zeros((128, 16), np.float32)
    for q in range(16):
        cseq[q * 8:(q + 1) * 8, q] = 1.0
    invc = np.zeros((128, 4, 16), np.float32)
    for g, w in enumerate((2, 4, 8, 16)):
        invc[:, g, :] = 1.0 / np.minimum(np.arange(16) + 1, w)
    def bucket(d):
        d = np.asarray(d)
        nf = np.maximum(d, 1).astype(np.float32)
        large = 16 + (np.log(nf / np.float32(16)) / np.float32(math.log(128 / 16)) * np.float32(16)).astype(np.int32)
        large = np.minimum(large, 31)
        return np.where(d < 16, d, large)
    ohd = np.zeros((32, 128), np.float32)
    ohd[bucket(np.arange(128)), np.arange(128)] = 1.0
    EE = (np.arange(8192)[None, :] // 64 == np.arange(128)[:, None]).astype(np.float32)
    nn_ = np.arange(512)[:, None]
    jj = np.arange(128)[None, :]
    Afull = ((nn_ >= 4 * jj) & (nn_ <= 4 * jj + 3)).astype(np.float32) + ((nn_ + 1 >= 4 * jj) & (nn_ + 1 <= 4 * jj + 3)).astype(np.float32)
    Afull[511] = 0.0
    A = np.ascontiguousarray(Afull.reshape(4, 128, 128).transpose(1, 0, 2))
    J = (np.arange(128)[None, :] - (np.arange(128)[:, None] >= 64)).astype(np.float32)
    col0 = np.zeros((128, 128), np.float32); col0[:, 0] = 1.0
    shift = np.zeros((32, 384), np.float32)
    shift[np.arange(32), np.arange(32) + 160] = 1.0
    flipm = np.eye(128, dtype=np.float32)[::-1].copy()
    neglow = np.where(s < t, -30000.0, 0.0).astype(np.float32)
    dw4 = np.where(np.arange(128)[:, None] < np.arange(8)[None, :], -30000.0, 0.0).astype(np.float32)
    dw4s = np.ascontiguousarray(np.tile(dw4, (1, 4)))
    sA = np.zeros((8, 128), np.float32); sA[:, :33] = 1.0
    forced = np.zeros((8, 128), np.float32); forced[:, [0, 31, 32]] = 1.0
    sC = 1000.0 * forced + sA - 1.0
    return {"c_ident": np.eye(128, dtype=np.float32), "c_masks": masks, "c_sm": sm, "c_seq": cseq, "c_invc": invc,
            "c_ohd": ohd, "c_EE": EE, "c_A": A, "c_J": J, "c_col0": col0, "c_shift": shift, "c_flip": flipm,
            "c_neglow": neglow, "c_dw4s": dw4s, "c_sampA": sA, "c_sampC": sC,
            "c_iota": np.arange(128, dtype=np.float32).reshape(128, 1)}


def _pk(v):
    return np.ascontiguousarray(np.asarray(v, np.float32).reshape(-1, 128).T)


def _rep(v):
    v = np.asarray(v, np.float32).reshape(1, -1)
    return np.ascontiguousarray(np.repeat(v, 128, axis=0))


_CACHE = {}


def kernel(**inp):
    A = {k_: np.asarray(v) for k_, v in inp.items()}
    if "nc" not in _CACHE:
        _CACHE["nc"] = build()
    nc, kb = _CACHE["nc"]
    C = _consts()
    cache2d = np.ascontiguousarray(A["cache_nsa_kv"][0].reshape(2560 * 128, 1024), dtype=np.float32)
    in_maps = []
    for c in range(8):
        b = c // 4
        sl = slice(16 * c, 16 * c + 16)
        m = dict(C)
        xs = A["x_sample"][sl].reshape(128, 1024)
        m["xT"] = np.ascontiguousarray(np.concatenate([A["x_prompt"][b], xs], axis=0).T)
        m["g_mix"] = np.ascontiguousarray(np.stack([_pk(A["norm_mix"][i]) for i in range(4)], axis=1))
        m["g_ffn"] = np.ascontiguousarray(np.stack([_pk(A["norm_ffn"][i]) for i in range(4)], axis=1))
        m["g_out"] = _pk(A["norm_out"])
        m["pool_scale"] = _pk(A["pool_scale"][0])
        for i in range(4):
            m["w_up%d" % i] = A["ffn_w_up"][i]
            m["w_dn%d" % i] = A["ffn_w_down"][i]
        for li in range(2):
            m["ssd_in%d" % li] = A["ssd_w_in"][li]
            m["ssd_out%d" % li] = A["ssd_w_out"][li]
            m["convw%d" % li] = np.ascontiguousarray(A["ssd_conv_w"][li].reshape(4, 24, 128).transpose(2, 1, 0))
            m["convb%d" % li] = np.ascontiguousarray(A["ssd_conv_b"][li].reshape(24, 128).T)
            m["dtb%d" % li] = _rep(A["ssd_dt_bias"][li])
            m["alog%d" % li] = _rep(A["ssd_a_log"][li])
            m["dskip%d" % li] = _rep(A["ssd_d"][li])
            m["normg%d" % li] = _rep(A["ssd_norm"][li])
            m["ssmT_in%d" % li] = np.ascontiguousarray(A["state_ssm"][li, sl].transpose(0, 3, 1, 2).reshape(16, 128, 2048))
            m["convT_in%d" % li] = np.ascontiguousarray(A["state_conv"][li, sl].reshape(16, 3, 24, 128).transpose(3, 2, 0, 1))
        m["pool_w"] = np.ascontiguousarray(A["pool_w"][0].reshape(1024, 256))
        m["nsa_in"] = A["nsa_w_in"][0]
        m["nsa_out"] = A["nsa_w_out"][0]
        m["spoolT"] = np.ascontiguousarray(A["state_pool"][0, sl].transpose(2, 0, 1).reshape(1024, 240))
        m["win_in"] = np.ascontiguousarray(A["state_nsa_win"][0, sl].reshape(16, 512, 512))
        m["cmp_w1"] = A["nsa_cmp_w1"][0].reshape(4096, 128)
        m["cmp_w2"] = np.ascontiguousarray(A["nsa_cmp_w2"][0].transpose(1, 0, 2).reshape(128, 128))
        m["peT"] = np.ascontiguousarray(A["nsa_cmp_pe"][0].transpose(2, 0, 1).reshape(64, 64))
        m["rel_bias"] = A["rel_bias"]
        m["rb31"] = np.ascontiguousarray(np.repeat(A["rel_bias"][31:32], 32, axis=0))
        m["cache"] = cache2d
        m = {k_: np.ascontiguousarray(v, dtype=np.float32) for k_, v in m.items()}
        m["page_table"] = np.ascontiguousarray(A["page_table"][sl], dtype=np.int32)
        in_maps.append(m)
    res = run_bass_kernel_spmd(nc, in_maps, core_ids=list(range(8)))
    R = res.results
    f32 = np.float32
    y_prompt = np.stack([R[4 * b]["yT"][:, :8192].T for b in range(2)]).astype(f32)
    y_sample = np.concatenate([R[c]["yT"][:, 8192:].T.reshape(16, 8, 1024) for c in range(8)]).astype(f32)
    kv_p = np.stack([R[4 * b]["kv_pp"].reshape(8192, 4, 4, 64) for b in range(2)])[None].astype(f32)
    kv_s = np.concatenate([R[c]["kv_ps"].reshape(16, 8, 4, 4, 64) for c in range(8)])[None].astype(f32)
    win_p = np.stack([R[4 * b]["win_pp"].reshape(512, 2, 4, 64) for b in range(2)])[None].astype(f32)
    win_s = np.concatenate([R[c]["win_ps"].reshape(16, 512, 2, 4, 64) for c in range(8)])[None].astype(f32)
    ssm_p = np.stack([np.stack([R[4 * b]["ssm_pp%d" % li].reshape(128, 32, 64).transpose(1, 2, 0) for b in range(2)])
                      for li in range(2)]).astype(f32)
    ssm_s = np.stack([np.concatenate([R[c]["ssm_ps%d" % li].reshape(16, 128, 32, 64).transpose(0, 2, 3, 1)
                                      for c in range(8)]) for li in range(2)]).astype(f32)
    conv_p = np.stack([np.stack([R[4 * b]["conv_pp%d" % li].transpose(2, 1, 0).reshape(3, 3072) for b in range(2)])
                       for li in range(2)]).astype(f32)
    conv_s = np.stack([np.concatenate([R[c]["conv_ps%d" % li].transpose(2, 3, 1, 0).reshape(16, 3, 3072)
                                       for c in range(8)]) for li in range(2)]).astype(f32)
    pool_p = np.stack([R[4 * b]["pool_pp"].transpose(2, 1, 0).reshape(15, 1024) for b in range(2)])[None].astype(f32)
    pool_s = np.concatenate([R[c]["pool_ps"].transpose(2, 3, 1, 0).reshape(16, 15, 1024) for c in range(8)])[None].astype(f32)
    return (y_prompt, y_sample, kv_p, kv_s, win_p, win_s, ssm_p, ssm_s, conv_p, conv_s, pool_p, pool_s)
```

```python
import numpy as np
import concourse.bass as bass
import concourse.mybir as mybir
from concourse.bass_utils import run_bass_kernel_spmd
from contextlib import ExitStack

F32 = mybir.dt.float32
BF16 = mybir.dt.bfloat16
I32 = mybir.dt.int32
U32 = mybir.dt.uint32
ALU = mybir.AluOpType
AF = mybir.ActivationFunctionType
AX = mybir.AxisListType


class Buf:
    __slots__ = ("name", "lw", "rd")

    def __init__(self, name=""):
        self.name = name
        self.lw = None
        self.rd = {}


class Op:
    __slots__ = ("eng", "fn", "deps", "sig", "sigval", "dma", "dsem", "dval", "prev")

    def __init__(self, eng, fn, dma):
        self.eng = eng
        self.fn = fn
        self.deps = []
        self.sig = False
        self.sigval = 0
        self.dma = dma
        self.dsem = None
        self.dval = 0
        self.prev = None


ENGS = ["pe", "act", "dve", "pool", "sp"]
NDMA = {"sp": 12, "pool": 6, "act": 4}


class Sched:
    def __init__(self, nc, es):
        self.nc = nc
        self.q = {e: [] for e in ENGS}
        self.ndma = {e: 0 for e in ENGS}
        self.dmas = {e: [] for e in ENGS}
        self.csem = {e: es.enter_context(nc.semaphore("c_" + e)) for e in ["pe", "act", "dve", "pool"]}
        self.dsem = {e: [es.enter_context(nc.semaphore("d_%s%d" % (e, i))) for i in range(NDMA[e])]
                     for e in NDMA}
        self.ccount = {e: 0 for e in ["pe", "act", "dve", "pool"]}
        self.nops = 0

    def add(self, eng, fn, rd=(), wr=(), dma=False):
        op = Op(eng, fn, dma)
        deps = {}

        def need(d, raw):
            if d is None or d is op:
                return
            if (not d.dma) and (not dma) and d.eng == eng and (eng == "pe" or not raw):
                return
            deps[id(d)] = d

        for b in rd:
            need(b.lw, True)
        for b in wr:
            need(b.lw, False)
            for d in b.rd.values():
                need(d, False)
        op.deps = list(deps.values())
        for d in op.deps:
            d.sig = True
        if dma:
            n = self.ndma[eng]
            self.ndma[eng] = n + 1
            K = NDMA[eng]
            op.dsem = n % K
            op.dval = 16 * (n // K + 1)
            if n >= K:
                op.prev = self.dmas[eng][n - K]
            self.dmas[eng].append(op)
        self.q[eng].append(op)
        self.nops += 1
        for b in rd:
            b.rd[(eng + str(self.nops)) if dma else eng] = op
        for b in wr:
            b.lw = op
            b.rd = {}
        return op

    def emit(self):
        nc = self.nc
        with ExitStack() as es:
            csem, dsem = self.csem, self.dsem
            for e in ["pe", "act", "dve", "pool"]:
                c = self.ccount[e]
                for op in self.q[e]:
                    if op.sig and not op.dma:
                        c += 1
                        op.sigval = c
                self.ccount[e] = c
            block = es.enter_context(nc.Block())
            engobj = {"pe": "tensor", "act": "scalar", "dve": "vector", "pool": "gpsimd", "sp": "sync"}

            def run(e, eng):
                waited = {}

                def wait(sem, val):
                    k = id(sem)
                    if waited.get(k, 0) >= val:
                        return
                    waited[k] = val
                    eng.wait_ge(sem, val)

                def wait_op(d):
                    if d.dma:
                        wait(dsem[d.eng][d.dsem], d.dval)
                    else:
                        wait(csem[d.eng], d.sigval)

                for op in self.q[e]:
                    for d in op.deps:
                        wait_op(d)
                    if op.prev is not None:
                        wait_op(op.prev)
                    ins = op.fn()
                    if op.dma:
                        ins.then_inc(dsem[e][op.dsem], 16)
                    elif op.sig:
                        ins.then_inc(csem[e], 1)
                if e == "sp":
                    for ee in NDMA:
                        K = NDMA[ee]
                        n = self.ndma[ee]
                        for i in range(min(K, n)):
                            cnt = (n - 1 - i) // K + 1
                            wait(dsem[ee][i], 16 * cnt)

            @block.tensor
            def _(eng):
                run("pe", eng)

            @block.scalar
            def _(eng):
                run("act", eng)

            @block.vector
            def _(eng):
                run("dve", eng)

            @block.gpsimd
            def _(eng):
                run("pool", eng)

            @block.sync
            def _(eng):
                run("sp", eng)

        self.q = {e: [] for e in ENGS}

D = 1024
DFF = 4096
EPS = 1e-6


class KB:
    def __init__(self, nc, NP, NS):
        self.nc = nc
        self.NP, self.NS = NP, NS
        self.NT = NP + NS
        self.es = ExitStack()
        self.pes = ExitStack()
        self.S = Sched(nc, self.es)
        self._db = {}
        self.din = {}
        self.dout = {}
        self.groups = [(t, 512) for t in range(0, NP, 512)] + ([(NP, NS)] if NS else [])
        self._uid = 0
        self.store_q = "pool"

    def uid(self, p):
        self._uid += 1
        return "%s_%d" % (p, self._uid)

    def dram_in(self, name, shape, dt=F32):
        t = self.nc.dram_tensor(name, list(shape), dt, kind="ExternalInput")
        self.din[name] = t
        return t

    def dram_out(self, name, shape, dt=F32):
        t = self.nc.dram_tensor(name, list(shape), dt, kind="ExternalOutput")
        self.dout[name] = t
        return t

    def dram_tmp(self, name, shape, dt):
        return self.nc.dram_tensor(name, list(shape), dt, kind="Internal")

    def sb(self, name, shape, dt, es=None):
        return (es or self.pes).enter_context(self.nc.sbuf_tensor(self.uid(name), list(shape), dt))

    def gsb(self, name, shape, dt):
        return self.es.enter_context(self.nc.sbuf_tensor(self.uid(name), list(shape), dt))

    def db(self, t, sub=None):
        k = (t.name, sub)
        if k not in self._db:
            self._db[k] = Buf(str(k))
        return self._db[k]

    def end_phase(self):
        self.S.emit()
        self.pes.close()
        self.pes = ExitStack()
        self._cs = None

    def ps(self, name, shape, dt):
        return self.es.enter_context(self.nc.psum_tensor(self.uid(name), list(shape), dt))

    def dma(self, out, in_, rd=(), wr=(), q="sp", **kw):
        eng = {"sp": self.nc.sync, "pool": self.nc.gpsimd, "act": self.nc.scalar}[q]
        return self.S.add(q, lambda: eng.dma_start(out=out, in_=in_, **kw), rd, wr, dma=True)

    def mm(self, out, lhsT, rhs, start, stop, rd=(), wr=(), **kw):
        nc = self.nc
        return self.S.add("pe", lambda: nc.tensor.matmul(out, lhsT, rhs, start=start, stop=stop, **kw), rd, wr)

    def tr(self, out, in_, ident, rd=(), wr=()):
        nc = self.nc
        return self.S.add("pe", lambda: nc.tensor.transpose(out, in_, ident), rd, wr)

    def act(self, out, in_, func, rd=(), wr=(), **kw):
        nc = self.nc
        return self.S.add("act", lambda: nc.scalar.activation(out, in_, func, **kw), rd, wr)

    def tt(self, out, in0, in1, op, rd=(), wr=(), eng="dve"):
        e = self.nc.vector if eng == "dve" else self.nc.gpsimd
        return self.S.add(eng, lambda: e.tensor_tensor(out, in0, in1, op), rd, wr)

    def ts(self, out, in0, s1, s2, op0, op1=None, rd=(), wr=(), eng="dve", **kw):
        e = self.nc.vector if eng == "dve" else self.nc.gpsimd
        if op1 is None:
            return self.S.add(eng, lambda: e.tensor_scalar(out, in0, s1, None, op0, **kw), rd, wr)
        return self.S.add(eng, lambda: e.tensor_scalar(out, in0, s1, s2, op0, op1, **kw), rd, wr)

    def stt(self, out, in0, scalar, in1, op0, op1, rd=(), wr=(), **kw):
        nc = self.nc
        return self.S.add("dve", lambda: nc.vector.scalar_tensor_tensor(out, in0, scalar, in1, op0, op1, **kw), rd, wr)

    def copy(self, out, in_, rd=(), wr=(), eng="dve"):
        nc = self.nc
        if eng == "act":
            return self.S.add("act", lambda: nc.scalar.copy(out, in_), rd, wr)
        e = nc.vector if eng == "dve" else nc.gpsimd
        return self.S.add(eng, lambda: e.tensor_copy(out, in_), rd, wr)

    def recip(self, out, in_, rd=(), wr=()):
        nc = self.nc
        return self.S.add("dve", lambda: nc.vector.reciprocal(out, in_), rd, wr)

    def memset(self, ap, val, wr=(), eng="dve"):
        e = self.nc.vector if eng == "dve" else self.nc.gpsimd
        return self.S.add(eng, lambda: e.memset(ap, val), (), wr)

    def setup_common(self):
        nc = self.nc
        self.pbank = [self.ps("pb", [128, 512], F32) for _ in range(8)]
        self.pbuf = [Buf("pb%d" % i) for i in range(8)]
        self._pn = 0
        self._held = set()
        cst = self.dram_in("c_ident", [128, 128])
        self.ident_f = self.gsb("identf", [128, 128], F32)
        self.ident_b = self.gsb("identb", [128, 128], BF16)
        self.ones_b = self.gsb("onesb", [128, 128], BF16)
        self.cb = Buf("consts")
        self.dma(self.ident_f[:], cst.ap(), wr=[self.cb])
        self.copy(self.ident_b[:], self.ident_f[:], rd=[self.cb], wr=[self.cb])
        self.memset(self.ones_b[:], 1.0, wr=[self.cb])
        self.epscol = self.gsb("eps", [128, 1], F32)
        self.memset(self.epscol[:], EPS, wr=[self.cb])
        self.onecol = self.gsb("one", [128, 1], F32)
        self.memset(self.onecol[:], 1.0, wr=[self.cb])

    def set_wslots(self, k, size=8192):
        self.NW, self.WS = k, size
        self.wt = [self.sb("wt", [128, size], BF16) for _ in range(k)]
        self.wb = [Buf("wt%d" % i) for i in range(k)]
        self._wn = 0

    def actmul(self, out, in_, c, rd=(), wr=()):
        nc = self.nc
        return self.S.add("act", lambda: nc.scalar.mul(out, in_, c), rd, wr)

    def bank(self, hold=False):
        while True:
            i = self._pn % 8
            self._pn += 1
            if i not in self._held:
                break
        if hold:
            self._held.add(i)
        return self.pbank[i], self.pbuf[i]

    def release(self, buf):
        self._held.discard(self.pbuf.index(buf))

    def wslot(self):
        i = self._wn % self.NW
        self._wn += 1
        return self.wt[i], self.wb[i]

    def cast_weight(self, src, dst, rows, cols):
        if getattr(self, "_cs", None) is None:
            self._cs = [(self.sb("cs", [128, 4096], F32), self.sb("cd", [128, 4096], BF16), Buf("cs"), Buf("cd"))
                        for _ in range(2)]
            self._cn = 0
        total = rows * cols
        per = total // 128
        sv = src.reshape([128, per]).ap()
        dv = dst.reshape([128, per]).ap()
        for c0 in range(0, per, 4096):
            n = min(4096, per - c0)
            a, b, ba, bb = self._cs[self._cn % 2]
            eng = ["dve", "act", "pool"][self._cn % 3]
            self._cn += 1
            self.dma(a[:, :n], sv[:, c0:c0 + n], wr=[ba])
            self.copy(b[:, :n], a[:, :n], rd=[ba], wr=[bb], eng=eng)
            self.dma(dv[:, c0:c0 + n], b[:, :n], rd=[bb], wr=[self.db(dst)])

    def rmsnorm_group(self, xg, xbuf, n, gcol, hT, hbuf, KC=8, tmp=None):
        sq, sqb, rs, rsb = tmp
        pb, pbb = self.bank()
        for kc in range(KC):
            self.act(sq[:, kc, :n], xg[:, kc, :n], AF.Square, rd=[xbuf], wr=[sqb])
        for kc in range(KC):
            self.mm(pb[:, :n], self.ones_b[:], sq[:, kc, :n], kc == 0, kc == KC - 1, rd=[sqb, self.cb], wr=[pbb])
        self.act(rs[:, :n], pb[:, :n], AF.Sqrt, rd=[pbb], wr=[rsb], scale=1.0 / (128 * KC), bias=self.epscol[:, 0:1])
        self.recip(rs[:, :n], rs[:, :n], rd=[rsb], wr=[rsb])
        for kc in range(KC):
            self.stt(hT[:, kc, :n], xg[:, kc, :n], gcol[:, kc:kc + 1], rs[:, :n], ALU.mult, ALU.mult,
                     rd=[xbuf, rsb, self.cb], wr=[hbuf])

    def dense_fm(self, wv, wrd, KC, F, hT, hbuf, n, evac, fpiece=None, cwid=128, skip=None):
        if fpiece is None:
            fpiece = self.WS // KC
        for f0 in range(0, F, fpiece):
            fw = min(fpiece, F - f0)
            wt, wb = self.wslot()
            wtv = wt[:, :KC * fw].rearrange("p (kc f) -> p kc f", kc=KC)
            self.dma(wtv, wv[:, :, f0:f0 + fw], rd=[wrd], wr=[wb])
            for c0 in range(0, fw, cwid):
                cw = min(cwid, fw - c0)
                if skip is not None and skip((f0 + c0) // cwid):
                    continue
                pb, pbb = self.bank()
                for kc in range(KC):
                    self.mm(pb[:cw, :n], wtv[:, kc, c0:c0 + cw], hT[:, kc, :n], kc == 0, kc == KC - 1,
                            rd=[wb, hbuf], wr=[pbb])
                evac((f0 + c0) // cwid, pb[:cw, :n], pbb)

    def mlp_layer(self, li, xT, gcols):
        es = None
        self.set_wslots(3)
        xg = [self.sb("xg", [128, 8, 512], F32, es) for _ in range(2)]
        xgb = [Buf("xg") for _ in range(2)]
        hT = self.sb("hT", [128, 8, 512], BF16, es)
        hb = Buf("hT")
        aT = self.sb("aT", [128, 32, 512], BF16, es)
        ab = Buf("aT")
        sq = self.sb("sq", [128, 8, 512], BF16, es)
        rs = self.sb("rs", [128, 512], F32, es)
        rl = [self.sb("rl", [128, 512], BF16, es) for _ in range(2)]
        rlb = [Buf("rl") for _ in range(2)]
        xo = [self.sb("xo", [128, 512], F32, es) for _ in range(2)]
        xob = [Buf("xo") for _ in range(2)]
        tmp = (sq, Buf("sq"), rs, Buf("rs"))
        xv = xT.ap().rearrange("(kc p) t -> p kc t", p=128)
        wup, wdn = self.w_up[li], self.w_dn[li]
        G = self.groups

        def load(gi):
            t0, n = G[gi]
            self.dma(xg[gi % 2][:, :, :n], xv[:, :, t0:t0 + n], rd=[self.db(xT, gi)], wr=[xgb[gi % 2]])

        load(0)
        cnt = [0]
        for gi, (t0, n) in enumerate(G):
            if gi + 1 < len(G):
                load(gi + 1)
            x, xb = xg[gi % 2], xgb[gi % 2]
            self.rmsnorm_group(x, xb, n, gcols, hT, hb, tmp=tmp)

            def ev_up(fc, p, pbb):
                j = cnt[0] % 2
                cnt[0] += 1
                self.act(rl[j][:, :n], p, AF.Relu, rd=[pbb], wr=[rlb[j]])
                self.tt(aT[:, fc, :n], rl[j][:, :n], rl[j][:, :n], ALU.mult, rd=[rlb[j]], wr=[ab])

            self.dense_fm(wup.ap().rearrange("(kc p) f -> p kc f", p=128), self.db(wup), 8, DFF, hT, hb, n, ev_up)

            def ev_dn(dc, p, pbb):
                j = cnt[0] % 2
                cnt[0] += 1
                self.tt(xo[j][:, :n], p, x[:, dc, :n], ALU.add, rd=[pbb, xb], wr=[xob[j]])
                self.dma(xT.ap()[dc * 128:(dc + 1) * 128, t0:t0 + n], xo[j][:, :n], rd=[xob[j]], wr=[self.db(xT, gi)], q=self.store_q)

            self.dense_fm(wdn.ap().rearrange("(kc p) f -> p kc f", p=128), self.db(wdn), 32, D, aT, ab, n, ev_dn)

def pool_layer(self, xres, gcol, scol, wpool_b, invc_d, spoolT, out_pp, out_ps):
    NPg = self.NP // 512
    self.set_wslots(1, 64)
    wp = self.sb("wp", [128, 8, 256], BF16)
    wpb = Buf("wp")
    self.dma(wp[:], wpool_b.ap().rearrange("(a p) d -> p a d", p=128), rd=[self.db(wpool_b)], wr=[wpb])
    invc = self.sb("invc", [128, 4, 16], F32)
    self.dma(invc[:], invc_d.ap(), wr=[wpb])
    xg = self.sb("xg", [128, 8, 512], F32)
    xgb = Buf("xg")
    hx = [self.sb("hx", [128, 8, 527], F32) for _ in range(2)]
    hxb = [Buf("hx") for _ in range(2)]
    st = [self.sb("st", [128, 527], F32) for _ in range(4)]
    stb = [Buf("st") for _ in range(4)]
    dT = self.sb("dT", [128, 8, 512], BF16)
    dTb = Buf("dT")
    sqt = [self.sb("sq", [128, 512], BF16) for _ in range(2)]
    sqb = [Buf("sq") for _ in range(2)]
    rs = self.sb("rs", [128, 512], F32)
    rsb = Buf("rs")
    xo = [self.sb("xo", [128, 512], F32) for _ in range(2)]
    xob = [Buf("xo") for _ in range(2)]
    xv = xres.ap().rearrange("(kc p) t -> p kc t", p=128)
    cnt = 0
    for gi, (t0, n) in enumerate(self.groups):
        samp = gi >= NPg
        nseq, L = (16, 8) if samp else (1, 512)
        W = 15 + L
        cur, curb = hx[gi % 2], hxb[gi % 2]
        prv, prvb = hx[(gi + 1) % 2], hxb[(gi + 1) % 2]
        v3 = lambda ap: ap.rearrange("p (s l) -> p s l", s=nseq)
        hv = lambda kc: cur[:, kc, :nseq * W].rearrange("p (s w) -> p s w", s=nseq)
        self.dma(xg[:, :, :n], xv[:, :, t0:t0 + n], rd=[self.db(xres, gi)], wr=[xgb])
        if samp:
            for kc in range(8):
                self.dma(hv(kc)[:, :, 0:15], spoolT.ap()[kc * 128:(kc + 1) * 128, :].rearrange("p (s r) -> p s r", s=16),
                         wr=[curb])
        elif gi == 0:
            self.memset(cur[:, :, 0:15], 0.0, wr=[curb], eng="pool")
        else:
            self.copy(cur[:, :, 0:15], prv[:, :, 512:527], rd=[prvb], wr=[curb], eng="pool")
        pb, pbb = self.bank()
        for kc in range(8):
            j = kc % 2
            self.act(sqt[j][:, :n], xg[:, kc, :n], AF.Square, rd=[xgb], wr=[sqb[j]])
            self.mm(pb[:, :n], self.ones_b[:], sqt[j][:, :n], kc == 0, kc == 7, rd=[sqb[j], self.cb], wr=[pbb])
        self.act(rs[:, :n], pb[:, :n], AF.Sqrt, rd=[pbb], wr=[rsb], scale=1.0 / 1024, bias=self.epscol[:, 0:1])
        self.recip(rs[:, :n], rs[:, :n], rd=[rsb], wr=[rsb])
        for kc in range(8):
            self.stt(hv(kc)[:, :, 15:W], v3(xg[:, kc, :n]), gcol[:, kc:kc + 1], v3(rs[:, :n]), ALU.mult, ALU.mult,
                     rd=[xgb, rsb, self.cb], wr=[curb])
        for kc in range(8):
            g = kc // 2
            w = 2 << g
            src, srcb = hv(kc), curb
            for lev in range(g + 1):
                sh = 1 << lev
                lo = (2 << lev) - 1
                k = cnt % 4
                cnt += 1
                dst = st[k][:, :nseq * W].rearrange("p (s w) -> p s w", s=nseq)
                self.tt(dst[:, :, lo:W], src[:, :, lo:W], src[:, :, lo - sh:W - sh], ALU.add, rd=[srcb], wr=[stb[k]],
                        eng="pool" if kc % 2 else "dve")
                src, srcb = dst, stb[k]
            self.stt(v3(dT[:, kc, :n]), src[:, :, 15:W], 1.0 / w, hv(kc)[:, :, 15:W], ALU.mult, ALU.subtract,
                     rd=[srcb, curb], wr=[dTb])
            if gi == 0:
                k = cnt % 4
                cnt += 1
                self.tt(st[k][:, 0:16], src[:, 0, 15:31], invc[:, g, :], ALU.mult, rd=[srcb, wpb], wr=[stb[k]])
                self.tt(dT[:, kc, 0:16], st[k][:, 0:16], cur[:, kc, 15:31], ALU.subtract, rd=[stb[k], curb], wr=[dTb])
        for g in range(4):
            for dc in range(2):
                pb, pbb = self.bank()
                for k2 in range(2):
                    self.mm(pb[:, :n], wp[:, 2 * g + k2, dc * 128:(dc + 1) * 128], dT[:, 2 * g + k2, :n], k2 == 0, k2 == 1,
                            rd=[wpb, dTb], wr=[pbb])
                o = 2 * g + dc
                j = cnt % 2
                cnt += 1
                self.stt(xo[j][:, :n], pb[:, :n], scol[:, o:o + 1], xg[:, o, :n], ALU.mult, ALU.add,
                         rd=[pbb, xgb, self.cb], wr=[xob[j]])
                self.dma(xres.ap()[o * 128:(o + 1) * 128, t0:t0 + n], xo[j][:, :n], rd=[xob[j]], wr=[self.db(xres, gi)])
        if gi == NPg - 1:
            self.dma(out_pp.ap(), cur[:, :, 512:527], rd=[curb], wr=[self.db(out_pp)])
        if samp:
            for kc in range(8):
                self.dma(out_ps.ap()[:, kc], hv(kc)[:, :, 8:23], rd=[curb], wr=[self.db(out_ps)])


KB.pool_layer = pool_layer

def ssd_layer(self, xres, gcol, P):
    GL = 512
    groups = [(t, GL) for t in range(0, self.NP, GL)] + [(self.NP, self.NS)]
    NPg = self.NP // GL
    self.set_wslots(2, 4096)
    M = self.masks
    mb = self.cb
    cnt = [0]

    def rot(lst):
        cnt[0] += 1
        return lst[cnt[0] % len(lst)]

    def mk(name, shape, dt, k=1):
        return [(self.sb(name, shape, dt), Buf(name)) for _ in range(k)]

    cw = self.sb("cw", [128, 24, 4], F32)
    cbias = self.sb("cbias", [128, 24], F32)
    dtb = self.sb("dtb", [128, 32], F32)
    abc = self.sb("abc", [128, 32], F32)
    dsk = self.sb("dsk", [128, 32], F32)
    ng = self.sb("ng", [128, 2048], F32)
    wdt = self.sb("wdt", [128, 8, 32], BF16)
    pb_ = Buf("params")
    self.dma(cw[:], P["convw"].ap(), wr=[pb_])
    self.dma(cbias[:], P["convb"].ap(), wr=[pb_])
    self.dma(dtb[:], P["dtb"].ap(), wr=[pb_])
    self.dma(abc[:], P["alog"].ap(), wr=[pb_])
    self.dma(dsk[:], P["dskip"].ap(), wr=[pb_])
    self.dma(ng[:], P["normg"].ap(), wr=[pb_])
    wiv = P["w_in_b"].ap().rearrange("(kc p) f -> p kc f", p=128)
    self.dma(wdt[:], wiv[:, :, 5120:5152], rd=[self.db(P["w_in_b"])], wr=[pb_])
    self.act(abc[:], abc[:], AF.Exp, rd=[pb_], wr=[pb_])
    self.ts(abc[:], abc[:], -1.0, None, ALU.mult, rd=[pb_], wr=[pb_])
    halo = self.sb("halo", [128, 24, 16, 3], F32)
    halob = Buf("halo")
    self.memset(halo[:, :, 0, :], 0.0, wr=[halob], eng="pool")

    (xg, xgb), = mk("xg", [128, 8, GL], F32)
    (hT, hb), = mk("hT", [128, 8, GL], BF16)
    sqs = mk("sq", [128, GL], BF16, 2)
    (rs, rsb), = mk("rs", [128, GL], F32)
    xos = mk("xo", [128, GL], F32, 2)
    upcs = mk("upc", [128, GL + 8], F32, 2)
    accs = mk("acc", [128, GL], F32, 2)
    (xcT, xcb), = mk("xcT", [128, 24, GL], BF16)
    (zs, zsb), = mk("zs", [128, 4, 2048], BF16)
    (dt, dtbuf), = mk("dt", [128, 4, 32], F32)
    (dta, dtab), = mk("dta", [128, 4, 32], F32)
    dtr = mk("dtr", [128, 32], F32, 2)
    (x_tm, xtb), = mk("x_tm", [128, 2048], BF16)
    (B_tm, btb), = mk("B_tm", [128, 512], BF16)
    (CBm, cbmb), = mk("CBm", [128, 4, 128], F32)
    (acT, actb), = mk("acT", [32, 128], F32)
    (nac, nacb), = mk("nac", [128, 32], F32)
    (eac, eacb), = mk("eac", [128, 32], F32)
    (wgt, wgtb), = mk("wgt", [128, 32], F32)
    (cdec, cdecb), = mk("cdec", [128, 32], F32)
    lexps = mk("lexp", [128, 128], F32, 4)
    whs = mk("wh", [128, 128], BF16, 4)
    (HT, htb), = mk("HT", [128, 2048], F32)
    (HTb, htbb), = mk("HTb", [128, 2048], BF16)
    (xw, xwb), = mk("xw", [128, 2048], BF16)
    (ytm, ytb), = mk("ytm", [128, 2048], F32)
    t1s = mk("t1", [128, 512], F32, 1)
    t2s = mk("t2", [128, 512], F32, 1)
    t3s = mk("t3", [128, 512], F32, 1)
    (ss, ssb), = mk("ss", [128, 4], F32)
    (junk, junkb), = mk("junk", [128, 512], BF16)
    (yn, ynb), = mk("yn", [128, 2048], BF16)
    yT, yTb = xg[:].rearrange("p k t -> p (k t)").bitcast(BF16).rearrange("p (j t) -> p j t", j=16), xgb
    xrs = mk("xr", [128, GL], F32, 2)
    bds = mk("bd", [32, 1024], F32, 2)
    h0s = [(HT[:, 0:512], Buf("h0")), (HT[:, 512:1024], Buf("h0"))]
    hbv = HT[:, 1216:1728].bitcast(BF16)
    h0bs = [(hbv[:, 0:512], Buf("h0b")), (hbv[:, 512:1024], Buf("h0b"))]
    CTm, ctmb = HTb[:, :].rearrange("p (s t) -> p s t", s=16), Buf("CTm")
    bmv = HT[:, 1024:1152].bitcast(BF16)
    bms = [(bmv[:, 0:128], Buf("bm")), (bmv[:, 128:256], Buf("bm"))]
    cds = [(HT[:, 1152:1184], Buf("cd")), (HT[:, 1184:1216], Buf("cd"))]

    self.memset(HT[:], 0.0, wr=[htb], eng="pool")
    self.memset(HTb[:], 0.0, wr=[htbb], eng="pool")
    xv = xres.ap().rearrange("(kc p) t -> p kc t", p=128)
    wov = P["w_out_b"].ap().rearrange("(kc p) f -> p kc f", p=128)

    for gi, (t0, n) in enumerate(groups):
        samp = gi >= NPg
        if samp:
            self.S.emit()
        nseq, L = (16, 8) if samp else (1, GL)
        W = L + 3
        nch = n // 128
        TRI = M["trib"] if samp else M["tri"]
        NEGm = M["negb"] if samp else M["neg"]
        BLK = M["blk"] if samp else M["onesf"]
        v3 = lambda ap: ap.rearrange("p (s l) -> p s l", s=nseq)
        self.dma(xg[:, :, :n], xv[:, :, t0:t0 + n], rd=[self.db(xres)], wr=[xgb])
        pb, pbb = self.bank()
        for kc in range(8):
            sq, sqb = rot(sqs)
            self.act(sq[:, :n], xg[:, kc, :n], AF.Square, rd=[xgb], wr=[sqb])
            self.mm(pb[:, :n], self.ones_b[:], sq[:, :n], kc == 0, kc == 7, rd=[sqb, mb], wr=[pbb])
        self.act(rs[:, :n], pb[:, :n], AF.Sqrt, rd=[pbb], wr=[rsb], scale=1.0 / 1024, bias=self.epscol[:, 0:1])
        self.recip(rs[:, :n], rs[:, :n], rd=[rsb], wr=[rsb])
        for kc in range(8):
            self.stt(hT[:, kc, :n], xg[:, kc, :n], gcol[:, kc:kc + 1], rs[:, :n], ALU.mult, ALU.mult,
                     rd=[xgb, rsb, mb], wr=[hb])
        if samp:
            self.dma(halo[:], P["convT_in"].ap(), wr=[halob])

        def ev_xbc(fc, p, pbb2):
            upc, upb = rot(upcs)
            acc, accb = rot(accs)
            u3 = upc[:, :nseq * W].rearrange("p (s w) -> p s w", s=nseq)
            self.copy(u3[:, :, 3:W], v3(p), rd=[pbb2], wr=[upb], eng="act")
            self.copy(u3[:, :, 0:3], halo[:, fc, :nseq, :], rd=[halob], wr=[upb], eng="pool")
            if samp:
                self.dma(P["conv_ps"].ap()[:, fc], u3[:, :, L:W], rd=[upb], wr=[self.db(P["conv_ps"])])
            else:
                self.copy(halo[:, fc, 0:1, :], u3[:, :, L:W], rd=[upb], wr=[halob], eng="pool")
            a3 = v3(acc[:, :n])
            self.ts(a3, u3[:, :, 0:L], cw[:, fc, 0:1], cbias[:, fc:fc + 1], ALU.mult, ALU.add,
                    rd=[upb, pb_], wr=[accb], eng="pool")
            for k in range(1, 4):
                self.stt(a3, u3[:, :, k:k + L], cw[:, fc, k:k + 1], a3, ALU.mult, ALU.add, rd=[upb, accb, pb_], wr=[accb])
            self.act(xcT[:, fc, :n], acc[:, :n], AF.Silu, rd=[accb], wr=[xcb])

        self.dense_fm(wiv[:, :, 2048:5120], self.db(P["w_in_b"]), 8, 3072, hT, hb, n, ev_xbc)
        for piece in range(4):
            wt, wb = self.wslot()
            wtv = wt[:, :4096].rearrange("p (kc f) -> p kc f", kc=8)
            self.dma(wtv, wiv[:, :, piece * 512:(piece + 1) * 512], rd=[self.db(P["w_in_b"])], wr=[wb])
            for ci in range(nch):
                pb, pbb = self.bank()
                for kc in range(8):
                    self.mm(pb[:, :], hT[:, kc, ci * 128:(ci + 1) * 128], wtv[:, kc, :],
                            kc == 0, kc == 7, rd=[hb, wb], wr=[pbb])
                c0 = piece * 512
                self.act(zs[:, ci, c0:c0 + 512], pb[:, :], AF.Silu, rd=[pbb], wr=[zsb])
        for ci in range(nch):
            pb, pbb = self.bank()
            for kc in range(8):
                self.mm(pb[:, :32], hT[:, kc, ci * 128:(ci + 1) * 128], wdt[:, kc, :], kc == 0, kc == 7, rd=[hb, pb_], wr=[pbb])
            d_, d_b = rot(dtr)
            self.tt(d_[:], pb[:, :32], dtb[:], ALU.add, rd=[pbb, pb_], wr=[d_b])
            self.act(d_[:], d_[:], AF.Exp, rd=[d_b], wr=[d_b])
            self.act(dt[:, ci, :], d_[:], AF.Ln, rd=[d_b], wr=[dtbuf], bias=self.onecol[:, 0:1])
            self.tt(dta[:, ci, :], dt[:, ci, :], abc[:], ALU.mult, rd=[dtbuf, pb_], wr=[dtab])
        pendT = [None]
        for ci in range(nch):
            cs = slice(ci * 128, (ci + 1) * 128)
            pb, pbb = self.bank()
            self.mm(pb[:32, 0:128], dta[:, ci, :], TRI[:, :], True, True, rd=[dtab, mb], wr=[pbb])
            self.mm(pb[:, 128:160], TRI[:, :], dta[:, ci, :], True, True, rd=[dtab, mb], wr=[pbb])
            self.mm(pb[:, 160:192], BLK[:, :], dta[:, ci, :], True, True, rd=[dtab, mb], wr=[pbb])
            self.copy(acT[:], pb[:32, 0:128], rd=[pbb], wr=[actb], eng="act")
            self.actmul(nac[:], pb[:, 128:160], -1.0, rd=[pbb], wr=[nacb])
            self.act(eac[:], pb[:, 128:160], AF.Exp, rd=[pbb], wr=[eacb])
            self.act(cdec[:], pb[:, 160:192], AF.Exp, rd=[pbb], wr=[cdecb])
            self.tt(wgt[:], pb[:, 160:192], nac[:], ALU.add, rd=[pbb, nacb], wr=[wgtb])
            self.act(wgt[:], wgt[:], AF.Exp, rd=[wgtb], wr=[wgtb])
            self.tt(wgt[:], wgt[:], dt[:, ci, :], ALU.mult, rd=[wgtb, dtbuf], wr=[wgtb])
            for j4 in range(5):
                pb, pbb = self.bank()
                pbv = pb[:, 0:256].bitcast(BF16)
                for jj in range(4):
                    j = j4 * 4 + jj
                    self.tr(pbv[:, jj * 128:(jj + 1) * 128], xcT[:, j, cs], self.ident_b[:], rd=[xcb, mb], wr=[pbb])
                if j4 < 4:
                    self.copy(x_tm[:, j4 * 512:(j4 + 1) * 512], pbv, rd=[pbb], wr=[xtb], eng="act" if j4 % 2 else "dve")
                else:
                    self.copy(B_tm[:], pbv, rd=[pbb], wr=[btb], eng="act")
            pb, pbb = self.bank()
            for g in range(4):
                self.mm(pb[:, g * 128:(g + 1) * 128], xcT[:, 16 + g, cs], xcT[:, 20 + g, cs], True, True, rd=[xcb], wr=[pbb])
            self.tt(CBm[:], pb[:, :].rearrange("p (g t) -> p g t", g=4), TRI[:, :].unsqueeze(1).to_broadcast([128, 4, 128]),
                    ALU.mult, rd=[pbb, mb], wr=[cbmb])
            self.tt(xw[:].rearrange("p (h q) -> p h q", h=32), x_tm[:].rearrange("p (h q) -> p h q", h=32),
                    wgt[:].unsqueeze(2).to_broadcast([128, 32, 64]), ALU.mult, rd=[xtb, wgtb], wr=[xwb], eng="pool")
            if pendT[0] is not None:
                pendT[0]()
                pendT[0] = None
            batches = [(g_, h4_) for g_ in range(4) for h4_ in range(2)]

            bdcur = [None]

            def emit_L(bi):
                g_, h4_ = batches[bi]
                if h4_ == 0:
                    bd, bdb = rot(bds)
                    self.tt(bd[:].rearrange("p (h t) -> p h t", h=8), acT[:, :].unsqueeze(1).to_broadcast([32, 8, 128]),
                            self.ident_f[0:32, g_ * 8:(g_ + 1) * 8].unsqueeze(2).to_broadcast([32, 8, 128]), ALU.mult,
                            rd=[actb, mb], wr=[bdb])
                    bdcur[0] = (bd, bdb)
                bd, bdb = bdcur[0]
                pbL_, pbLb_ = self.bank(hold=True)
                self.mm(pbL_[:, :], M["onesf"][0:32, :], bd[:, h4_ * 512:(h4_ + 1) * 512], True, True, rd=[bdb, mb], wr=[pbLb_])
                return pbL_, pbLb_

            Ls = {0: emit_L(0)}
            for bi, (g, hh4) in enumerate(batches):
                if bi + 1 < len(batches):
                    Ls[bi + 1] = emit_L(bi + 1)
                if hh4 == 0:
                    pbA, pbAb = self.bank(hold=True)
                    pbB, pbBb = self.bank(hold=True)
                pbL, pbLb = Ls.pop(bi)
                for k in range(4):
                    h = g * 8 + hh4 * 4 + k
                    hh = hh4 * 4 + k
                    sl = slice(k * 128, (k + 1) * 128)
                    lx, lxb = rot(lexps)
                    wh, whb = rot(whs)
                    self.ts(lx[:], pbL[:, sl], nac[:, h:h + 1], 0.0, ALU.add, ALU.min, rd=[pbLb, nacb], wr=[lxb])
                    self.act(lx[:], lx[:], AF.Exp, rd=[lxb], wr=[lxb])
                    self.stt(wh[:], lx[:], dt[:, ci, h:h + 1], CBm[:, g, :], ALU.mult, ALU.mult,
                             rd=[lxb, dtbuf, cbmb], wr=[whb])
                    self.mm(pbA[:, hh * 64:(hh + 1) * 64], wh[:], x_tm[:, h * 64:(h + 1) * 64], True, True,
                            rd=[whb, xtb], wr=[pbAb])
                self.release(pbLb)
                if hh4 == 0:
                    continue
                gs = slice(g * 512, (g + 1) * 512)
                if not samp:
                    self.mm(pbB[:, :], xcT[:, 20 + g, cs], HTb[:, gs], True, True, rd=[xcb, htbb], wr=[pbBb])
                else:
                    self.tt(CTm, xcT[:, 20 + g, cs].unsqueeze(1).to_broadcast([128, 16, 128]), M["sm"][:, :, :], ALU.mult,
                            rd=[xcb, mb], wr=[ctmb])
                    for sq_ in range(16):
                        h0, h0b = rot(h0s)
                        h0c, h0cb = rot(h0bs)
                        self.dma(h0, P["ssmT_in"].ap()[sq_, :, gs], wr=[h0b])
                        self.copy(h0c, h0, rd=[h0b], wr=[h0cb], eng="pool")
                        self.mm(pbB[:, :], CTm[:, sq_, :], h0c, sq_ == 0, sq_ == 15, rd=[ctmb, h0cb], wr=[pbBb])
                        bm, bmb = rot(bms)
                        self.ts(bm, B_tm[:, g * 128:(g + 1) * 128], M["cseq"][:, sq_:sq_ + 1], None, ALU.mult,
                                rd=[btb, mb], wr=[bmb])
                        pbS, pbSb = self.bank()
                        self.mm(pbS[:, :], bm, xw[:, gs], True, True, rd=[bmb, xwb], wr=[pbSb])
                        pbc, pbcb = self.bank()
                        self.mm(pbc[:, :32], M["cseq"][:, sq_:sq_ + 1].to_broadcast([128, 128]), dta[:, ci, :], True, True,
                                rd=[dtab, mb], wr=[pbcb])
                        cd, cdb = rot(cds)
                        self.act(cd, pbc[:, :32], AF.Exp, rd=[pbcb], wr=[cdb])
                        self.tt(h0.rearrange("p (h q) -> p h q", h=8), h0.rearrange("p (h q) -> p h q", h=8),
                                cd[:, g * 8:(g + 1) * 8].unsqueeze(2).to_broadcast([128, 8, 64]), ALU.mult,
                                rd=[h0b, cdb, h0cb], wr=[h0b])
                        self.tt(h0, pbS[:, :], h0, ALU.add, rd=[pbSb, h0b], wr=[h0b])
                        self.dma(P["ssm_ps"].ap()[sq_, :, gs], h0, rd=[h0b], wr=[self.db(P["ssm_ps"])])
                t1, t1b = rot(t1s)
                t2, t2b = rot(t2s)
                t3, t3b = rot(t3s)
                v8 = lambda ap: ap.rearrange("p (h q) -> p h q", h=8)
                self.tt(v8(t1[:]), v8(pbB[:, :]), eac[:, g * 8:(g + 1) * 8].unsqueeze(2).to_broadcast([128, 8, 64]), ALU.mult,
                        rd=[pbBb, eacb], wr=[t1b])
                self.tt(t2[:], pbA[:, :], t1[:], ALU.add, rd=[pbAb, t1b], wr=[t2b])
                self.tt(v8(t3[:]), v8(x_tm[:, gs]), dsk[:, g * 8:(g + 1) * 8].unsqueeze(2).to_broadcast([128, 8, 64]), ALU.mult,
                        rd=[xtb, pb_], wr=[t3b], eng="pool")
                self.tt(t2[:], t2[:], t3[:], ALU.add, rd=[t2b, t3b], wr=[t2b], eng="pool")
                self.tt(ytm[:, gs], t2[:], zs[:, ci, gs], ALU.mult, rd=[t2b, zsb], wr=[ytb])
                self.release(pbAb)
                self.release(pbBb)
                self.act(junk[:], ytm[:, gs], AF.Square, rd=[ytb], wr=[junkb, ssb], accum_out=ss[:, g:g + 1])
            if not samp:
                for g in range(4):
                    gs = slice(g * 512, (g + 1) * 512)
                    pbS, pbSb = self.bank()
                    self.mm(pbS[:, :], B_tm[:, g * 128:(g + 1) * 128], xw[:, gs], True, True, rd=[btb, xwb], wr=[pbSb])
                    v8 = lambda ap: ap.rearrange("p (h q) -> p h q", h=8)
                    self.tt(v8(HT[:, gs]), v8(HT[:, gs]), cdec[:, g * 8:(g + 1) * 8].unsqueeze(2).to_broadcast([128, 8, 64]),
                            ALU.mult, rd=[htb, cdecb, htbb], wr=[htb])
                    self.tt(HT[:, gs], pbS[:, :], HT[:, gs], ALU.add, rd=[pbSb, htb], wr=[htb])
                    self.copy(HTb[:, gs], HT[:, gs], rd=[htb], wr=[htbb], eng="act")

            def tail(ci=ci, cs=cs):
                self.act(ss[:], ss[:], AF.Sqrt, rd=[ssb], wr=[ssb], scale=1.0 / 512, bias=self.epscol[:, 0:1])
                self.recip(ss[:], ss[:], rd=[ssb], wr=[ssb])
                for g in range(4):
                    gs = slice(g * 512, (g + 1) * 512)
                    self.stt(yn[:, gs], ytm[:, gs], ss[:, g:g + 1], ng[:, gs], ALU.mult, ALU.mult, rd=[ytb, ssb, pb_], wr=[ynb])
                for j4 in range(4):
                    pb, pbb = self.bank()
                    pbv = pb[:, 0:256].bitcast(BF16)
                    for jj in range(4):
                        j = j4 * 4 + jj
                        self.tr(pbv[:, jj * 128:(jj + 1) * 128], yn[:, j * 128:(j + 1) * 128], self.ident_b[:], rd=[ynb, mb], wr=[pbb])
                    self.copy(yT[:, j4 * 4:(j4 + 1) * 4, cs], pbv.rearrange("p (j t) -> p j t", j=4), rd=[pbb], wr=[yTb],
                              eng="act" if j4 % 2 else "dve")

            pendT[0] = tail
        if pendT[0] is not None:
            pendT[0]()
            pendT[0] = None

        def ev_out(dc, p, pbb2):
            xo, xob = rot(xos)
            xr, xrb = rot(xrs)
            self.dma(xr[:, :n], xres.ap()[dc * 128:(dc + 1) * 128, t0:t0 + n], rd=[self.db(xres)], wr=[xrb], q=self.store_q)
            self.tt(xo[:, :n], p, xr[:, :n], ALU.add, rd=[pbb2, xrb], wr=[xob])
            self.dma(xres.ap()[dc * 128:(dc + 1) * 128, t0:t0 + n], xo[:, :n], rd=[xob], wr=[self.db(xres, ("s", gi))], q=self.store_q)

        self.dense_fm(wov, self.db(P["w_out_b"]), 16, 1024, yT, yTb, n, ev_out)
        if gi == NPg - 1:
            self.dma(P["conv_pp"].ap(), halo[:, :, 0, :], rd=[halob], wr=[self.db(P["conv_pp"])])
            self.dma(P["ssm_pp"].ap(), HT[:], rd=[htb], wr=[self.db(P["ssm_pp"])])


KB.ssd_layer = ssd_layer

OFF = 512
NEGV = -30000.0


def nsa_layer(self, xres, gcol, P):
    NPg = self.NP // 512
    NP, NT = self.NP, self.NT
    S = P["scr"]
    cnt = [0]

    def rot(lst):
        cnt[0] += 1
        return lst[cnt[0] % len(lst)]

    def mk(name, shape, dt, k=1):
        return [(self.sb(name, shape, dt), Buf(name)) for _ in range(k)]

    cb = self.cb
    flip = P["flip"]
    self.set_wslots(1, 64)
    (rb, rbb), = mk("rb", [32, 16], F32)
    (rb31, _x), = mk("rb31", [32, 16], F32)
    (ohd, _x), = mk("ohd", [32, 128], F32)
    (vrow, vrb), = mk("vrow", [16, 3072], F32)
    self.dma(rb[:], P["rel_bias"].ap(), wr=[rbb])
    self.dma(rb31[:], P["rb31"].ap(), wr=[rbb])
    self.dma(ohd[:], P["c_ohd"].ap(), wr=[rbb])
    self.tt(rb[:], rb[:], rb31[:], ALU.subtract, rd=[rbb], wr=[rbb])
    self.memset(vrow[:, 0:OFF], NEGV, wr=[vrb])
    self.memset(vrow[:, OFF + 128:], 0.0, wr=[vrb])
    pb, pbb = self.bank()
    self.mm(pb[:16, :128], rb[:], ohd[:], True, True, rd=[rbb], wr=[pbb])
    self.copy(vrow[:, OFF:OFF + 128], pb[:16, :128], rd=[pbb], wr=[vrb])
    vext = S["vext"]
    self.dma(vext.ap(), vrow[:], rd=[vrb], wr=[self.db(vext)])
    self.end_phase()

    self.set_wslots(2, 4096)
    (xg, xgb), = mk("xg", [128, 8, 512], F32)
    (hT, hb), = mk("hT", [128, 8, 512], BF16)
    sqs = mk("sq", [128, 512], BF16, 2)
    (rs, rsb), = mk("rs", [128, 512], F32)
    kvs = mk("kv", [128, 512], F32, 3)
    vbs = mk("vb", [128, 256], BF16, 2)
    fms = mk("fm", [64, 512], BF16, 3)
    gts = mk("gt", [128, 48], F32, 2)
    (wg, wgb), = mk("wg", [128, 8, 48], BF16)
    xv = xres.ap().rearrange("(kc p) t -> p kc t", p=128)
    wiv = P["w_in_b"].ap().rearrange("(kc p) f -> p kc f", p=128)
    self.dma(wg[:], wiv[:, :, 2560:2608], rd=[self.db(P["w_in_b"])], wr=[wgb])
    self.dma(P["win_ps"].ap()[:, 0:504, :], P["win_in"].ap()[:, 8:512, :], wr=[self.db(P["win_ps"])])
    qT_s, kT_s = S["qT"], S["kT"]
    fm_map = {}
    for h in range(16):
        fm_map[h] = ("q", h)
    for ty, j in enumerate((0, 1, 2, 4)):
        for g in range(4):
            fm_map[16 + j * 4 + g] = ("k", ty * 4 + g)
    for gi, (t0, n) in enumerate(self.groups):
        samp = gi >= NPg
        self.dma(xg[:, :, :n], xv[:, :, t0:t0 + n], rd=[self.db(xres, gi)], wr=[xgb])
        pb, pbb = self.bank()
        for kc in range(8):
            sq, sqb = rot(sqs)
            self.act(sq[:, :n], xg[:, kc, :n], AF.Square, rd=[xgb], wr=[sqb])
            self.mm(pb[:, :n], self.ones_b[:], sq[:, :n], kc == 0, kc == 7, rd=[sqb, cb], wr=[pbb])
        self.act(rs[:, :n], pb[:, :n], AF.Sqrt, rd=[pbb], wr=[rsb], scale=1.0 / 1024, bias=self.epscol[:, 0:1])
        self.recip(rs[:, :n], rs[:, :n], rd=[rsb], wr=[rsb])
        for kc in range(8):
            self.stt(hT[:, kc, :n], xg[:, kc, :n], gcol[:, kc:kc + 1], rs[:, :n], ALU.mult, ALU.mult,
                     rd=[xgb, rsb, cb], wr=[hb])
        for piece in range(3):
            wt, wb = self.wslot()
            wtv = wt[:, :4096].rearrange("p (kc f) -> p kc f", kc=8)
            self.dma(wtv, wiv[:, :, 1024 + piece * 512:1024 + (piece + 1) * 512], rd=[self.db(P["w_in_b"])], wr=[wb])
            for ci in range(n // 128):
                pb, pbb = self.bank()
                for kc in range(8):
                    self.mm(pb[:, :], hT[:, kc, ci * 128:(ci + 1) * 128], wtv[:, kc, :], kc == 0, kc == 7, rd=[hb, wb], wr=[pbb])
                kv, kvb = rot(kvs)
                self.copy(kv[:], pb[:, :], rd=[pbb], wr=[kvb], eng="act" if ci % 2 else "dve")
                r0 = t0 + ci * 128
                if piece < 2:
                    if not samp:
                        self.dma(P["kv_pp"].ap()[r0:r0 + 128, piece * 512:(piece + 1) * 512], kv[:], rd=[kvb],
                                 wr=[self.db(P["kv_pp"], r0)], q=self.store_q)
                    else:
                        self.dma(P["kv_ps"].ap()[:, piece * 512:(piece + 1) * 512], kv[:], rd=[kvb], wr=[self.db(P["kv_ps"], piece)])
                else:
                    if not samp and r0 >= NP - 512:
                        w0 = r0 - (NP - 512)
                        self.dma(P["win_pp"].ap()[w0:w0 + 128, :], kv[:], rd=[kvb], wr=[self.db(P["win_pp"], w0)])
                    if samp:
                        for s_ in range(16):
                            self.dma(P["win_ps"].ap()[s_, 504:512, :], kv[s_ * 8:(s_ + 1) * 8, :], rd=[kvb],
                                     wr=[self.db(P["win_ps"], s_)])
                if piece >= 1:
                    vb, vbb = rot(vbs)
                    self.copy(vb[:], kv[:, 256:512], rd=[kvb], wr=[vbb], eng="pool")
                    dst = S["vsel"] if piece == 1 else S["vwin"]
                    self.dma(dst.ap()[r0:r0 + 128, :], vb[:], rd=[vbb], wr=[self.db(dst, r0)], q=self.store_q)
        for ci in range(n // 128):
            pb, pbb = self.bank()
            for kc in range(8):
                self.mm(pb[:, :48], hT[:, kc, ci * 128:(ci + 1) * 128], wg[:, kc, :], kc == 0, kc == 7, rd=[hb, wgb], wr=[pbb])
            gt, gtb = rot(gts)
            self.act(gt[:], pb[:, :48], AF.Sigmoid, rd=[pbb], wr=[gtb])
            r0 = t0 + ci * 128
            self.dma(S["gates"].ap()[r0:r0 + 128, :], gt[:], rd=[gtb], wr=[self.db(S["gates"], r0)], q=self.store_q)

        def ev_fm(c64, p, pbb2):
            kind, idx = fm_map[c64]
            fm, fmb = rot(fms)
            if kind == "q":
                self.actmul(fm[:, :n], p, 0.125, rd=[pbb2], wr=[fmb])
                self.dma(qT_s.ap()[idx, :, t0:t0 + n], fm[:, :n], rd=[fmb], wr=[self.db(qT_s, (idx, gi))], q=self.store_q)
            else:
                self.copy(fm[:, :n], p, rd=[pbb2], wr=[fmb])
                self.dma(kT_s.ap()[idx, :, t0:t0 + n], fm[:, :n], rd=[fmb], wr=[self.db(kT_s, (idx, gi))], q=self.store_q)

        self.dense_fm(wiv[:, :, 0:2560], self.db(P["w_in_b"]), 8, 2560, hT, hb, n, ev_fm, fpiece=512, cwid=64,
                      skip=lambda c: c not in fm_map)
    self.end_phase()

    self.set_wslots(1, 64)
    (D0, d0b), = mk("D0", [128, 4, 128], F32)
    (D1, d1b), = mk("D1", [128, 4, 128], F32)
    (Band, bandb), = mk("Band", [32, 4, 128], F32)
    (Ds1, ds1b), = mk("Ds1", [128, 16, 8], F32)
    (Bs, bsb), = mk("Bs", [128, 16, 8], F32)
    (Ds0, ds0b), = mk("Ds0", [8, 16, 8], F32)
    (EE, eeb), = mk("EE", [128, 8192], BF16)
    (Am, amb), = mk("Am", [128, 4, 128], F32)
    (Jt, jtb), = mk("Jt", [128, 128], F32)
    (col0, _x), = mk("col0", [128, 128], F32)
    (shiftI, _x), = mk("shiftI", [32, 384], F32)
    (neglow, _x), = mk("neglow", [128, 128], BF16)
    (dw4s, _x), = mk("dw4s", [128, 32], F32)
    (sampA, _x), = mk("sampA", [8, 128], F32)
    (sampC, _x), = mk("sampC", [8, 128], F32)
    (w1b, w1bb), = mk("w1b", [64, 64, 128], BF16)
    (w2b, w2bb), = mk("w2b", [128, 2, 64], BF16)
    (peT, petb), = mk("peT", [64, 64], BF16)
    (cbias, cbb), = mk("cbias", [128, 2], F32)
    (kcT, kctb), = mk("kcT", [64, 4, 512], BF16)
    (vcA, vcab), = mk("vcA", [128, 4, 4, 65], BF16)
    (tmpf, tmpfb), = mk("tmpf", [128, 512], F32)
    cst = Buf("nsaconst")

    def toeplitz(dst, dstb, nrow, ncol, base, pstride, fl, h0=0, nh=16):
        src = bass.AP(tensor=vext, offset=base + 3072 * h0, ap=[[pstride, nrow], [3072, nh], [1, ncol]])
        tv = tmpf[:nrow, :nh * ncol].rearrange("p (h c) -> p h c", h=nh)
        self.dma(tv, src, rd=[self.db(vext)], wr=[tmpfb])
        tot = nh * ncol
        for c0 in range(0, tot, 512):
            cw = min(512, tot - c0)
            pb, pbb = self.bank()
            self.mm(pb[:nrow, :cw], fl, tmpf[:nrow, c0:c0 + cw], True, True, rd=[tmpfb, cb], wr=[pbb])
            self.copy(dst.rearrange("p h c -> p (h c)")[:, c0:c0 + cw], pb[:nrow, :cw], rd=[pbb], wr=[dstb])

    toeplitz(Ds1[:], ds1b, 128, 8, OFF + 1, 1, flip[:, :])
    toeplitz(Bs[:], bsb, 128, 8, OFF - 15, 16, flip[:, :])
    toeplitz(Ds0[:], ds0b, 8, 8, OFF - 7, 1, flip[0:8, 120:128])
    for c0 in range(0, 8192, 512):
        self.dma(tmpf[:, :], P["c_EE"].ap()[:, c0:c0 + 512], wr=[tmpfb])
        self.copy(EE[:, c0:c0 + 512], tmpf[:, :], rd=[tmpfb], wr=[eeb], eng="pool")
    self.dma(Am[:], P["c_A"].ap(), wr=[cst])
    self.dma(Jt[:], P["c_J"].ap(), wr=[cst])
    self.dma(col0[:], P["c_col0"].ap(), wr=[cst])
    self.dma(shiftI[:], P["c_shift"].ap(), wr=[cst])
    self.dma(dw4s[:], P["c_dw4s"].ap(), wr=[cst])
    self.dma(sampA[:], P["c_sampA"].ap(), wr=[cst])
    self.dma(sampC[:], P["c_sampC"].ap(), wr=[cst])
    self.dma(tmpf[:, 0:128], P["c_neglow"].ap(), wr=[tmpfb])
    self.copy(neglow[:], tmpf[:, 0:128], rd=[tmpfb], wr=[cst])
    self.dma(w1b[:], P["w1_b"].ap().rearrange("(a d) e -> d a e", d=64), rd=[self.db(P["w1_b"])], wr=[w1bb])
    self.dma(tmpf[:, 0:128], P["cmp_w2"].ap(), wr=[tmpfb])
    self.copy(w2b[:].rearrange("p a d -> p (a d)"), tmpf[:, 0:128], rd=[tmpfb], wr=[w2bb])
    self.dma(tmpf[:64, 128:192], P["peT"].ap(), wr=[tmpfb])
    self.copy(peT[:], tmpf[:64, 128:192], rd=[tmpfb], wr=[petb])
    for kv in range(2):
        pb, pbb = self.bank()
        for js in range(32):
            self.mm(pb[:, 0:1], w1b[:, kv * 32 + js, :], peT[:, kv * 32 + js:kv * 32 + js + 1], js == 0, js == 31,
                    rd=[w1bb, petb], wr=[pbb])
        self.copy(cbias[:, kv:kv + 1], pb[:, 0:1], rd=[pbb], wr=[cbb])
    self.memset(vcA[:, :, :, 64:65], 1.0, wr=[vcab])

    (kbig, kbigb), = mk("kbig", [64, 8192], BF16)
    (kbig2, kbig2b), = mk("kbig2", [64, 8192], BF16)
    hids = mk("hid", [128, 512], BF16, 2)
    NC = 511
    for g in range(4):
        for kv in range(2):
            kb_, kbb_ = (kbig, kbigb) if kv == 0 else (kbig2, kbig2b)
            self.dma(kb_[:], kT_s.ap()[kv * 4 + g, :, 0:NP], wr=[kbb_])
            pb, pbb = self.bank()
            for js in range(32):
                rhs = kb_[:, js:js + 16 * (NC - 1) + 1:16]
                self.mm(pb[:, :NC], w1b[:, kv * 32 + js, :], rhs, js == 0, js == 31, rd=[w1bb, kbb_], wr=[pbb])
            hid, hidb = rot(hids)
            self.act(hid[:, :NC], pb[:, :NC], AF.Silu, rd=[pbb, cbb], wr=[hidb], bias=cbias[:, kv:kv + 1])
            if kv == 0:
                pb2, pbb2 = self.bank()
                self.mm(pb2[:64, :NC], w2b[:, 0, :], hid[:, :NC], True, True, rd=[w2bb, hidb], wr=[pbb2])
                self.copy(kcT[:, g, :NC], pb2[:64, :NC], rd=[pbb2], wr=[kctb])
            else:
                for m in range(4):
                    nn = min(128, NC - m * 128)
                    pb2, pbb2 = self.bank()
                    self.mm(pb2[:nn, :64], hid[:, m * 128:m * 128 + nn], w2b[:, 1, :], True, True, rd=[w2bb, hidb], wr=[pbb2])
                    self.copy(vcA[:nn, m, g, 0:64], pb2[:nn, :64], rd=[pbb2], wr=[vcab])

    KselT, kselb = kbig, kbigb
    KwinT, kwinb = kbig2, kbig2b
    (VselA, vselb), = mk("VselA", [128, 68, 65], BF16)
    (VwinA, vwinb), = mk("VwinA", [128, 64, 65], BF16)
    self.memset(VselA[:, :, 64:65], 1.0, wr=[vselb])
    self.memset(VwinA[:, :, 64:65], 1.0, wr=[vwinb])
    qts = mk("qt", [64, 4, 128], BF16, 2)
    gats = mk("gat", [128, 48], F32, 2)
    ETs = mk("ET", [128, 4, 512], BF16, 1)
    ets = mk("et", [128, 512], BF16, 5)
    (rden, rdenb), = mk("rden", [128, 512], F32)
    (PT, ptb), = mk("PT", [128, 512], F32)
    (PsT, pstb), = mk("PsT", [128, 4, 128], F32)
    (Ai, aib), = mk("Ai", [128, 128], F32)
    (Ci, cib), = mk("Ci", [128, 128], F32)
    (sc, scb), = mk("sc", [128, 128], F32)
    (m8, m8b), = mk("m8", [128, 8], F32)
    (sel, selb), = mk("sel", [128, 128], BF16)
    (selT, seltb), = mk("selT", [128, 128], BF16)
    (coef, coefb), = mk("coef", [128, 4], F32)
    os_ = mk("o", [128, 256], F32, 2)
    obs = mk("ob", [128, 256], BF16, 2)
    masks_ = mk("mk", [128, 128], BF16, 2)
    ET, etb = ETs[0]
    o_s = S["o"]

    def tile_stage1(KT, ktb_, kcols, nk, qt, qtb, nq, add_tile=None, add_neg=None, band=None, mask_sel=None, keepE=None):
        N = 4 * nq
        pbS, pbSb = self.bank()
        extra = (add_tile is not None) or (add_neg is not None) or (band is not None)
        self.mm(pbS[:nk, :N], KT[:, kcols], qt.rearrange("p h q -> p (h q)"), True, not extra, rd=[ktb_, qtb], wr=[pbSb])
        if add_tile is not None:
            t_ap, t_b = add_tile
            self.mm(pbS[:nk, :N], self.ident_f[:nk, :nk], t_ap, False, (add_neg is None), rd=[t_b, cb], wr=[pbSb])
        if add_neg is not None:
            self.mm(pbS[:nk, :N], self.ident_b[:nk, :nk], add_neg, False, True, rd=[cst, cb], wr=[pbSb])
        if band is not None:
            sh_ap, b_ap = band
            self.mm(pbS[:nk, :N], sh_ap, b_ap, False, True, rd=[cst, bandb], wr=[pbSb])
        if mask_sel is not None:
            ee_cols, selT_ap = mask_sel
            pbM, pbMb = self.bank()
            self.mm(pbM[:nk, :nq], EE[:, ee_cols], selT_ap, True, True, rd=[eeb, seltb], wr=[pbMb])
        if keepE is not None:
            e_ap, e_b = keepE
        else:
            e_t, e_b = rot(ets)
            e_ap = e_t[:nk, :N]
        self.act(e_ap, pbS[:nk, :N], AF.Exp, rd=[pbSb], wr=[e_b])
        if mask_sel is not None:
            ev = e_ap.rearrange("p (h q) -> p h q", h=4)
            self.tt(ev, ev, pbM[:nk, :nq].unsqueeze(1).to_broadcast([nk, 4, nq]), ALU.mult, rd=[e_b, pbMb], wr=[e_b])
        return e_ap, e_b

    def tile_stage2(acc, accb, e, nq, VA_ap, vab_, first, last):
        e_ap, e_b = e
        for h in range(4):
            self.mm(acc[:nq, h * 65:(h + 1) * 65], e_ap[:, h * nq:(h + 1) * nq], VA_ap, first and h == 0, last,
                    rd=[e_b, vab_], wr=[accb], skip_group_check=True)

    def run_tiles(acc, accb, specs, nq, look=2):
        pend = []
        nt = len(specs)
        for t, (s1, VA_ap, vab_) in enumerate(specs):
            pend.append((tile_stage1(**s1), VA_ap, vab_))
            if t >= look:
                e, va, vb_ = pend[t - look]
                tile_stage2(acc, accb, e, nq, va, vb_, t - look == 0, t - look == nt - 1)
        for t in range(max(0, nt - look), nt):
            e, va, vb_ = pend[t]
            tile_stage2(acc, accb, e, nq, va, vb_, t == 0, t == nt - 1)

    def epilogue(acc, accb, nq, gat, gatb, g, br, o, ob, firstbr):
        a3 = acc[:nq, 0:260].rearrange("p (h d) -> p h d", h=4)
        self.ts(coef[:nq, :], a3[:, :, 64], 1e-30, None, ALU.max, rd=[accb], wr=[coefb])
        self.recip(coef[:nq, :], coef[:nq, :], rd=[coefb], wr=[coefb])
        gv = gat[:nq, g * 12:(g + 1) * 12].rearrange("p (h b) -> p h b", b=3)[:, :, br]
        self.tt(coef[:nq, :], coef[:nq, :], gv, ALU.mult, rd=[coefb, gatb], wr=[coefb])
        for h in range(4):
            if firstbr:
                self.ts(o[:nq, h * 64:(h + 1) * 64], a3[:, h, 0:64], coef[:nq, h:h + 1], None, ALU.mult,
                        rd=[accb, coefb], wr=[ob])
            else:
                self.stt(o[:nq, h * 64:(h + 1) * 64], a3[:, h, 0:64], coef[:nq, h:h + 1], o[:nq, h * 64:(h + 1) * 64],
                         ALU.mult, ALU.add, rd=[accb, coefb, ob], wr=[ob])

    def select_part1(nq, imp_ps, imp_b, A_ap, C_ap, arb):
        self.tt(sc[:nq, :], imp_ps, A_ap, ALU.mult, rd=[imp_b] + arb, wr=[scb])
        self.tt(sc[:nq, :], sc[:nq, :], C_ap, ALU.add, rd=[scb] + arb, wr=[scb])
        nc = self.nc
        self.S.add("dve", lambda: nc.vector.max(m8[:nq, :], sc[:nq, :]), [scb], [m8b])
        self.ts(sel[:nq, :], sc[:nq, :], m8[:nq, 7:8], None, ALU.is_ge, rd=[scb, m8b], wr=[selb])

    def select_part2(nq):
        pbT, pbTb = self.bank()
        pv = pbT[:, 0:64].bitcast(BF16)
        self.tr(pv[:, :nq], sel[:nq, :], self.ident_b[:nq, :nq], rd=[selb, cb], wr=[pbTb])
        self.copy(selT[:, :nq], pv[:, :nq], rd=[pbTb], wr=[seltb], eng="act")

    def select_blocks(nq, imp_ps, imp_b, A_ap, C_ap, arb):
        select_part1(nq, imp_ps, imp_b, A_ap, C_ap, arb)
        select_part2(nq)

    for g in range(4):
        toeplitz(D0[:], d0b, 128, 128, OFF - 127, 1, flip[:, :], 4 * g, 4)
        toeplitz(D1[:], d1b, 128, 128, OFF + 128 - 127, 1, flip[:, :], 4 * g, 4)
        toeplitz(Band[:], bandb, 32, 128, OFF - 367, 16, flip[0:32, 96:128], 4 * g, 4)
        self.dma(KselT[:], kT_s.ap()[8 + g, :, 0:NP], wr=[kselb])
        self.dma(KwinT[:], kT_s.ap()[12 + g, :, 0:NP], wr=[kwinb])
        self.dma(VselA[:, 0:64, 0:64], S["vsel"].ap()[0:NP, g * 64:(g + 1) * 64].rearrange("(kt p) d -> p kt d", p=128), wr=[vselb])
        self.dma(VwinA[:, :, 0:64], S["vwin"].ap()[0:NP, g * 64:(g + 1) * 64].rearrange("(kt p) d -> p kt d", p=128), wr=[vwinb])
        for i in range(NP // 128):
            q0 = i * 128
            qt, qtb = rot(qts)
            gat, gatb = rot(gats)
            self.dma(qt[:], qT_s.ap()[4 * g:4 * g + 4, :, q0:q0 + 128].rearrange("h d q -> d h q"), wr=[qtb])
            self.dma(gat[:], S["gates"].ap()[q0:q0 + 128, :], wr=[gatb])
            o, ob = rot(os_)
            ncmp = min(NC, 8 * i + 7)
            ntile = (ncmp + 127) // 128
            accC, accCb = self.bank(hold=True)
            blo, bhi = 8 * i - 10, 8 * i + 6
            specs = []
            for m in range(ntile):
                nn = min(128, ncmp - m * 128)
                band = None
                r0 = blo - 128 * m
                if bhi >= 128 * m and blo < 128 * m + nn:
                    c0 = 160 - r0
                    band = (shiftI[:, c0:c0 + nn], Band[:].rearrange("p h q -> p (h q)"))
                specs.append((dict(KT=kcT[:, g, :], ktb_=kctb, kcols=slice(m * 128, m * 128 + nn), nk=nn, qt=qt[:], qtb=qtb,
                                   nq=128, band=band, keepE=(ET[:nn, m, :], etb)), vcA[:nn, m, g, :], vcab))
            run_tiles(accC, accCb, specs, 128)
            pbD, pbDb = self.bank()
            for m in range(ntile):
                nn = min(128, ncmp - m * 128)
                self.mm(pbD[:, :], self.ones_b[:nn, :], ET[:nn, m, :], m == 0, m == ntile - 1, rd=[etb, cb], wr=[pbDb])
            self.ts(rden[:], pbD[:, :], 1e-30, None, ALU.max, rd=[pbDb], wr=[rdenb])
            self.recip(rden[:], rden[:], rd=[rdenb], wr=[rdenb])
            pbI, pbIb = self.bank(hold=True)
            for m in range(ntile):
                nn = min(128, ncmp - m * 128)
                self.tt(PT[:nn, :], ET[:nn, m, :], rden[:nn, :], ALU.mult, rd=[etb, rdenb], wr=[ptb])
                nc = self.nc
                pin = PT[:nn, :].rearrange("p (h q) -> p q h", h=4)
                pout = PsT[:nn, m, :]
                self.S.add("dve", lambda pout=pout, pin=pin: nc.vector.tensor_reduce(pout, pin, AX.X, ALU.add), [ptb], [pstb])
                self.mm(pbI[:, :128], PsT[:nn, m, :], Am[:nn, m, :], m == 0, m == ntile - 1, rd=[pstb, amb, cst], wr=[pbIb])
            self.ts(Ai[:], Jt[:], float(2 * i), None, ALU.is_le, rd=[cst], wr=[aib])
            self.ts(Ci[:], Jt[:], float(2 * i), None, ALU.is_equal, rd=[cst], wr=[cib])
            self.stt(Ci[:], Jt[:], float(2 * i - 1), Ci[:], ALU.is_equal, ALU.max, rd=[cst, cib], wr=[cib])
            self.tt(Ci[:], Ci[:], col0[:], ALU.max, rd=[cib, cst], wr=[cib])
            self.stt(Ci[:], Ci[:], 1000.0, Ai[:], ALU.mult, ALU.add, rd=[cib, aib], wr=[cib])
            self.ts(Ci[:], Ci[:], -1.0, None, ALU.add, rd=[cib], wr=[cib])
            select_part1(128, pbI[:, :128], pbIb, Ai[:], Ci[:], [aib, cib])
            self.release(pbIb)
            epilogue(accC, accCb, 128, gat, gatb, g, 0, o, ob, True)
            self.release(accCb)
            accW, accWb = self.bank(hold=True)
            kts = list(range(max(0, i - 4), i + 1))
            specs = []
            for kt in kts:
                add = None
                neg = None
                if kt == i:
                    add = (D0[:].rearrange("p h q -> p (h q)"), d0b)
                elif kt == i - 1:
                    add = (D1[:].rearrange("p h q -> p (h q)"), d1b)
                elif kt == i - 4:
                    neg = neglow[:, :].unsqueeze(1).to_broadcast([128, 4, 128])
                specs.append((dict(KT=KwinT, ktb_=kwinb, kcols=slice(kt * 128, (kt + 1) * 128), nk=128, qt=qt[:], qtb=qtb,
                                   nq=128, add_tile=add, add_neg=neg), VwinA[:, kt, :], vwinb))
            run_tiles(accW, accWb, specs, 128)
            select_part2(128)
            epilogue(accW, accWb, 128, gat, gatb, g, 2, o, ob, False)
            self.release(accWb)
            accS, accSb = self.bank(hold=True)
            specs = []
            for kt in range(i + 1):
                add = None
                if kt == i:
                    add = (D0[:].rearrange("p h q -> p (h q)"), d0b)
                elif kt == i - 1:
                    add = (D1[:].rearrange("p h q -> p (h q)"), d1b)
                specs.append((dict(KT=KselT, ktb_=kselb, kcols=slice(kt * 128, (kt + 1) * 128), nk=128, qt=qt[:], qtb=qtb,
                                   nq=128, add_tile=add, mask_sel=(slice(kt * 128, (kt + 1) * 128), selT[:, :128])),
                              VselA[:, kt, :], vselb))
            run_tiles(accS, accSb, specs, 128)
            epilogue(accS, accSb, 128, gat, gatb, g, 1, o, ob, False)
            self.release(accSb)
            obf, obfb = rot(obs)
            self.copy(obf[:], o[:], rd=[ob], wr=[obfb], eng="pool")
            self.dma(o_s.ap()[q0:q0 + 128, g * 256:(g + 1) * 256], obf[:], rd=[obfb], wr=[self.db(o_s, (q0, g))], q=self.store_q)
    self.S.emit()

    KsS = kbig[:, :].rearrange("p (g t) -> p g t", g=4)
    KcS = kbig2[:, :].rearrange("p (g t) -> p g t", g=4)
    (VcSt, vcsb), = mk("VcS", [64, 4, 2048], BF16)
    VcS = VcSt[:, :, :]
    (KsN, ksnb), = mk("KsN", [64, 4, 8], BF16)
    (KwS, kwsb), = mk("KwS", [64, 4, 520], BF16)
    VsS = VselA[:, 0:68, :].rearrange("p (kt g) d -> p kt g d", g=4)
    VwS = VwinA[:, 0:20, :].rearrange("p (kt g) d -> p kt g d", g=4)
    pgs = mk("pg", [128, 1024], F32, 2)
    (idx, idxb), = mk("idx", [128, 256], I32)
    (ptb_, ptbb), = mk("ptbc", [128, 256], I32)
    winrs = mk("winr", [128, 512], F32, 2)
    (hidS, hidSb), = mk("hidS", [128, 512], BF16)
    (kcS, kcsb), = mk("kcS", [64, 4, 128], BF16)
    (vcS, vcsab), = mk("vcSA", [128, 4, 65], BF16)
    s32s = mk("s32", [128, 32], F32, 3)
    qtss = mk("qts", [64, 4, 8], BF16, 2)
    (ETs_, etsb), = mk("ETs", [128, 32], BF16)
    self.memset(vcS[:, :, 64:65], 1.0, wr=[vcsab])
    self.dma(ptb_[:], P["page_table"].ap().rearrange("s p -> (s p)").partition_broadcast(128), wr=[ptbb])
    self.ts(idx[:], ptb_[:], 128.0, P["iota_col"], ALU.mult, ALU.add, rd=[ptbb, cb], wr=[idxb])
    cache = P["cache"]
    for sq_ in range(16):
        tcol = NP + sq_ * 8
        for pg_i in range(16):
            pg, pgb = rot(pgs)
            nc = self.nc
            col = sq_ * 16 + pg_i
            self.S.add("pool", lambda pg=pg, col=col: nc.gpsimd.indirect_dma_start(
                out=pg[:, :], out_offset=None, in_=cache.ap(),
                in_offset=bass.IndirectOffsetOnAxis(ap=idx[:, col:col + 1], axis=0)), [idxb], [pgb], dma=True)
            self.copy(VsS[:, pg_i, :, 0:64], pg[:, 768:1024].rearrange("p (g d) -> p g d", g=4), rd=[pgb], wr=[vselb], eng="pool")
            for ty, dstT, dstb_ in ((0, KcS, kbig2b), (1, VcS, vcsb), (2, KsS, kbigb)):
                pb, pbb = self.bank()
                for g in range(4):
                    c0 = ty * 256 + g * 64
                    self.tr(pb[:64, g * 128:(g + 1) * 128], pg[:, c0:c0 + 64], self.ident_f[:, :], rd=[pgb, cb], wr=[pbb])
                self.copy(dstT[:, :, pg_i * 128:(pg_i + 1) * 128], pb[:64, :].rearrange("p (g t) -> p g t", g=4),
                          rd=[pbb], wr=[dstb_], eng="act" if ty % 2 else "dve")
        self.dma(KsN[:], kT_s.ap()[8:12, :, tcol:tcol + 8].rearrange("g d t -> d g t"), wr=[ksnb])
        self.dma(VsS[0:8, 16, :, 0:64], S["vsel"].ap()[tcol:tcol + 8, :].rearrange("t (g d) -> t g d", g=4), wr=[vselb])
        for kt in range(4):
            winr, winrb = rot(winrs)
            self.dma(winr[:], P["win_in"].ap()[sq_, kt * 128:(kt + 1) * 128, :], wr=[winrb])
            self.copy(VwS[:, kt, :, 0:64], winr[:, 256:512].rearrange("p (g d) -> p g d", g=4), rd=[winrb], wr=[vwinb], eng="pool")
            pb, pbb = self.bank()
            for g in range(4):
                self.tr(pb[:64, g * 128:(g + 1) * 128], winr[:, g * 64:(g + 1) * 64], self.ident_f[:, :], rd=[winrb, cb], wr=[pbb])
            self.copy(KwS[:, :, kt * 128:(kt + 1) * 128], pb[:64, :].rearrange("p (g t) -> p g t", g=4), rd=[pbb], wr=[kwsb])
        self.dma(KwS[:, :, 512:520], kT_s.ap()[12:16, :, tcol:tcol + 8].rearrange("g d t -> d g t"), wr=[kwsb])
        self.dma(VwS[0:8, 4, :, 0:64], S["vwin"].ap()[tcol:tcol + 8, :].rearrange("t (g d) -> t g d", g=4), wr=[vwinb])
        NCs = 127
        for kv in range(2):
            src, srcb = (KcS, kbig2b) if kv == 0 else (VcS, vcsb)
            pb, pbb = self.bank()
            for js in range(32):
                rhs = src[:, :, js:js + 16 * (NCs - 1) + 1:16]
                self.mm(pb[:, :4 * NCs].rearrange("p (g n) -> p g n", g=4), w1b[:, kv * 32 + js, :], rhs, js == 0, js == 31,
                        rd=[w1bb, srcb], wr=[pbb])
            self.act(hidS[:, :4 * NCs], pb[:, :4 * NCs], AF.Silu, rd=[pbb, cbb], wr=[hidSb], bias=cbias[:, kv:kv + 1])
            if kv == 0:
                pb2, pbb2 = self.bank()
                self.mm(pb2[:64, :4 * NCs], w2b[:, 0, :], hidS[:, :4 * NCs], True, True, rd=[w2bb, hidSb], wr=[pbb2])
                self.copy(kcS[:, :, :NCs], pb2[:64, :4 * NCs].rearrange("p (g n) -> p g n", g=4), rd=[pbb2], wr=[kcsb])
            else:
                for g in range(4):
                    pb2, pbb2 = self.bank()
                    self.mm(pb2[:NCs, :64], hidS[:, g * NCs:(g + 1) * NCs], w2b[:, 1, :], True, True, rd=[w2bb, hidSb], wr=[pbb2])
                    self.copy(vcS[:NCs, g, 0:64], pb2[:NCs, :64], rd=[pbb2], wr=[vcsab])
        gat, gatb = rot(gats)
        self.dma(gat[:8, :], S["gates"].ap()[tcol:tcol + 8, :], wr=[gatb])
        for g in range(4):
            qt, qtb = rot(qtss)
            self.dma(qt[:], qT_s.ap()[4 * g:4 * g + 4, :, tcol:tcol + 8].rearrange("h d q -> d h q"), wr=[qtb])
            o, ob = rot(os_)

            def small_s1(KT_ap, ktb_, nk, addt=None, mask=None, keep=False):
                pbS, pbSb = self.bank()
                self.mm(pbS[:nk, :32], KT_ap, qt[:].rearrange("p h q -> p (h q)"), True, True, rd=[ktb_, qtb], wr=[pbSb])
                if mask is not None:
                    pbM, pbMb = self.bank()
                    self.mm(pbM[:nk, :8], EE[:, mask], selT[:, :8], True, True, rd=[eeb, seltb], wr=[pbMb])
                src_ap = pbS[:nk, :32]
                if addt is not None:
                    s32, s32b = rot(s32s)
                    self.tt(s32[:nk, :], pbS[:nk, :32], addt[0], ALU.add, rd=[pbSb, addt[1]], wr=[s32b])
                    src_ap = s32[:nk, :]
                    rdl = [s32b]
                else:
                    rdl = [pbSb]
                if keep:
                    e_ap, e_b = ETs_[:nk, :], etsb
                else:
                    e_t, e_b = rot(ets)
                    e_ap = e_t[:nk, :32]
                self.act(e_ap, src_ap, AF.Exp, rd=rdl, wr=[e_b])
                if mask is not None:
                    ev = e_ap.rearrange("p (h q) -> p h q", h=4)
                    self.tt(ev, ev, pbM[:nk, :8].unsqueeze(1).to_broadcast([nk, 4, 8]), ALU.mult, rd=[e_b, pbMb], wr=[e_b])
                return e_ap, e_b

            def small_s2(acc, accb, e, VA_ap, vab_, first, last):
                e_ap, e_b = e
                for h in range(4):
                    self.mm(acc[:8, h * 65:(h + 1) * 65], e_ap[:, h * 8:(h + 1) * 8], VA_ap, first and h == 0, last,
                            rd=[e_b, vab_], wr=[accb], skip_group_check=True)

            def small_run(acc, accb, specs, look=2):
                pend = []
                nt = len(specs)
                for t, (s1, va, vb_) in enumerate(specs):
                    pend.append((small_s1(**s1), va, vb_))
                    if t >= look:
                        e, va2, vb2 = pend[t - look]
                        small_s2(acc, accb, e, va2, vb2, t - look == 0, t - look == nt - 1)
                for t in range(max(0, nt - look), nt):
                    e, va2, vb2 = pend[t]
                    small_s2(acc, accb, e, va2, vb2, t == 0, t == nt - 1)

            accC, accCb = self.bank(hold=True)
            small_run(accC, accCb, [(dict(KT_ap=kcS[:, g, :NCs], ktb_=kcsb, nk=NCs,
                                          addt=(Bs[:NCs, 4 * g:4 * g + 4, :].rearrange("p h q -> p (h q)"), bsb), keep=True),
                                     vcS[:NCs, g, :], vcsab)])
            pbD, pbDb = self.bank()
            self.mm(pbD[:, :32], self.ones_b[:NCs, :], ETs_[:NCs, :], True, True, rd=[etsb, cb], wr=[pbDb])
            self.ts(rden[:, :32], pbD[:, :32], 1e-30, None, ALU.max, rd=[pbDb], wr=[rdenb])
            self.recip(rden[:, :32], rden[:, :32], rd=[rdenb], wr=[rdenb])
            self.tt(PT[:NCs, :32], ETs_[:NCs, :], rden[:NCs, :32], ALU.mult, rd=[etsb, rdenb], wr=[ptb])
            nc = self.nc
            pin = PT[:NCs, :32].rearrange("p (h q) -> p q h", h=4)
            pout = PsT[:NCs, 0, :8]
            self.S.add("dve", lambda pout=pout, pin=pin: nc.vector.tensor_reduce(pout, pin, AX.X, ALU.add), [ptb], [pstb])
            pbI, pbIb = self.bank(hold=True)
            self.mm(pbI[:8, :128], PsT[:NCs, 0, :8], Am[:NCs, 0, :], True, True, rd=[pstb, cst], wr=[pbIb])
            epilogue(accC, accCb, 8, gat, gatb, g, 0, o, ob, True)
            self.release(accCb)
            select_blocks(8, pbI[:8, :128], pbIb, sampA[:, :], sampC[:, :], [cst])
            self.release(pbIb)
            accS, accSb = self.bank(hold=True)
            specs = []
            for kt in range(17):
                if kt < 16:
                    addt = (Ds1[:, 4 * g:4 * g + 4, :].rearrange("p h q -> p (h q)"), ds1b) if kt == 15 else None
                    specs.append((dict(KT_ap=KsS[:, g, kt * 128:(kt + 1) * 128], ktb_=kbigb, nk=128, addt=addt,
                                       mask=slice(kt * 128, (kt + 1) * 128)), VsS[:, kt, g, :], vselb))
                else:
                    specs.append((dict(KT_ap=KsN[:, g, :], ktb_=ksnb, nk=8,
                                       addt=(Ds0[:, 4 * g:4 * g + 4, :].rearrange("p h q -> p (h q)"), ds0b),
                                       mask=slice(2048, 2056)), VsS[0:8, 16, g, :], vselb))
            small_run(accS, accSb, specs)
            epilogue(accS, accSb, 8, gat, gatb, g, 1, o, ob, False)
            self.release(accSb)
            accW, accWb = self.bank(hold=True)
            specs = []
            for kt in range(5):
                if kt == 0:
                    addt = (dw4s[:, :], cst)
                elif kt == 3:
                    addt = (Ds1[:, 4 * g:4 * g + 4, :].rearrange("p h q -> p (h q)"), ds1b)
                elif kt == 4:
                    addt = (Ds0[:, 4 * g:4 * g + 4, :].rearrange("p h q -> p (h q)"), ds0b)
                else:
                    addt = None
                if kt < 4:
                    specs.append((dict(KT_ap=KwS[:, g, kt * 128:(kt + 1) * 128], ktb_=kwsb, nk=128, addt=addt),
                                  VwS[:, kt, g, :], vwinb))
                else:
                    specs.append((dict(KT_ap=KwS[:, g, 512:520], ktb_=kwsb, nk=8, addt=addt), VwS[0:8, 4, g, :], vwinb))
            small_run(accW, accWb, specs)
            epilogue(accW, accWb, 8, gat, gatb, g, 2, o, ob, False)
            self.release(accWb)
            obf, obfb = rot(obs)
            self.copy(obf[:8, :], o[:8, :], rd=[ob], wr=[obfb], eng="pool")
            self.dma(o_s.ap()[tcol:tcol + 8, g * 256:(g + 1) * 256], obf[:8, :], rd=[obfb], wr=[self.db(o_s, (tcol, g))])
    self.end_phase()

    self.set_wslots(1, 64)
    (wout, woutb), = mk("wout", [128, 8, 1024], BF16)
    self.dma(wout[:], P["w_out_b"].ap().rearrange("(kc p) f -> p kc f", p=128), rd=[self.db(P["w_out_b"])], wr=[woutb])
    (xg, xgb), = mk("xg", [128, 8, 512], F32)
    xv = xres.ap().rearrange("(kc p) t -> p kc t", p=128)
    (otm, otmb), = mk("otm", [128, 4, 1024], BF16)
    (oT, oTb), = mk("oT", [128, 8, 512], BF16)
    xos = mk("xo", [128, 512], F32, 2)
    for gi, (t0, n) in enumerate(self.groups):
        nchk = n // 128
        self.dma(xg[:, :, :n], xv[:, :, t0:t0 + n], rd=[self.db(xres, gi)], wr=[xgb])
        self.dma(otm[:, :nchk, :], o_s.ap()[t0:t0 + n, :].rearrange("(c p) f -> p c f", p=128), wr=[otmb])
        for ci in range(nchk):
            for j4 in range(2):
                pb, pbb = self.bank()
                pbv = pb[:, 0:256].bitcast(BF16)
                for jj in range(4):
                    j = j4 * 4 + jj
                    self.tr(pbv[:, jj * 128:(jj + 1) * 128], otm[:, ci, j * 128:(j + 1) * 128], self.ident_b[:], rd=[otmb, cb], wr=[pbb])
                self.copy(oT[:, j4 * 4:(j4 + 1) * 4, ci * 128:(ci + 1) * 128], pbv.rearrange("p (j t) -> p j t", j=4),
                          rd=[pbb], wr=[oTb], eng="act" if j4 % 2 else "dve")
        for dc in range(8):
            pb, pbb = self.bank()
            for kc in range(8):
                self.mm(pb[:, :n], wout[:, kc, dc * 128:(dc + 1) * 128], oT[:, kc, :n], kc == 0, kc == 7, rd=[woutb, oTb], wr=[pbb])
            xo, xob = rot(xos)
            self.tt(xo[:, :n], pb[:, :n], xg[:, dc, :n], ALU.add, rd=[pbb, xgb], wr=[xob])
            self.dma(xres.ap()[dc * 128:(dc + 1) * 128, t0:t0 + n], xo[:, :n], rd=[xob], wr=[self.db(xres, gi)], q=self.store_q)


KB.nsa_layer = nsa_layer
import math

def final_norm(self, xres, gcol, yT):
    self.set_wslots(1, 64)
    xg = self.sb("xg", [128, 8, 512], F32)
    xgb = Buf("xg")
    yo = self.sb("yo", [128, 8, 512], F32)
    yob = Buf("yo")
    sqs = [(self.sb("sq", [128, 512], BF16), Buf("sq")) for _ in range(2)]
    rs = self.sb("rs", [128, 512], F32)
    rsb = Buf("rs")
    xv = xres.ap().rearrange("(kc p) t -> p kc t", p=128)
    yv = yT.ap().rearrange("(kc p) t -> p kc t", p=128)
    for gi, (t0, n) in enumerate(self.groups):
        self.dma(xg[:, :, :n], xv[:, :, t0:t0 + n], rd=[self.db(xres, gi)], wr=[xgb])
        pb, pbb = self.bank()
        for kc in range(8):
            sq, sqb = sqs[kc % 2]
            self.act(sq[:, :n], xg[:, kc, :n], AF.Square, rd=[xgb], wr=[sqb])
            self.mm(pb[:, :n], self.ones_b[:], sq[:, :n], kc == 0, kc == 7, rd=[sqb, self.cb], wr=[pbb])
        self.act(rs[:, :n], pb[:, :n], AF.Sqrt, rd=[pbb], wr=[rsb], scale=1.0 / 1024, bias=self.epscol[:, 0:1])
        self.recip(rs[:, :n], rs[:, :n], rd=[rsb], wr=[rsb])
        for kc in range(8):
            self.stt(yo[:, kc, :n], xg[:, kc, :n], gcol[:, kc:kc + 1], rs[:, :n], ALU.mult, ALU.mult,
                     rd=[xgb, rsb, self.cb], wr=[yob])
        self.dma(yv[:, :, t0:t0 + n], yo[:, :, :n], rd=[yob], wr=[self.db(yT)], q=self.store_q)


KB.final_norm = final_norm

NP_, NS_ = 8192, 128


def build():
    nc = bass.Bass("TRN2", target_bir_lowering=False)
    k = KB(nc, NP_, NS_)
    NT = k.NT
    k.setup_common()
    cm = k.dram_in("c_masks", [128, 6, 128])
    csm = k.dram_in("c_sm", [128, 16, 128])
    cseq = k.dram_in("c_seq", [128, 16])
    gm = k.dram_in("g_mix", [128, 4, 8])
    gf = k.dram_in("g_ffn", [128, 4, 8])
    go = k.dram_in("g_out", [128, 8])
    psc = k.dram_in("pool_scale", [128, 8])
    mf = k.gsb("mf", [128, 6, 128], F32)
    negs = k.gsb("negs", [128, 2, 128], BF16)
    smb = k.gsb("smb", [128, 16, 128], BF16)
    cs = k.gsb("cseq", [128, 16], F32)
    gmix = k.gsb("gmix", [128, 4, 8], F32)
    gffn = k.gsb("gffn", [128, 4, 8], F32)
    gout = k.gsb("gout", [128, 8], F32)
    pscs = k.gsb("psc", [128, 8], F32)
    flipt = k.gsb("flip", [128, 128], F32)
    iotac = k.gsb("iotac", [128, 1], F32)
    cb = k.cb
    k.dma(mf[:], cm.ap(), wr=[cb])
    k.dma(cs[:], cseq.ap(), wr=[cb])
    k.dma(gmix[:], gm.ap(), wr=[cb])
    k.dma(gffn[:], gf.ap(), wr=[cb])
    k.dma(gout[:], go.ap(), wr=[cb])
    k.dma(pscs[:], psc.ap(), wr=[cb])
    k.dma(flipt[:], k.dram_in("c_flip", [128, 128]).ap(), wr=[cb])
    k.dma(iotac[:], k.dram_in("c_iota", [128, 1]).ap(), wr=[cb])
    k.copy(negs[:], mf[:, 2:4, :], rd=[cb], wr=[cb])
    k.masks = {"tri": mf[:, 0, :], "trib": mf[:, 1, :], "neg": negs[:, 0, :], "negb": negs[:, 1, :],
               "blk": mf[:, 4, :], "onesf": mf[:, 5, :], "sm": smb[:, :, :], "cseq": cs[:, :]}
    xT = k.dram_in("xT", [1024, NT])
    yT = k.dram_out("yT", [1024, NT])
    xres = k.dram_tmp("xres", [1024, NT], F32)
    wf = {}
    wb = {}

    def wdecl(name, rows, cols):
        wf[name] = k.dram_in(name, [rows, cols])
        wb[name] = k.dram_tmp(name + "_b", [rows, cols], BF16)

    for i in range(4):
        wdecl("w_up%d" % i, 1024, 4096)
        wdecl("w_dn%d" % i, 4096, 1024)
    for li in range(2):
        wdecl("ssd_in%d" % li, 1024, 5152)
        wdecl("ssd_out%d" % li, 2048, 1024)
    wdecl("pool_w", 1024, 256)
    wdecl("nsa_in", 1024, 2608)
    wdecl("nsa_out", 1024, 1024)
    wdecl("cmp_w1", 4096, 128)
    SP = []
    for li in range(2):
        SP.append({
            "w_in_b": wb["ssd_in%d" % li], "w_out_b": wb["ssd_out%d" % li],
            "convw": k.dram_in("convw%d" % li, [128, 24, 4]), "convb": k.dram_in("convb%d" % li, [128, 24]),
            "dtb": k.dram_in("dtb%d" % li, [128, 32]), "alog": k.dram_in("alog%d" % li, [128, 32]),
            "dskip": k.dram_in("dskip%d" % li, [128, 32]), "normg": k.dram_in("normg%d" % li, [128, 2048]),
            "ssmT_in": k.dram_in("ssmT_in%d" % li, [16, 128, 2048]),
            "convT_in": k.dram_in("convT_in%d" % li, [128, 24, 16, 3]),
            "conv_pp": k.dram_out("conv_pp%d" % li, [128, 24, 3]),
            "conv_ps": k.dram_out("conv_ps%d" % li, [128, 24, 16, 3]),
            "ssm_pp": k.dram_out("ssm_pp%d" % li, [128, 2048]),
            "ssm_ps": k.dram_out("ssm_ps%d" % li, [16, 128, 2048]),
        })
    invc = k.dram_in("c_invc", [128, 4, 16])
    spoolT = k.dram_in("spoolT", [1024, 16 * 15])
    pool_pp = k.dram_out("pool_pp", [128, 8, 15])
    pool_ps = k.dram_out("pool_ps", [128, 8, 16, 15])
    scr = {"qT": k.dram_tmp("qT_s", [16, 64, NT], BF16), "kT": k.dram_tmp("kT_s", [16, 64, NT], BF16),
           "vsel": k.dram_tmp("vsel_s", [NT, 256], BF16), "vwin": k.dram_tmp("vwin_s", [NT, 256], BF16),
           "gates": k.dram_tmp("gates_s", [NT, 48], F32), "o": k.dram_tmp("o_s", [NT, 1024], BF16),
           "vext": k.dram_tmp("vext", [16, 3072], F32),
           "gT": k.dram_tmp("gT_s", [48, NT], F32), "oT": k.dram_tmp("oT_s", [16, 64, NP_], BF16)}
    NP = {"w_in_b": wb["nsa_in"], "w_out_b": wb["nsa_out"], "w1_b": wb["cmp_w1"], "scr": scr,
          "flip": flipt, "iota_col": iotac[:, 0:1],
          "rel_bias": k.dram_in("rel_bias", [32, 16]), "rb31": k.dram_in("rb31", [32, 16]),
          "c_ohd": k.dram_in("c_ohd", [32, 128]), "c_EE": k.dram_in("c_EE", [128, 8192]),
          "c_A": k.dram_in("c_A", [128, 4, 128]), "c_J": k.dram_in("c_J", [128, 128]),
          "c_col0": k.dram_in("c_col0", [128, 128]), "c_shift": k.dram_in("c_shift", [32, 384]),
          "c_neglow": k.dram_in("c_neglow", [128, 128]), "c_dw4s": k.dram_in("c_dw4s", [128, 32]),
          "c_sampA": k.dram_in("c_sampA", [8, 128]), "c_sampC": k.dram_in("c_sampC", [8, 128]),
          "cmp_w2": k.dram_in("cmp_w2", [128, 128]), "peT": k.dram_in("peT", [64, 64]),
          "cache": k.dram_in("cache", [2560 * 128, 1024]), "page_table": k.dram_in("page_table", [16, 16], I32),
          "win_in": k.dram_in("win_in", [16, 512, 512]),
          "kv_pp": k.dram_out("kv_pp", [NP_, 1024]), "kv_ps": k.dram_out("kv_ps", [128, 1024]),
          "win_pp": k.dram_out("win_pp", [512, 512]), "win_ps": k.dram_out("win_ps", [16, 512, 512])}
    smf = k.sb("smf", [128, 16, 128], F32)
    k.dma(smf[:], csm.ap(), wr=[cb])
    k.copy(smb[:], smf[:], rd=[cb], wr=[cb])
    for gi, (t0, n) in enumerate(k.groups):
        k.dma(xres.ap()[:, t0:t0 + n], xT.ap()[:, t0:t0 + n], wr=[k.db(xres, gi)])
    for name in wf:
        r, c = wf[name].shape
        k.cast_weight(wf[name], wb[name], r, c)
    k.end_phase()
    k.w_up = [wb["w_up%d" % i] for i in range(4)]
    k.w_dn = [wb["w_dn%d" % i] for i in range(4)]
    for i in range(4):
        kind, li = i % 3, i // 3
        if kind == 0:
            k.ssd_layer(xres, gmix[:, i, :], SP[li])
        elif kind == 1:
            k.pool_layer(xres, gmix[:, i, :], pscs[:, :], wb["pool_w"], invc, spoolT, pool_pp, pool_ps)
        else:
            k.nsa_layer(xres, gmix[:, i, :], NP)
        k.end_phase()
        k.mlp_layer(i, xres, gffn[:, i, :])
        k.end_phase()
    k.final_norm(xres, gout[:, :], yT)
    k.end_phase()
    return nc, k


def _consts():
    s = np.arange(128)[:, None]
    t = np.arange(128)[None, :]
    same = (s // 8) == (t // 8)
    tri = (s <= t).astype(np.float32)
    trib = (same & (s <= t)).astype(np.float32)
    neg = np.where(s <= t, 0.0, -30000.0).astype(np.float32)
    negb = np.where(same & (s <= t), 0.0, -30000.0).astype(np.float32)
    blk = same.astype(np.float32)
    onesf = np.ones((128, 128), np.float32)
    masks = np.stack([tri, trib, neg, negb, blk, onesf], axis=1)
    sm = np.zeros((128, 16, 128), np.float32)
    for q in range(16):
        sm[:, q, q * 8:(q + 1) * 8] = 1.0
    cseq = np.zeros((128, 16), np.float32)
    for q in range(16):
        cseq[q * 8:(q + 1) * 8, q] = 1.0
    invc = np.zeros((128, 4, 16), np.float32)
    for g, w in enumerate((2, 4, 8, 16)):
        invc[:, g, :] = 1.0 / np.minimum(np.arange(16) + 1, w)
    def bucket(d):
        d = np.asarray(d)
        nf = np.maximum(d, 1).astype(np.float32)
        large = 16 + (np.log(nf / np.float32(16)) / np.float32(math.log(128 / 16)) * np.float32(16)).astype(np.int32)
        large = np.minimum(large, 31)
        return np.where(d < 16, d, large)
    ohd = np.zeros((32, 128), np.float32)
    ohd[bucket(np.arange(128)), np.arange(128)] = 1.0
    EE = (np.arange(8192)[None, :] // 64 == np.arange(128)[:, None]).astype(np.float32)
    nn_ = np.arange(512)[:, None]
    jj = np.arange(128)[None, :]
    Afull = ((nn_ >= 4 * jj) & (nn_ <= 4 * jj + 3)).astype(np.float32) + ((nn_ + 1 >= 4 * jj) & (nn_ + 1 <= 4 * jj + 3)).astype(np.float32)
    Afull[511] = 0.0
    A = np.ascontiguousarray(Afull.reshape(4, 128, 128).transpose(1, 0, 2))
    J = (np.arange(128)[None, :] - (np.arange(128)[:, None] >= 64)).astype(np.float32)
    col0 = np.zeros((128, 128), np.float32); col0[:, 0] = 1.0
    shift = np.zeros((32, 384), np.float32)
    shift[np.arange(32), np.arange(32) + 160] = 1.0
    flipm = np.eye(128, dtype=np.float32)[::-1].copy()
    neglow = np.where(s < t, -30000.0, 0.0).astype(np.float32)
    dw4 = np.where(np.arange(128)[:, None] < np.arange(8)[None, :], -30000.0, 0.0).astype(np.float32)
    dw4s = np.ascontiguousarray(np.tile(dw4, (1, 4)))
    sA = np.zeros((8, 128), np.float32); sA[:, :33] = 1.0
    forced = np.zeros((8, 128), np.float32); forced[:, [0, 31, 32]] = 1.0
    sC = 1000.0 * forced + sA - 1.0
    return {"c_ident": np.eye(128, dtype=np.float32), "c_masks": masks, "c_sm": sm, "c_seq": cseq, "c_invc": invc,
            "c_ohd": ohd, "c_EE": EE, "c_A": A, "c_J": J, "c_col0": col0, "c_shift": shift, "c_flip": flipm,
            "c_neglow": neglow, "c_dw4s": dw4s, "c_sampA": sA, "c_sampC": sC,
            "c_iota": np.arange(128, dtype=np.float32).reshape(128, 1)}


def _pk(v):
    return np.ascontiguousarray(np.asarray(v, np.float32).reshape(-1, 128).T)


def _rep(v):
    v = np.asarray(v, np.float32).reshape(1, -1)
    return np.ascontiguousarray(np.repeat(v, 128, axis=0))


_CACHE = {}


def kernel(**inp):
    A = {k_: np.asarray(v) for k_, v in inp.items()}
    if "nc" not in _CACHE:
        _CACHE["nc"] = build()
    nc, kb = _CACHE["nc"]
    C = _consts()
    cache2d = np.ascontiguousarray(A["cache_nsa_kv"][0].reshape(2560 * 128, 1024), dtype=np.float32)
    in_maps = []
    for c in range(8):
        b = c // 4
        sl = slice(16 * c, 16 * c + 16)
        m = dict(C)
        xs = A["x_sample"][sl].reshape(128, 1024)
        m["xT"] = np.ascontiguousarray(np.concatenate([A["x_prompt"][b], xs], axis=0).T)
        m["g_mix"] = np.ascontiguousarray(np.stack([_pk(A["norm_mix"][i]) for i in range(4)], axis=1))
        m["g_ffn"] = np.ascontiguousarray(np.stack([_pk(A["norm_ffn"][i]) for i in range(4)], axis=1))
        m["g_out"] = _pk(A["norm_out"])
        m["pool_scale"] = _pk(A["pool_scale"][0])
        for i in range(4):
            m["w_up%d" % i] = A["ffn_w_up"][i]
            m["w_dn%d" % i] = A["ffn_w_down"][i]
        for li in range(2):
            m["ssd_in%d" % li] = A["ssd_w_in"][li]
            m["ssd_out%d" % li] = A["ssd_w_out"][li]
            m["convw%d" % li] = np.ascontiguousarray(A["ssd_conv_w"][li].reshape(4, 24, 128).transpose(2, 1, 0))
            m["convb%d" % li] = np.ascontiguousarray(A["ssd_conv_b"][li].reshape(24, 128).T)
            m["dtb%d" % li] = _rep(A["ssd_dt_bias"][li])
            m["alog%d" % li] = _rep(A["ssd_a_log"][li])
            m["dskip%d" % li] = _rep(A["ssd_d"][li])
            m["normg%d" % li] = _rep(A["ssd_norm"][li])
            m["ssmT_in%d" % li] = np.ascontiguousarray(A["state_ssm"][li, sl].transpose(0, 3, 1, 2).reshape(16, 128, 2048))
            m["convT_in%d" % li] = np.ascontiguousarray(A["state_conv"][li, sl].reshape(16, 3, 24, 128).transpose(3, 2, 0, 1))
        m["pool_w"] = np.ascontiguousarray(A["pool_w"][0].reshape(1024, 256))
        m["nsa_in"] = A["nsa_w_in"][0]
        m["nsa_out"] = A["nsa_w_out"][0]
        m["spoolT"] = np.ascontiguousarray(A["state_pool"][0, sl].transpose(2, 0, 1).reshape(1024, 240))
        m["win_in"] = np.ascontiguousarray(A["state_nsa_win"][0, sl].reshape(16, 512, 512))
        m["cmp_w1"] = A["nsa_cmp_w1"][0].reshape(4096, 128)
        m["cmp_w2"] = np.ascontiguousarray(A["nsa_cmp_w2"][0].transpose(1, 0, 2).reshape(128, 128))
        m["peT"] = np.ascontiguousarray(A["nsa_cmp_pe"][0].transpose(2, 0, 1).reshape(64, 64))
        m["rel_bias"] = A["rel_bias"]
        m["rb31"] = np.ascontiguousarray(np.repeat(A["rel_bias"][31:32], 32, axis=0))
        m["cache"] = cache2d
        m = {k_: np.ascontiguousarray(v, dtype=np.float32) for k_, v in m.items()}
        m["page_table"] = np.ascontiguousarray(A["page_table"][sl], dtype=np.int32)
        in_maps.append(m)
    res = run_bass_kernel_spmd(nc, in_maps, core_ids=list(range(8)))
    R = res.results
    f32 = np.float32
    y_prompt = np.stack([R[4 * b]["yT"][:, :8192].T for b in range(2)]).astype(f32)
    y_sample = np.concatenate([R[c]["yT"][:, 8192:].T.reshape(16, 8, 1024) for c in range(8)]).astype(f32)
    kv_p = np.stack([R[4 * b]["kv_pp"].reshape(8192, 4, 4, 64) for b in range(2)])[None].astype(f32)
    kv_s = np.concatenate([R[c]["kv_ps"].reshape(16, 8, 4, 4, 64) for c in range(8)])[None].astype(f32)
    win_p = np.stack([R[4 * b]["win_pp"].reshape(512, 2, 4, 64) for b in range(2)])[None].astype(f32)
    win_s = np.concatenate([R[c]["win_ps"].reshape(16, 512, 2, 4, 64) for c in range(8)])[None].astype(f32)
    ssm_p = np.stack([np.stack([R[4 * b]["ssm_pp%d" % li].reshape(128, 32, 64).transpose(1, 2, 0) for b in range(2)])
                      for li in range(2)]).astype(f32)
    ssm_s = np.stack([np.concatenate([R[c]["ssm_ps%d" % li].reshape(16, 128, 32, 64).transpose(0, 2, 3, 1)
                                      for c in range(8)]) for li in range(2)]).astype(f32)
    conv_p = np.stack([np.stack([R[4 * b]["conv_pp%d" % li].transpose(2, 1, 0).reshape(3, 3072) for b in range(2)])
                       for li in range(2)]).astype(f32)
    conv_s = np.stack([np.concatenate([R[c]["conv_ps%d" % li].transpose(2, 3, 1, 0).reshape(16, 3, 3072)
                                       for c in range(8)]) for li in range(2)]).astype(f32)
    pool_p = np.stack([R[4 * b]["pool_pp"].transpose(2, 1, 0).reshape(15, 1024) for b in range(2)])[None].astype(f32)
    pool_s = np.concatenate([R[c]["pool_ps"].transpose(2, 3, 1, 0).reshape(16, 15, 1024) for c in range(8)])[None].astype(f32)
    return (y_prompt, y_sample, kv_p, kv_s, win_p, win_s, ssm_p, ssm_s, conv_p, conv_s, pool_p, pool_s)
```
